# Optimizing a Trainium2 kernel written in Bass

```python
import math
import jax
import jax.numpy as jnp
from jax import lax
import numpy as np

D_MODEL = 2048
BATCH = 16
SEQ = 256
DEPTH = 4
DEC_BATCH = 2
DEC_SEQ = 2048
PAST_LEN = 256

GRID_W = 64
HEAD_DIM = 64
GROUP_WIDTH = D_MODEL // 4
MIX_WIDTH = 4 * GROUP_WIDTH
NA_HEADS = GROUP_WIDTH // HEAD_DIM
NA_WIN_R = 8
NA_WIN_C = 16
NA_COL_BLOCK = 16
NA_COL_BAND = NA_COL_BLOCK + NA_WIN_C
CONV_CH = GROUP_WIDTH
CONV_K = 3
SWA_HEADS = GROUP_WIDTH // HEAD_DIM
SWA_KV_HEADS = 2
SWA_GROUP = SWA_HEADS // SWA_KV_HEADS
SWA_WINDOW = 128
SWA_BLOCK = 128
DIFF_HEADS = GROUP_WIDTH // (2 * HEAD_DIM)
DIFF_V = 2 * HEAD_DIM
DIFF_BLOCK = 128
MLP_HIDDEN = 4 * D_MODEL
N_MOD = 6
ROPE_BASE = 10000.0
RMS_EPS = 1e-6
NEG_INF = -1e30
SCALE = HEAD_DIM ** -0.5
IN_SIZES = (GROUP_WIDTH, GROUP_WIDTH, GROUP_WIDTH,
            CONV_CH, CONV_CH, CONV_CH,
            SWA_HEADS * HEAD_DIM, SWA_KV_HEADS * HEAD_DIM, SWA_KV_HEADS * HEAD_DIM,
            DIFF_HEADS * 2 * HEAD_DIM, DIFF_HEADS * 2 * HEAD_DIM, DIFF_HEADS * DIFF_V)
IN_COLS = sum(IN_SIZES)

kernel_name = 'hybrid_diffusion_prefix_trunk'


def rmsnorm(x, g):
    xf = x.astype(jnp.float32)
    y = xf * lax.rsqrt(jnp.mean(xf * xf, axis=-1, keepdims=True) + RMS_EPS)
    return (y * g.astype(jnp.float32)).astype(x.dtype)


def heads(t, h):
    return t.reshape(t.shape[:-1] + (h, t.shape[-1] // h))


def split_proj(p):
    out, off = [], 0
    for n in IN_SIZES:
        out.append(p[..., off:off + n])
        off += n
    return out


def modulation(cvec, w, b):
    m = jax.nn.silu(cvec) @ w + b
    return m.reshape(m.shape[:-1] + (N_MOD, D_MODEL))


def mod_term(m, i):
    return m[..., i, :][..., None, :]


def pre_mix(x, m, g):
    return rmsnorm(x, g[0]) * (1 + mod_term(m, 1)) + mod_term(m, 0)


def post_layer(x, mix, m, g, w_out, w1, w2):
    x = x + mod_term(m, 2) * rmsnorm(mix @ w_out, g[1])
    h = rmsnorm(x, g[2]) * (1 + mod_term(m, 4)) + mod_term(m, 3)
    f = jnp.square(jax.nn.relu(h @ w1)) @ w2
    return x + mod_term(m, 5) * rmsnorm(f, g[3])


def axial_angles(T):
    t = jnp.arange(T)
    rows = (t // GRID_W).astype(jnp.float32)
    cols = (t % GRID_W).astype(jnp.float32)
    n = HEAD_DIM // 4
    inv = ROPE_BASE ** (-jnp.arange(n, dtype=jnp.float32) / n)
    return rows[:, None] * inv[None], cols[:, None] * inv[None]


def _rotate(x, ang):
    n = ang.shape[-1]
    cos = jnp.cos(ang)[None, :, None, :].astype(x.dtype)
    sin = jnp.sin(ang)[None, :, None, :].astype(x.dtype)
    x1, x2 = x[..., :n], x[..., n:]
    return jnp.concatenate([x1 * cos - x2 * sin, x1 * sin + x2 * cos], axis=-1)


def rope_2d(x, ang_r, ang_c):
    h = x.shape[-1] // 2
    return jnp.concatenate([_rotate(x[..., :h], ang_r), _rotate(x[..., h:], ang_c)], axis=-1)


def conv_mixer(u, gb, gc, w):
    z = gc * u
    T = z.shape[1]
    zp = jnp.pad(z, ((0, 0), (1, 1), (0, 0)))
    conv = zp[:, :T] * w[0] + zp[:, 1:T + 1] * w[1] + zp[:, 2:] * w[2]
    return gb * conv


def sink_softmax(s, sink):
    m = jnp.maximum(jnp.max(s, axis=-1, keepdims=True), sink)
    e = jnp.exp(s - m)
    return e / (jnp.sum(e, axis=-1, keepdims=True) + jnp.exp(sink - m))


def ctx_mha(q, k, v):
    s = jnp.einsum('bqhd,bkhd->bhqk', q, k).astype(jnp.float32) * SCALE
    p = jax.nn.softmax(s, axis=-1).astype(v.dtype)
    o = jnp.einsum('bhqk,bkhd->bqhd', p, v)
    return o.reshape(o.shape[:2] + (-1,))


def ctx_gqa_sink(q, k, v, sink):
    B, S = q.shape[:2]
    qg = q.reshape(B, S, SWA_KV_HEADS, SWA_GROUP, HEAD_DIM)
    s = jnp.einsum('bqngd,bknd->bngqk', qg, k).astype(jnp.float32) * SCALE
    sk = sink.astype(jnp.float32).reshape(SWA_KV_HEADS, SWA_GROUP)[None, :, :, None, None]
    p = sink_softmax(s, sk).astype(v.dtype)
    o = jnp.einsum('bngqk,bknd->bqngd', p, v)
    return o.reshape(B, S, SWA_HEADS * HEAD_DIM)


def lambda_value(lam_p, lam_init):
    lp = lam_p.astype(jnp.float32)
    return jnp.exp(jnp.sum(lp[0] * lp[1])) - jnp.exp(jnp.sum(lp[2] * lp[3])) + lam_init


def diff_combine(q1, q2, k1, k2, v, lam):
    a1 = jax.nn.softmax(jnp.einsum('bqhd,bkhd->bhqk', q1, k1).astype(jnp.float32) * SCALE, axis=-1)
    a2 = jax.nn.softmax(jnp.einsum('bqhd,bkhd->bhqk', q2, k2).astype(jnp.float32) * SCALE, axis=-1)
    return jnp.einsum('bhqk,bkhe->bqhe', (a1 - lam * a2).astype(v.dtype), v)


def diff_finish(o, g, lam_init):
    o = rmsnorm(o, g) * (1.0 - lam_init)
    return o.reshape(o.shape[:2] + (DIFF_HEADS * DIFF_V,))


def neighbourhood_attention(q, k, v, k_ctx, v_ctx, rpb):
    B, T, H, d = q.shape
    rows = T // GRID_W
    kr = min(NA_WIN_R, rows)
    ncb = GRID_W // NA_COL_BLOCK
    r = jnp.arange(rows)
    row_idx = jnp.clip(r - kr // 2, 0, rows - kr)[:, None] + jnp.arange(kr)[None]
    cb = jnp.arange(ncb)
    band = jnp.clip(cb * NA_COL_BLOCK - NA_WIN_C // 2, 0, GRID_W - NA_COL_BAND)
    col_idx = band[:, None] + jnp.arange(NA_COL_BAND)[None]
    qc = cb[:, None] * NA_COL_BLOCK + jnp.arange(NA_COL_BLOCK)[None]
    cs = jnp.clip(qc - NA_WIN_C // 2, 0, GRID_W - NA_WIN_C)
    kc = col_idx[:, None, :]
    col_ok = (kc >= cs[..., None]) & (kc < cs[..., None] + NA_WIN_C)
    kloc = kr * NA_COL_BAND
    mask = jnp.broadcast_to(col_ok[:, :, None, :], (ncb, NA_COL_BLOCK, kr, NA_COL_BAND)).reshape(ncb, NA_COL_BLOCK, kloc)
    ri = row_idx[:, None, :, None]
    ci = col_idx[None, :, None, :]
    k_blk = k.reshape(B, rows, GRID_W, H, d)[:, ri, ci].reshape(B, rows, ncb, kloc, H, d)
    v_blk = v.reshape(B, rows, GRID_W, H, d)[:, ri, ci].reshape(B, rows, ncb, kloc, H, d)
    q_blk = q.reshape(B, rows, ncb, NA_COL_BLOCK, H, d)
    dr = row_idx - r[:, None] + NA_WIN_R - 1
    dc = jnp.clip(kc - qc[..., None], -(NA_WIN_C - 1), NA_WIN_C - 1) + NA_WIN_C - 1
    bias = rpb[:, dr[:, None, None, :, None], dc[None, :, :, None, :]]
    bias = bias.reshape(H, rows, ncb, NA_COL_BLOCK, kloc).astype(jnp.float32)
    s_loc = jnp.einsum('brcqhd,brckhd->bhrcqk', q_blk, k_blk).astype(jnp.float32) * SCALE + bias[None]
    s_loc = jnp.where(mask[None, None, None], s_loc, NEG_INF)
    s_ctx = jnp.einsum('brcqhd,bkhd->bhrcqk', q_blk, k_ctx).astype(jnp.float32) * SCALE
    p = jax.nn.softmax(jnp.concatenate([s_loc, s_ctx], axis=-1), axis=-1).astype(v.dtype)
    o = (jnp.einsum('bhrcqk,brckhd->brcqhd', p[..., :kloc], v_blk)
         + jnp.einsum('bhrcqk,bkhd->brcqhd', p[..., kloc:], v_ctx))
    return o.reshape(B, T, H * d)


def window_gqa_sink(q, k, v, k_ctx, v_ctx, sink):
    B, T, H, d = q.shape
    nb = T // SWA_BLOCK
    bw = SWA_BLOCK + 2 * SWA_WINDOW
    pad = ((0, 0), (SWA_WINDOW, SWA_WINDOW), (0, 0), (0, 0))
    idx = jnp.arange(nb)[:, None] * SWA_BLOCK + jnp.arange(bw)[None]
    k_blk = jnp.pad(k, pad)[:, idx]
    v_blk = jnp.pad(v, pad)[:, idx]
    qpos = jnp.arange(nb)[:, None] * SWA_BLOCK + jnp.arange(SWA_BLOCK)[None]
    kpos = idx - SWA_WINDOW
    valid = ((jnp.abs(qpos[:, :, None] - kpos[:, None, :]) <= SWA_WINDOW)
             & (kpos[:, None, :] >= 0) & (kpos[:, None, :] < T))
    qb = q.reshape(B, nb, SWA_BLOCK, SWA_KV_HEADS, SWA_GROUP, d)
    s_loc = jnp.einsum('bnqhgd,bnjhd->bhgnqj', qb, k_blk).astype(jnp.float32) * SCALE
    s_loc = jnp.where(valid[None, None, None], s_loc, NEG_INF)
    s_ctx = jnp.einsum('bnqhgd,bjhd->bhgnqj', qb, k_ctx).astype(jnp.float32) * SCALE
    sk = sink.astype(jnp.float32).reshape(SWA_KV_HEADS, SWA_GROUP)[None, :, :, None, None, None]
    p = sink_softmax(jnp.concatenate([s_loc, s_ctx], axis=-1), sk).astype(v.dtype)
    o = (jnp.einsum('bhgnqj,bnjhd->bnqhgd', p[..., :bw], v_blk)
         + jnp.einsum('bhgnqj,bjhd->bnqhgd', p[..., bw:], v_ctx))
    return o.reshape(B, T, H * d)


def diff_attention_latent(q1, q2, k1, k2, v, k1c, k2c, vc, lam):
    B, T, H, d = q1.shape
    nb = T // DIFF_BLOCK
    K1 = jnp.concatenate([k1, k1c], axis=1)
    K2 = jnp.concatenate([k2, k2c], axis=1)
    V = jnp.concatenate([v, vc], axis=1)
    qb = jnp.stack([q1, q2]).reshape(2, B, nb, DIFF_BLOCK, H, d).transpose(2, 0, 1, 3, 4, 5)
    o = lax.map(lambda qs: diff_combine(qs[0], qs[1], K1, K2, V, lam), qb)
    return o.transpose(1, 0, 2, 3, 4).reshape(B, T, H, DIFF_V)


def context_mixer(p, conv_w, sink, lam, lam_init, dg):
    B, S, _ = p.shape
    na_q, na_k, na_v, u, gb, gc, sq, sk, sv, dq, dk, dv = split_proj(p)
    na_q, na_k, na_v = heads(na_q, NA_HEADS), heads(na_k, NA_HEADS), heads(na_v, NA_HEADS)
    o_a = ctx_mha(na_q, na_k, na_v)
    o_b = conv_mixer(u, gb, gc, conv_w)
    sq, sk, sv = heads(sq, SWA_HEADS), heads(sk, SWA_KV_HEADS), heads(sv, SWA_KV_HEADS)
    o_c = ctx_gqa_sink(sq, sk, sv, sink)
    dq, dk, dv = heads(dq, DIFF_HEADS), heads(dk, DIFF_HEADS), heads(dv, DIFF_HEADS)
    o_d = diff_finish(diff_combine(dq[..., :HEAD_DIM], dq[..., HEAD_DIM:], dk[..., :HEAD_DIM], dk[..., HEAD_DIM:], dv, lam), dg, lam_init)
    mix = jnp.concatenate([o_a, o_b, o_c, o_d], axis=-1)
    return mix, jnp.stack([na_k, na_v], axis=1), jnp.stack([sk, sv], axis=1), jnp.stack([dk, dv], axis=1)


def latent_mixer(p, na_kv, swa_kv, diff_kv, conv_w, rpb, sink, lam, lam_init, dg, ang_r, ang_c):
    na_q, na_k, na_v, u, gb, gc, sq, sk, sv, dq, dk, dv = split_proj(p)
    na_q, na_k, na_v = heads(na_q, NA_HEADS), heads(na_k, NA_HEADS), heads(na_v, NA_HEADS)
    o_a = neighbourhood_attention(na_q, na_k, na_v, na_kv[:, 0], na_kv[:, 1], rpb)
    o_b = conv_mixer(u, gb, gc, conv_w)
    sq = rope_2d(heads(sq, SWA_HEADS), ang_r, ang_c)
    sk = rope_2d(heads(sk, SWA_KV_HEADS), ang_r, ang_c)
    o_c = window_gqa_sink(sq, sk, heads(sv, SWA_KV_HEADS), swa_kv[:, 0], swa_kv[:, 1], sink)
    dq, dk, dv = heads(dq, DIFF_HEADS), heads(dk, DIFF_HEADS), heads(dv, DIFF_HEADS)
    q1 = rope_2d(dq[..., :HEAD_DIM], ang_r, ang_c)
    q2 = rope_2d(dq[..., HEAD_DIM:], ang_r, ang_c)
    k1 = rope_2d(dk[..., :HEAD_DIM], ang_r, ang_c)
    k2 = rope_2d(dk[..., HEAD_DIM:], ang_r, ang_c)
    kc = diff_kv[:, 0]
    o_d = diff_finish(diff_attention_latent(q1, q2, k1, k2, dv, kc[..., :HEAD_DIM], kc[..., HEAD_DIM:], diff_kv[:, 1], lam), dg, lam_init)
    return jnp.concatenate([o_a, o_b, o_c, o_d], axis=-1)


def setup_inputs(seed: int = 0) -> dict:
    key = jax.random.key(seed)
    ks = jax.random.split(key, 20)
    nrm = jax.random.normal
    f32 = jnp.float32
    return {
        'x_prompt': nrm(ks[0], (BATCH, SEQ, D_MODEL), f32),
        'x_sample': nrm(ks[1], (DEC_BATCH, DEC_SEQ, D_MODEL), f32),
        'cache_na_kv': nrm(ks[2], (DEC_BATCH, DEPTH, 2, PAST_LEN, NA_HEADS, HEAD_DIM), f32),
        'cache_swa_kv': nrm(ks[3], (DEC_BATCH, DEPTH, 2, PAST_LEN, SWA_KV_HEADS, HEAD_DIM), f32),
        'cache_diff_kv': nrm(ks[4], (DEC_BATCH, DEPTH, 2, PAST_LEN, DIFF_HEADS, DIFF_V), f32),
        'c': nrm(ks[5], (DEC_BATCH, D_MODEL), f32),
        'c_ctx': nrm(ks[6], (D_MODEL,), f32),
        'ada_w': nrm(ks[7], (DEPTH, D_MODEL, N_MOD * D_MODEL), f32) * (0.5 * D_MODEL ** -0.5),
        'ada_b': nrm(ks[8], (DEPTH, N_MOD * D_MODEL), f32) * 0.01,
        'norm_g': 1.0 + 0.05 * nrm(ks[9], (DEPTH, 4, D_MODEL), f32),
        'w_in': nrm(ks[10], (DEPTH, D_MODEL, IN_COLS), f32) * D_MODEL ** -0.5,
        'conv_w': nrm(ks[11], (DEPTH, CONV_K, CONV_CH), f32) * CONV_K ** -0.5,
        'na_rpb': nrm(ks[12], (DEPTH, NA_HEADS, 2 * NA_WIN_R - 1, 2 * NA_WIN_C - 1), f32) * 0.1,
        'swa_sink': nrm(ks[13], (DEPTH, SWA_HEADS), f32),
        'diff_lambda': nrm(ks[14], (DEPTH, 4, HEAD_DIM), f32) * 0.1,
        'diff_norm_g': 1.0 + 0.05 * nrm(ks[15], (DEPTH, DIFF_V), f32),
        'w_out': nrm(ks[16], (DEPTH, MIX_WIDTH, D_MODEL), f32) * MIX_WIDTH ** -0.5,
        'mlp_w1': nrm(ks[17], (DEPTH, D_MODEL, MLP_HIDDEN), f32) * D_MODEL ** -0.5,
        'mlp_w2': nrm(ks[18], (DEPTH, MLP_HIDDEN, D_MODEL), f32) * MLP_HIDDEN ** -0.5,
    }


def reference(x_prompt, x_sample, cache_na_kv, cache_swa_kv, cache_diff_kv, c, c_ctx,
              ada_w, ada_b, norm_g, w_in, conv_w, na_rpb, swa_sink, diff_lambda, diff_norm_g,
              w_out, mlp_w1, mlp_w2):
    ang_r, ang_c = axial_angles(x_sample.shape[1])
    yp, ys = x_prompt, x_sample
    na_st, swa_st, diff_st = [], [], []
    for l in range(DEPTH):
        lam_init = 0.8 - 0.6 * math.exp(-0.3 * l)
        lam = lambda_value(diff_lambda[l], lam_init)
        mod_p = modulation(c_ctx, ada_w[l], ada_b[l])
        h = pre_mix(yp, mod_p, norm_g[l])
        mix, na_kv, swa_kv, d_kv = context_mixer(h @ w_in[l], conv_w[l], swa_sink[l], lam, lam_init, diff_norm_g[l])
        yp = post_layer(yp, mix, mod_p, norm_g[l], w_out[l], mlp_w1[l], mlp_w2[l])
        na_st.append(na_kv)
        swa_st.append(swa_kv)
        diff_st.append(d_kv)
        mod_s = modulation(c, ada_w[l], ada_b[l])
        h = pre_mix(ys, mod_s, norm_g[l])
        mix = latent_mixer(h @ w_in[l], cache_na_kv[:, l], cache_swa_kv[:, l], cache_diff_kv[:, l], conv_w[l],
                           na_rpb[l], swa_sink[l], lam, lam_init, diff_norm_g[l], ang_r, ang_c)
        ys = post_layer(ys, mix, mod_s, norm_g[l], w_out[l], mlp_w1[l], mlp_w2[l])
    return (yp, ys, jnp.stack(na_st, axis=1), jnp.stack(swa_st, axis=1), jnp.stack(diff_st, axis=1))
```

```python
import math
import numpy as np
import concourse.bass as bass
import concourse.mybir as mybir
from concourse.bass_utils import run_bass_kernel_spmd

F32 = mybir.dt.float32
BF16 = mybir.dt.bfloat16
AF = mybir.ActivationFunctionType
ALU = mybir.AluOpType
AX = mybir.AxisListType

D = 2048
DEPTH = 4
NTOK = 2560
TS = 512
NT = 5
HID = 8192
INC = 5376
SCALE = 64 ** -0.5
EPS = 1e-6
NEG = -30000.0


class Chan:
    def __init__(self, nc, name, inc=16):
        self.sem = nc.alloc_semaphore(name)
        self.count = 0
        self.inc = inc
        self.last_op = None


class Op:
    __slots__ = ("eng", "fn", "deps", "signal", "chan", "val", "is_dma", "waits")


class _Rec:
    def __getattr__(self, name):
        def f(*a, **k):
            self.__dict__["call"] = (name, a, k)
            return self
        return f


class Sched:
    ENGS = ("pe", "act", "dve", "pool", "sp")

    def __init__(self, nc):
        self.nc = nc
        self.ops = []
        self.last_w = {}
        self.readers = {}
        self.esem = {e: nc.alloc_semaphore("prog_" + e) for e in self.ENGS}
        self.nchan = 0
        self.fence_idx = None

    def chan(self, name=None, inc=16):
        self.nchan += 1
        return Chan(self.nc, name or ("ch%d" % self.nchan), inc)

    def op(self, eng, fn, reads=(), writes=(), chan=None):
        o = Op()
        o.eng = eng
        rec = _Rec()
        fn(rec)
        o.fn = rec.__dict__["call"]
        o.chan = chan
        o.is_dma = chan is not None
        o.signal = o.is_dma
        o.val = None
        deps = set()
        for r in reads:
            w = self.last_w.get(r)
            if w is not None:
                deps.add(w)
        for r in writes:
            w = self.last_w.get(r)
            if w is not None:
                deps.add(w)
            for rd in self.readers.get(r, {}).values():
                if isinstance(rd, list):
                    deps.update(rd)
                else:
                    deps.add(rd)
        i = len(self.ops)
        if self.fence_idx is not None:
            deps.add(self.fence_idx)
        if chan is not None:
            if callable(chan):
                chan = chan()
                o.chan = chan
            if chan.last_op is not None:
                deps.add(chan.last_op)
            chan.last_op = i
        o.deps = deps
        self.ops.append(o)
        for r in reads:
            d = self.readers.setdefault(r, {})
            if o.is_dma:
                d.setdefault("dma", []).append(i)
            else:
                d[eng] = i
        for r in writes:
            self.last_w[r] = i
            self.readers[r] = {}
        if o.is_dma:
            chan.count += chan.inc
            o.val = chan.count
        return i

    def fence(self, fn):
        names = list(dict.fromkeys(list(self.last_w.keys()) + list(self.readers.keys())))
        self.fence_idx = self.op("dve", fn, writes=names)

    def finalize(self, final_waits=()):
        ops = self.ops
        for o in ops:
            for d in o.deps:
                p = ops[d]
                if p.is_dma:
                    continue
                if p.eng == "pe" and o.eng == "pe":
                    continue
                p.signal = True
        cnt = {e: 0 for e in self.ENGS}
        for o in ops:
            if o.signal and not o.is_dma:
                cnt[o.eng] += 1
                o.val = cnt[o.eng]
        seen = {e: {} for e in self.ENGS}
        chan_count_at = {}
        for i, o in enumerate(ops):
            waits = {}
            for d in o.deps:
                p = ops[d]
                if p.is_dma:
                    sem = p.chan.sem
                    v = chan_count_at[id(p.chan)]
                    key = ("c", id(p.chan))
                else:
                    if p.eng == "pe" and o.eng == "pe":
                        continue
                    sem = self.esem[p.eng]
                    v = p.val
                    key = ("e", p.eng)
                if seen[o.eng].get(key, 0) >= v:
                    continue
                if key not in waits or waits[key][1] < v:
                    waits[key] = (sem, v)
            for key, (sem, v) in waits.items():
                seen[o.eng][key] = v
            o.waits = list(waits.values())
            if o.is_dma:
                chan_count_at[id(o.chan)] = o.val
        self.final = [(c.sem, c.count) for c in final_waits]
        return cnt

    def emit(self):
        nc = self.nc
        per = {e: [o for o in self.ops if o.eng == e] for e in self.ENGS}
        esem = self.esem
        final = self.final

        def run(eng_name, eng):
            for o in per[eng_name]:
                for sem, v in o.waits:
                    eng.wait_ge(sem, v)
                name, a, k = o.fn
                ins = getattr(eng, name)(*a, **k)
                if o.is_dma:
                    ins.then_inc(o.chan.sem, o.chan.inc)
                elif o.signal:
                    ins.then_inc(esem[eng_name], 1)

        with nc.Block() as block:
            @block.sync
            def _(e):
                run("sp", e)
                for sem, v in final:
                    e.wait_ge(sem, v)

            @block.scalar
            def _(e):
                run("act", e)

            @block.vector
            def _(e):
                run("dve", e)

            @block.gpsimd
            def _(e):
                run("pool", e)

            @block.tensor
            def _(e):
                run("pe", e)


def chunk_type(m):
    if m < 4: return "na_q"
    if m < 8: return "na_k"
    if m < 12: return "na_v"
    if m < 16: return "u"
    if m < 20: return "gb"
    if m < 24: return "gc"
    if m < 28: return "sq"
    if m == 28: return "sk"
    if m == 29: return "sv"
    if m < 34: return "dq"
    if m < 38: return "dk"
    return "dv"


ROPE_T = ("sq", "sk", "dq", "dk")
VCOL = {"na_v": 0, "sv": 512, "dv": 640}
VBASE = {"na_v": 8, "sv": 29, "dv": 38}
KBASE = {"na_k": 4, "sk": 28, "dk": 34}


def na_pat(i):
    return {0: 0, 1: 1, 14: 3, 15: 4}.get(i, 2)


def na_ks(i):
    return min(max(2 * i - 4, 0), 23)


def build_program(depth=DEPTH):
    nc = bass.Bass("TRN2", target_bir_lowering=False)

    def din(name, shape, dt=F32):
        return nc.dram_tensor(name, list(shape), dt, kind="ExternalInput").ap()

    def dout(name, shape, dt=F32):
        return nc.dram_tensor(name, list(shape), dt, kind="ExternalOutput").ap()

    def dscr(name, shape, dt):
        return nc.dram_tensor(name, list(shape), dt).ap()

    xT_in = din("xT", [D, NTOK])
    cvec = din("cvec", [128, 16, 2])
    ada_w = din("ada_w", [depth, D, 6 * D])
    ada_bT = din("ada_bT", [128, DEPTH, 96])
    normgT = din("normgT", [128, DEPTH, 4, 16])
    w_in = din("w_in", [depth, D, INC])
    w_out = din("w_out", [depth, D, D])
    w1 = din("w1", [depth, D, HID])
    w2 = din("w2", [depth, HID, D])
    convwT = din("convwT", [128, DEPTH, 4, 3])
    nab = din("nab", [depth, 5, 8, 128, 576])
    sinkB = din("sinkB", [128, DEPTH, 8])
    lamP = din("lamP", [128, DEPTH, 4, 64])
    dgB = din("dgB", [128, DEPTH, 128])
    cnaKT = din("cnaKT", [DEPTH, 4, 128, 256])
    cnaV = din("cnaV", [DEPTH, 256, 512])
    cswaKT = din("cswaKT", [DEPTH, 128, 256])
    cswaV = din("cswaV", [DEPTH, 256, 128])
    cdiffKT = din("cdiffKT", [DEPTH, 4, 128, 256])
    cdiffV = din("cdiffV", [DEPTH, 256, 512])
    cosT_in = din("cosT", [128, 2048])
    sinT_in = din("sinT", [128, 2048])
    perm_in = din("perm", [128, 128])
    ident_in = din("ident", [128, 128])
    swamask_in = din("swamask", [128, 3, 384])

    yT = dout("yT", [D, NTOK])
    ona = dout("ona", [2, DEPTH, 2, 256, 512])
    oswa = dout("oswa", [2, DEPTH, 2, 256, 128])
    odiff = dout("odiff", [2, DEPTH, 2, 256, 512])

    XS = dscr("XS", [D, NTOK], F32)
    PTs = dscr("PTs", [42 * 128, NTOK], BF16)
    VT = dscr("VT", [NTOK, 1152], BF16)
    AH = dscr("AH", [HID, NTOK], BF16)

    BASE = 16512
    LIMIT = BASE + 212000
    cur = [BASE]

    def sb(name, shape, dt, at=None):
        esz = 4 if dt == F32 else 2
        n = esz
        for s in shape[1:]:
            n *= s
        n = (n + 31) // 32 * 32
        if at is None:
            off = cur[0]
            cur[0] += n
            assert cur[0] <= LIMIT, (name, cur[0])
        else:
            off = at[0]
            at[0] += n
            assert at[0] <= at[1], (name, at[0], at[1])
        return nc.alloc_sbuf_tensor_at(name, list(shape), dt, offset=off)

    ident = sb("ident", [128, 128], BF16)
    perm = sb("perm", [128, 128], BF16)
    ones = sb("ones", [128, 128], BF16)
    cosT = sb("cosT", [128, 2048], BF16)
    sinT = sb("sinT", [128, 2048], BF16)
    modT = sb("modT", [128, DEPTH, 96, 2], F32)
    coef = sb("coef", [128, DEPTH, 6, 16, 2], F32)
    normg = sb("normg", [128, DEPTH, 4, 16], F32)
    adab = sb("adab", [128, DEPTH, 96], F32)
    cv32 = sb("cv32", [128, 16, 2], F32)
    cvb = sb("cvb", [128, 16, 2], BF16)
    convw = sb("convw", [128, DEPTH, 4, 3], F32)
    sinkT = sb("sinkT", [128, DEPTH, 8], F32)
    lamp = sb("lamp", [128, DEPTH, 4, 64], F32)
    lamv = sb("lamv", [128, DEPTH, 4], F32)
    dgs = sb("dgs", [128, DEPTH, 128], F32)
    swamask = sb("swamask", [128, 3, 384], F32)
    epsT = sb("epsT", [128, 1], F32)
    small = sb("small", [128, 64], F32)
    hall = sb("hall", [128, 16, NTOK], BF16)
    WB = [sb("wb%d" % i, [128, 16, 512], BF16) for i in range(2)]
    wbase = WB[0]
    WORK0 = cur[0]
    WORK_END = LIMIT
    ATT0 = WORK0 - 2 * 16 * 512 * 2

    ps = [nc.alloc_psum_tensor("ps%d" % i, [128, 512], F32) for i in range(4)]
    psT = [nc.alloc_psum_tensor("psT%d" % i, [128, 1024], BF16) for i in range(2)]
    psO = [nc.alloc_psum_tensor("psO%d" % i, [128, 512], F32) for i in range(2)]

    S = Sched(nc)
    rotc = {}

    def rot(name, n):
        v = rotc.get(name, 0)
        rotc[name] = v + 1
        return v % n

    ch_w = [S.chan("w0"), S.chan("w1")]
    LDP = {"sp": [S.chan("spld%d" % i) for i in range(12)], "pool": [S.chan("plld%d" % i) for i in range(6)]}
    STP = [S.chan("spst%d" % i) for i in range(12)]

    class _Pick:
        def __init__(self, lst, key):
            self.lst, self.key = lst, key

        def __call__(self):
            return self.lst[rot(self.key, len(self.lst))]

    ld_sp = _Pick(LDP["sp"], "ldsp")
    ld_pl = _Pick(LDP["pool"], "ldpl")
    st_sp = _Pick(STP, "stsp")
    ch_in = ld_sp
    ch_x = ld_sp
    ch_xo = st_sp
    ch_stg = [st_sp] * 4
    ch_out = st_sp
    ch_ld = [ld_sp] * 8

    def ld_const(dst, src, eng="sp", name=None):
        S.op(eng, lambda e: e.dma_start(out=dst, in_=src), writes=[name], chan=(ld_sp if eng == "sp" else ld_pl))

    ld_const(ident[:], ident_in, "pool", "ident")
    ld_const(perm[:], perm_in, "pool", "perm")
    ld_const(cosT[:], cosT_in, "pool", "cosT")
    ld_const(sinT[:], sinT_in, "pool", "sinT")
    ld_const(adab[:], ada_bT, "sp", "adab")
    ld_const(normg[:], normgT, "sp", "normg")
    ld_const(cv32[:], cvec, "sp", "cv32")
    ld_const(convw[:], convwT, "sp", "convw")
    ld_const(sinkT[:], sinkB, "sp", "sinkT")
    ld_const(lamp[:], lamP, "sp", "lamp")
    ld_const(dgs[:], dgB, "sp", "dgs")
    ld_const(swamask[:], swamask_in, "sp", "swamask")
    S.op("dve", lambda e: e.memset(ones[:], 1.0), writes=["ones"])
    S.op("dve", lambda e: e.memset(epsT[:], EPS), writes=["epsT"])
    S.op("act", lambda e: e.activation(out=cvb[:], in_=cv32[:], func=AF.Silu), reads=["cv32"], writes=["cvb"])

    import os as _os
    STOP = int(_os.environ.get("KSTOP", "99"))
    lam_inits = [0.8 - 0.6 * math.exp(-0.3 * l) for l in range(DEPTH)]
    lprod = sb("lprod", [128, 2, 64], F32)
    for l in range(depth if STOP >= -1 else 0):
        for j in range(2):
            S.op("dve", lambda e, l=l, j=j: e.tensor_tensor(out=lprod[:, j, :], in0=lamp[:, l, 2 * j, :],
                                                            in1=lamp[:, l, 2 * j + 1, :], op=ALU.mult),
                 reads=["lamp"], writes=[("lprod", j)])
            S.op("dve", lambda e, l=l, j=j: e.tensor_reduce(out=lamv[:, l, j:j + 1], in_=lprod[:, j, :], axis=AX.X, op=ALU.add),
                 reads=[("lprod", j)], writes=[("lamv", l, j)])
            S.op("act", lambda e, l=l, j=j: e.activation(out=lamv[:, l, j:j + 1], in_=lamv[:, l, j:j + 1], func=AF.Exp),
                 reads=[("lamv", l, j)], writes=[("lamv", l, j)])
        S.op("dve", lambda e, l=l: e.tensor_tensor(out=lamv[:, l, 2:3], in0=lamv[:, l, 0:1], in1=lamv[:, l, 1:2], op=ALU.subtract),
             reads=[("lamv", l, 0), ("lamv", l, 1)], writes=[("lamv", l, 2)])
        S.op("dve", lambda e, l=l: e.tensor_scalar_add(out=lamv[:, l, 2:3], in0=lamv[:, l, 2:3], scalar1=float(lam_inits[l])),
             reads=[("lamv", l, 2)], writes=[("lamv", l, 2)])
        S.op("dve", lambda e, l=l: e.tensor_scalar_mul(out=dgs[:, l, :], in0=dgs[:, l, :], scalar1=float(1.0 - lam_inits[l])),
             reads=["dgs"], writes=["dgs"])

    def load_w(src_ap, kc, ncols):
        i = rot("wb", 2)
        buf = WB[i]
        flat = kc * ncols
        dst = buf[:].rearrange("p a b -> p (a b)")[:, 0:flat].rearrange("p (a b) -> p a b", a=kc)
        S.op("pool", lambda e: e.dma_start(out=dst, in_=src_ap.rearrange("(a p) c -> p a c", p=128)),
             writes=[("wb", i)], chan=ch_w[i])
        return dst, ("wb", i)

    for l in range(depth if STOP >= 0 else 0):
        for wt in range(24):
            wv, wr = load_w(ada_w[l, :, wt * 512:(wt + 1) * 512], 16, 512)
            for j in range(4 if _os.environ.get("KSUB") != "nomm" else 0):
                chn = wt * 4 + j
                pi = rot("ps", 4)
                for kc in range(16):
                    S.op("pe", lambda e, wv=wv, j=j, kc=kc, pi=pi: e.matmul(ps[pi][:, 0:2], lhsT=wv[:, kc, j * 128:(j + 1) * 128],
                                                                         rhs=cvb[:, kc, :], start=(kc == 0), stop=(kc == 15)),
                         reads=[wr, "cvb"], writes=[("ps", pi)])
                S.op("dve", lambda e, l=l, chn=chn, pi=pi: e.tensor_scalar_add(out=modT[:, l, chn, :], in0=ps[pi][:, 0:2],
                                                                            scalar1=adab[:, l, chn:chn + 1]),
                     reads=[("ps", pi), "adab"], writes=[("modT", l)])
        for g in range(2 if _os.environ.get("KSUB") not in ("nomm", "nocoef") else 0):
            def mt(i, l=l, g=g):
                return modT[:, l, i * 16:(i + 1) * 16, g]
            S.op("dve", lambda e, l=l, g=g, mt=mt: e.scalar_tensor_tensor(out=coef[:, l, 0, :, g], in0=mt(1), scalar=1.0, in1=normg[:, l, 0, :],
                                                                          op0=ALU.add, op1=ALU.mult), reads=[("modT", l), "normg"], writes=[("coef", l)])
            S.op("dve", lambda e, l=l, g=g, mt=mt: e.tensor_copy(out=coef[:, l, 1, :, g], in_=mt(0)), reads=[("modT", l)], writes=[("coef", l)])
            S.op("dve", lambda e, l=l, g=g, mt=mt: e.tensor_tensor(out=coef[:, l, 2, :, g], in0=mt(2), in1=normg[:, l, 1, :], op=ALU.mult),
                 reads=[("modT", l), "normg"], writes=[("coef", l)])
            S.op("dve", lambda e, l=l, g=g, mt=mt: e.scalar_tensor_tensor(out=coef[:, l, 3, :, g], in0=mt(4), scalar=1.0, in1=normg[:, l, 2, :],
                                                                          op0=ALU.add, op1=ALU.mult), reads=[("modT", l), "normg"], writes=[("coef", l)])
            S.op("dve", lambda e, l=l, g=g, mt=mt: e.tensor_copy(out=coef[:, l, 4, :, g], in_=mt(3)), reads=[("modT", l)], writes=[("coef", l)])
            S.op("dve", lambda e, l=l, g=g, mt=mt: e.tensor_tensor(out=coef[:, l, 5, :, g], in0=mt(5), in1=normg[:, l, 3, :], op=ALU.mult),
                 reads=[("modT", l), "normg"], writes=[("coef", l)])

    def xview(ap2d, t):
        return ap2d[:, t * TS:(t + 1) * TS].rearrange("(c p) t -> p c t", p=128)

    def rstd_from_psum(pso_i, rs, scale):
        n = rs.shape[-1] if hasattr(rs, "shape") else None
        S.op("act", lambda e: e.activation(out=rs, in_=psO[pso_i][:, 0:rs.shape[1]], func=AF.Sqrt, bias=epsT[:], scale=scale),
             reads=[("psO", pso_i), "epsT"], writes=["rs"])
        S.op("dve", lambda e: e.reciprocal(out=rs, in_=rs), reads=["rs"], writes=["rs"])

    def premix_tile(l, t, xt, kind, xres):
        g = 1 if t < 4 else 0
        ks, kb = (0, 1) if kind == 0 else (3, 4)
        sq = wk["sq"]
        po = rot("psO", 2)
        for c in range(16):
            si = rot("sq", 2)
            S.op("act", lambda e, c=c, si=si: e.activation(out=sq[si][:], in_=xt[:, c, :], func=AF.Square),
                 reads=[xres], writes=[("sq", si)])
            S.op("pe", lambda e, c=c, si=si, po=po: e.matmul(psO[po][:], lhsT=ones[:], rhs=sq[si][:], start=(c == 0), stop=(c == 15)),
                 reads=[("sq", si), "ones"], writes=[("psO", po)])
        rs = wk["rs"]
        rstd_from_psum(po, rs[:], 1.0 / D)
        for c in range(16):
            ti = rot("tmp", 2)
            tmp = wk["tmp"][ti]
            S.op("dve", lambda e, c=c, tmp=tmp: e.scalar_tensor_tensor(out=tmp[:], in0=xt[:, c, :], scalar=coef[:, l, ks, c, g:g + 1], in1=rs[:],
                                                                      op0=ALU.mult, op1=ALU.mult),
                 reads=[xres, "rs", ("coef", l)], writes=[("tmp", ti)])
            S.op("act", lambda e, c=c, tmp=tmp: e.activation(out=hall[:, c, t * TS:(t + 1) * TS], in_=tmp[:], func=AF.Identity,
                                                             bias=coef[:, l, kb, c, g:g + 1], scale=1.0),
                 reads=[("tmp", ti), ("coef", l)], writes=[("hall", t)])

    wk = {}

    def alloc_work(names):
        at = [WORK0, WORK_END]
        wk.clear()
        for nm, shape, dt, n in names:
            if n == 1:
                wk[nm] = sb("wk_%s_%d" % (nm, rot("wkname", 10 ** 9)), shape, dt, at)
            else:
                wk[nm] = [sb("wk_%s_%d" % (nm, rot("wkname", 10 ** 9)), shape, dt, at) for _ in range(n)]

    def fence(tag):
        S.fence(lambda e: e.memset(small[:, 63:64], 0.0))

    import os as _os
    STOP = int(_os.environ.get("KSTOP", "99"))
    for l in range(depth):
        if STOP < 1:
            break
        src_x = xT_in if l == 0 else XS
        fence("s1a")
        alloc_work([("x", [128, 16, TS], F32, 1), ("sq", [128, TS], BF16, 2), ("rs", [128, TS], F32, 1),
                    ("tmp", [128, TS], F32, 2), ("stg", [128, TS], BF16, 4), ("xb", [128, TS], BF16, 2),
                    ("t1", [128, TS], F32, 2), ("fst", [128, 256], F32, 2), ("vst", [128, 256], BF16, 2)])
        for t in range(NT):
            xt = wk["x"]
            S.op("sp", lambda e, t=t, xt=xt: e.dma_start(out=xt[:], in_=xview(src_x, t)), reads=[("XS", t)], writes=["xt"], chan=ch_x)
            premix_tile(l, t, xt, 0, "xt")
        if STOP < 2:
            break
        for wt in range(21):
            wv, wr = load_w(w_in[l, :, wt * 256:(wt + 1) * 256], 16, 256)
            for t in range(NT):
                for j in range(2):
                    m = wt * 2 + j
                    typ = chunk_type(m)
                    if typ in VBASE:
                        continue
                    pi = rot("ps", 4)
                    for kc in range(16):
                        S.op("pe", lambda e, j=j, kc=kc, pi=pi, t=t, wv=wv: e.matmul(ps[pi][:], lhsT=wv[:, kc, j * 128:(j + 1) * 128],
                                                                                   rhs=hall[:, kc, t * TS:(t + 1) * TS],
                                                                                   start=(kc == 0), stop=(kc == 15)),
                             reads=[wr, ("hall", t)], writes=[("ps", pi)])
                    si = rot("stg", 4)
                    stg = wk["stg"][si]
                    if typ in ROPE_T and t < 4 and _os.environ.get("KSUB") != "norope":
                        xi = rot("xb", 2)
                        xb = wk["xb"][xi]
                        t1 = wk["t1"][xi]
                        S.op("act", lambda e, pi=pi, xb=xb: e.copy(out=xb[:], in_=ps[pi][:]), reads=[("ps", pi)], writes=[("xb", xi)])
                        p2 = rot("ps", 4)
                        S.op("pe", lambda e, p2=p2, xb=xb: e.matmul(ps[p2][:], lhsT=perm[:], rhs=xb[:], start=True, stop=True),
                             reads=[("xb", xi), "perm"], writes=[("ps", p2)])
                        S.op("dve", lambda e, xb=xb, t1=t1, t=t: e.tensor_tensor(out=t1[:], in0=xb[:], in1=cosT[:, t * TS:(t + 1) * TS], op=ALU.mult),
                             reads=[("xb", xi), "cosT"], writes=[("t1", xi)])
                        S.op("dve", lambda e, p2=p2, t=t, xb=xb: e.tensor_tensor(out=xb[:], in0=ps[p2][:], in1=sinT[:, t * TS:(t + 1) * TS], op=ALU.mult),
                             reads=[("ps", p2), "sinT"], writes=[("xb", xi)])
                        S.op("dve", lambda e, xb=xb, t1=t1, stg=stg: e.tensor_tensor(out=stg[:], in0=t1[:], in1=xb[:], op=ALU.add),
                             reads=[("xb", xi), ("t1", xi)], writes=[("stg", si)])
                    else:
                        S.op("act", lambda e, pi=pi, stg=stg: e.copy(out=stg[:], in_=ps[pi][:]), reads=[("ps", pi)], writes=[("stg", si)])
                    S.op("sp", lambda e, m=m, t=t, stg=stg: e.dma_start(out=PTs[m * 128:(m + 1) * 128, t * TS:(t + 1) * TS], in_=stg[:]),
                         reads=[("stg", si)], writes=[("PTs", m, t)], chan=ch_stg[si])
                passes = []
                for j in range(2):
                    m = wt * 2 + j
                    typ = chunk_type(m)
                    if typ in VBASE or (typ in KBASE and t == 4):
                        passes.append((j, m, typ))
                if not passes or _os.environ.get("KSUB") == "notok":
                    continue
                j0 = passes[0][0]
                ncol = 128 * len(passes)
                for tb in range(4):
                    pi = rot("ps", 4)
                    tok0 = t * TS + tb * 128
                    for kc in range(16):
                        S.op("pe", lambda e, kc=kc, pi=pi, tok0=tok0, wv=wv, j0=j0, ncol=ncol: e.matmul(
                            ps[pi][:, 0:ncol], lhsT=hall[:, kc, tok0:tok0 + 128], rhs=wv[:, kc, j0 * 128:j0 * 128 + ncol],
                            start=(kc == 0), stop=(kc == 15)), reads=[wr, ("hall", t)], writes=[("ps", pi)])
                    for pj, (j, m, typ) in enumerate(passes):
                        c0 = pj * 128
                        fi = rot("fst", 2)
                        fst = wk["fst"][fi]
                        S.op("act", lambda e, pi=pi, c0=c0, fst=fst: e.copy(out=fst[:, 0:128], in_=ps[pi][:, c0:c0 + 128]),
                             reads=[("ps", pi)], writes=[("fst", fi)])
                        if typ in VBASE:
                            vi = rot("vst", 2)
                            vst = wk["vst"][vi]
                            vc = VCOL[typ] + (m - VBASE[typ]) * 128
                            S.op("dve", lambda e, fst=fst, vst=vst: e.tensor_copy(out=vst[:, 0:128], in_=fst[:, 0:128]),
                                 reads=[("fst", fi)], writes=[("vst", vi)])
                            S.op("sp", lambda e, vst=vst, tok0=tok0, vc=vc: e.dma_start(out=VT[tok0:tok0 + 128, vc:vc + 128], in_=vst[:, 0:128]),
                                 reads=[("vst", vi)], writes=[("VT", m, tok0)], chan=st_sp)
                        if t == 4:
                            sq_i = tb // 2
                            r0 = (tb % 2) * 128
                            if typ in ("na_k", "na_v"):
                                base = KBASE["na_k"] if typ == "na_k" else VBASE["na_v"]
                                dst = ona[sq_i, l, 0 if typ == "na_k" else 1, r0:r0 + 128, (m - base) * 128:(m - base + 1) * 128]
                            elif typ in ("sk", "sv"):
                                dst = oswa[sq_i, l, 0 if typ == "sk" else 1, r0:r0 + 128, :]
                            else:
                                base = KBASE["dk"] if typ == "dk" else VBASE["dv"]
                                dst = odiff[sq_i, l, 0 if typ == "dk" else 1, r0:r0 + 128, (m - base) * 128:(m - base + 1) * 128]
                            S.op("sp", lambda e, fst=fst, dst=dst: e.dma_start(out=dst, in_=fst[:, 0:128]),
                                 reads=[("fst", fi)], chan=ch_out)

        if STOP < 3:
            break
        fence("s2")
        at = [ATT0, WORK_END]
        A = {}

        def asb(nm, shape, dt, n=1):
            if n == 1:
                A[nm] = sb("at_%s_%d" % (nm, rot("wkname", 10 ** 9)), shape, dt, at)
            else:
                A[nm] = [sb("at_%s_%d" % (nm, rot("wkname", 10 ** 9)), shape, dt, at) for _ in range(n)]

        asb("Sb", [128, 2312], F32, 2)
        asb("P", [128, 2304], BF16, 2)
        asb("Pt", [128, 8, 128], BF16, 2)
        asb("KT", [128, 2304], BF16)
        asb("VD", [128, 18, 128], BF16)
        asb("QT", [128, 4, 128], BF16, 2)
        asb("KW", [128, 4, 576], BF16)
        asb("VW", [128, 5, 512], BF16)
        asb("bias", [128, 8, 576], BF16)
        asb("cK", [128, 4, 256], BF16)
        asb("cV", [128, 2, 512], BF16)
        asb("otok", [128, 128], BF16, 2)
        asb("junk", [128, 128], F32)
        asb("cz", [128, 3, 516], BF16)
        asb("cacc", [128, 512], F32, 2)
        mixT = hall

        def scores(qT, kT, nk, sbi, off, bias=None, kres=()):
            Sb = A["Sb"][sbi]
            for c0 in range(0, nk, 512):
                n = min(512, nk - c0)
                pi = rot("ps", 4)
                S.op("pe", lambda e, pi=pi, n=n, c0=c0: e.matmul(ps[pi][:, 0:n], lhsT=qT, rhs=kT[:, c0:c0 + n], start=True, stop=True),
                     reads=["QT"] + list(kres), writes=[("ps", pi)])
                if bias is None:
                    S.op("act", lambda e, pi=pi, n=n, c0=c0: e.activation(out=Sb[:, off + c0:off + c0 + n], in_=ps[pi][:, 0:n], func=AF.Identity, scale=SCALE),
                         reads=[("ps", pi)], writes=[("Sb", sbi, off + c0)])
                else:
                    S.op("dve", lambda e, pi=pi, n=n, c0=c0: e.scalar_tensor_tensor(out=Sb[:, off + c0:off + c0 + n], in0=ps[pi][:, 0:n], scalar=SCALE,
                                                                                   in1=bias[:, c0:c0 + n], op0=ALU.mult, op1=ALU.add),
                         reads=[("ps", pi), "bias"], writes=[("Sb", sbi, off + c0)])
            return [("Sb", sbi, off + c0) for c0 in range(0, nk, 512)]

        def softmax(sbi, W, sres, k):
            Sb = A["Sb"][sbi]
            P = A["P"][sbi]
            mx = small[:, 4 * k:4 * k + 1]
            nmx = small[:, 4 * k + 1:4 * k + 2]
            rs = small[:, 4 * k + 2:4 * k + 3]
            sm = ("small", k)
            S.op("dve", lambda e: e.tensor_reduce(out=mx, in_=Sb[:, 0:W], axis=AX.X, op=ALU.max), reads=sres, writes=[sm])
            S.op("dve", lambda e: e.tensor_scalar_mul(out=nmx, in0=mx, scalar1=-1.0), reads=[sm], writes=[sm])
            S.op("dve", lambda e: e.memset(rs, 0.0), writes=[sm])
            Wp = min(W, 2304)
            S.op("act", lambda e: e.activation(out=P[:, 0:Wp], in_=Sb[:, 0:Wp], func=AF.Exp, bias=nmx, scale=1.0, accum_out=rs),
                 reads=sres + [sm], writes=[("P", sbi), sm])
            if W > Wp:
                S.op("act", lambda e: e.activation(out=Sb[:, Wp:W], in_=Sb[:, Wp:W], func=AF.Exp, bias=nmx, scale=1.0, accum_out=rs),
                     reads=sres + [sm], writes=[sm] + sres)
            S.op("dve", lambda e: e.reciprocal(out=rs, in_=rs), reads=[sm], writes=[sm])
            return rs

        def pv(sbi, blocks, dv, vres):
            P = A["P"][sbi]
            po = rot("psO", 2)
            nb = len(blocks)
            for g0 in range(0, nb, 8):
                grp = blocks[g0:g0 + 8]
                ti = rot("psT", 2)
                pti = rot("Pt", 2)
                Pt = A["Pt"][pti]
                for i, (off, nk, vap) in enumerate(grp):
                    S.op("pe", lambda e, i=i, off=off, nk=nk, ti=ti: e.transpose(out=psT[ti][0:nk, i * 128:(i + 1) * 128], in_=P[:, off:off + nk], identity=ident[:]),
                         reads=[("P", sbi), "ident"], writes=[("psT", ti)])
                ng = len(grp)
                full = all(nk == 128 for (_, nk, _) in grp)
                eng = "act" if rot("pte", 2) == 0 else "dve"
                if full:
                    if eng == "act":
                        S.op("act", lambda e, ti=ti, ng=ng, Pt=Pt: e.copy(out=Pt[:, 0:ng, :].rearrange("p a b -> p (a b)"), in_=psT[ti][:, 0:ng * 128]),
                             reads=[("psT", ti)], writes=[("Pt", pti)])
                    else:
                        S.op("dve", lambda e, ti=ti, ng=ng, Pt=Pt: e.tensor_copy(out=Pt[:, 0:ng, :].rearrange("p a b -> p (a b)"), in_=psT[ti][:, 0:ng * 128]),
                             reads=[("psT", ti)], writes=[("Pt", pti)])
                else:
                    for i, (off, nk, vap) in enumerate(grp):
                        S.op("dve", lambda e, i=i, nk=nk, ti=ti, Pt=Pt: e.tensor_copy(out=Pt[0:nk, i, :], in_=psT[ti][0:nk, i * 128:(i + 1) * 128]),
                             reads=[("psT", ti)], writes=[("Pt", pti)])
                for i, (off, nk, vap) in enumerate(grp):
                    S.op("pe", lambda e, i=i, nk=nk, vap=vap, po=po, first=(g0 + i == 0), last=(g0 + i == nb - 1), Pt=Pt: e.matmul(
                        psO[po][:, 0:dv], lhsT=Pt[0:nk, i, :], rhs=vap, start=first, stop=last),
                        reads=[("Pt", pti)] + list(vres), writes=[("psO", po)])
            return po

        def put_mix(oi, chunk, tok0):
            ot = A["otok"][oi]
            ti = rot("psT", 2)
            S.op("pe", lambda e: e.transpose(out=psT[ti][:, 0:128], in_=ot[:], identity=ident[:]), reads=[("otok", oi), "ident"], writes=[("psT", ti)])
            S.op("act", lambda e: e.copy(out=mixT[:, chunk, tok0:tok0 + 128], in_=psT[ti][:, 0:128]), reads=[("psT", ti)],
                 writes=[("hall", tok0 // TS)])

        def plain_head(qT, segs, sink, hloc, oi, vres, kres):
            off = 0
            sres = []
            blocks = []
            for (kT, nk, bias, vbl) in segs:
                sres += scores(qT, kT, nk, 0, off, bias, kres)
                o2 = off
                for (nkb, vap) in vbl:
                    blocks.append((o2, nkb, vap))
                    o2 += nkb
                off += nk
            W = off
            if sink is not None:
                Sb = A["Sb"][0]
                S.op("dve", lambda e, W=W: e.tensor_copy(out=Sb[:, W:W + 1], in_=sink), reads=["sinkT"], writes=[("Sb", 0, "sink")])
                sres.append(("Sb", 0, "sink"))
                W += 1
            Pw = off
            rs = softmax_w(0, W, Pw, sres, 0)
            po = pv(0, blocks, 64, vres)
            ot = A["otok"][oi]
            S.op("act", lambda e, po=po: e.activation(out=ot[:, hloc * 64:(hloc + 1) * 64], in_=psO[po][:, 0:64], func=AF.Identity, scale=rs),
                 reads=[("psO", po), ("small", 0)], writes=[("otok", oi)])

        def softmax_w(sbi, W, Pw, sres, k):
            Sb = A["Sb"][sbi]
            P = A["P"][sbi]
            mx = small[:, 4 * k:4 * k + 1]
            nmx = small[:, 4 * k + 1:4 * k + 2]
            rs = small[:, 4 * k + 2:4 * k + 3]
            r2 = small[:, 4 * k + 3:4 * k + 4]
            sm = ("small", k)
            S.op("dve", lambda e: e.tensor_reduce(out=mx, in_=Sb[:, 0:W], axis=AX.X, op=ALU.max), reads=sres, writes=[sm])
            S.op("dve", lambda e: e.tensor_scalar_mul(out=nmx, in0=mx, scalar1=-1.0), reads=[sm], writes=[sm])
            S.op("dve", lambda e: e.memset(rs, 0.0), writes=[sm])
            S.op("act", lambda e: e.activation(out=P[:, 0:Pw], in_=Sb[:, 0:Pw], func=AF.Exp, bias=nmx, scale=1.0, accum_out=rs),
                 reads=sres + [sm], writes=[("P", sbi), sm])
            if W > Pw:
                S.op("act", lambda e: e.activation(out=r2, in_=Sb[:, Pw:W], func=AF.Exp, bias=nmx, scale=1.0), reads=sres + [sm], writes=[sm])
                S.op("dve", lambda e: e.tensor_tensor(out=rs, in0=rs, in1=r2, op=ALU.add), reads=[sm], writes=[sm])
            S.op("dve", lambda e: e.reciprocal(out=rs, in_=rs), reads=[sm], writes=[sm])
            return rs

        def diff_head(qT, segs, oi, vres, kres):
            rss = []
            W = sum(s[1] for s in segs)
            for side in range(2):
                off = 0
                sres = []
                for (kT, nk, vbl) in segs:
                    sres += scores(qT[side * 64:(side + 1) * 64, :], kT[side * 64:(side + 1) * 64, :], nk, side, off, None, kres)
                    off += nk
                rss.append(softmax_w(side, W, W, sres, side))
            blocks = []
            off = 0
            for (kT, nk, vbl) in segs:
                o2 = off
                for (nkb, vap) in vbl:
                    blocks.append((o2, nkb, vap))
                    o2 += nkb
                off += nk
            P1, P2 = A["P"][0], A["P"][1]
            lr2 = small[:, 12:13]
            S.op("dve", lambda e: e.tensor_tensor(out=lr2, in0=rss[1], in1=lamv[:, l, 2:3], op=ALU.mult),
                 reads=[("small", 1), ("lamv", l, 2)], writes=[("small", 3)])
            S.op("dve", lambda e: e.tensor_scalar_mul(out=P2[:, 0:W], in0=P2[:, 0:W], scalar1=lr2),
                 reads=[("P", 1), ("small", 3)], writes=[("P", 1)])
            S.op("dve", lambda e: e.scalar_tensor_tensor(out=P1[:, 0:W], in0=P1[:, 0:W], scalar=rss[0], in1=P2[:, 0:W], op0=ALU.mult, op1=ALU.subtract),
                 reads=[("P", 0), ("P", 1), ("small", 0)], writes=[("P", 0)])
            po = pv(0, blocks, 128, vres)
            ss = small[:, 13:14]
            S.op("dve", lambda e: e.memset(ss, 0.0), writes=[("small", 4)])
            S.op("act", lambda e, po=po: e.activation(out=A["junk"][:], in_=psO[po][:, 0:128], func=AF.Square, accum_out=ss),
                 reads=[("psO", po), ("small", 4)], writes=[("small", 4), "junk"])
            S.op("act", lambda e: e.activation(out=ss, in_=ss, func=AF.Sqrt, bias=epsT[:], scale=1.0 / 128), reads=[("small", 4), "epsT"], writes=[("small", 4)])
            S.op("dve", lambda e: e.reciprocal(out=ss, in_=ss), reads=[("small", 4)], writes=[("small", 4)])
            ot = A["otok"][oi]
            S.op("dve", lambda e, po=po: e.scalar_tensor_tensor(out=ot[:], in0=psO[po][:, 0:128], scalar=ss, in1=dgs[:, l, :], op0=ALU.mult, op1=ALU.mult),
                 reads=[("psO", po), ("small", 4), "dgs"], writes=[("otok", oi)])

        def ld(dst, src, names_w, reads=(), eng="sp", ci=0):
            S.op(eng, lambda e: e.dma_start(out=dst, in_=src), reads=list(reads), writes=list(names_w), chan=(ld_sp if eng == "sp" else ld_pl))

        def pts_reads(chunks, t0, t1):
            return [("PTs", m, t) for m in chunks for t in range(t0 // TS, (t1 - 1) // TS + 1)]

        def vt_reads(typ, t0, t1):
            n = 1 if typ == "sv" else 4
            return [("VT", VBASE[typ] + k, tk) for k in range(n) for tk in range(t0 // 128 * 128, t1, 128)]

        def conv_seg(tok0, n, left, right):
            cz = A["cz"]
            for cc in range(4):
                lo = tok0 - (1 if left else 0)
                hi = tok0 + n + (1 if right else 0)
                o = 0 if left else 1
                for k, base in enumerate((12, 20, 16)):
                    m = base + cc
                    ld(cz[:, k, o:o + hi - lo], PTs[m * 128:(m + 1) * 128, lo:hi], [("cz", k)], pts_reads([m], lo, hi), "sp", 1)
                if not left:
                    S.op("pool", lambda e: e.memset(cz[:, 0:2, 0:1], 0.0), writes=[("cz", 0), ("cz", 1)])
                if not right:
                    S.op("pool", lambda e, n=n: e.memset(cz[:, 0:2, n + 1:n + 2], 0.0), writes=[("cz", 0), ("cz", 1)])
                S.op("pool", lambda e, n=n: e.tensor_tensor(out=cz[:, 0, 0:n + 2], in0=cz[:, 0, 0:n + 2], in1=cz[:, 1, 0:n + 2], op=ALU.mult),
                     reads=[("cz", 0), ("cz", 1)], writes=[("cz", 0)])
                ai = rot("cacc", 2)
                acc = A["cacc"][ai]
                S.op("pool", lambda e, n=n, cc=cc, acc=acc: e.tensor_scalar_mul(out=acc[:, 0:n], in0=cz[:, 0, 0:n], scalar1=convw[:, l, cc, 0:1]),
                     reads=[("cz", 0), "convw"], writes=[("cacc", ai)])
                for k in (1, 2):
                    S.op("dve", lambda e, n=n, cc=cc, acc=acc, k=k: e.scalar_tensor_tensor(out=acc[:, 0:n], in0=cz[:, 0, k:k + n], scalar=convw[:, l, cc, k:k + 1],
                                                                                         in1=acc[:, 0:n], op0=ALU.mult, op1=ALU.add),
                         reads=[("cz", 0), "convw", ("cacc", ai)], writes=[("cacc", ai)])
                S.op("pool", lambda e, n=n, cc=cc, acc=acc: e.tensor_tensor(out=mixT[:, 4 + cc, tok0:tok0 + n], in0=acc[:, 0:n], in1=cz[:, 2, 1:n + 1], op=ALU.mult),
                     reads=[("cacc", ai), ("cz", 2)], writes=[("hall", tok0 // TS)])

        for t in range(4):
            conv_seg(t * TS, TS, t > 0, t < 3)
        conv_seg(2048, 256, False, False)
        conv_seg(2304, 256, False, False)

        ld(A["cK"][:], cnaKT[l].rearrange("c p k -> p c k"), ["cK"], (), "pool", 2)
        ld(A["cV"][:], cnaV[l].rearrange("(b p) c -> p b c", p=128), ["cV"], (), "pool", 2)
        for i in range(16):
            tok0 = i * 128
            k0 = na_ks(i) * 64
            qi = rot("QT", 2)
            QT = A["QT"][qi]
            ld(QT[:], PTs[0:512, tok0:tok0 + 128].rearrange("(c p) t -> p c t", p=128), ["QT"], pts_reads(range(0, 4), tok0, tok0 + 128), "sp", 0)
            ld(A["KW"][:], PTs[512:1024, k0:k0 + 576].rearrange("(c p) t -> p c t", p=128), ["KW"], pts_reads(range(4, 8), k0, k0 + 576), "sp", 0)
            ld(A["VW"][:, 0:4, :], VT[k0:k0 + 512, 0:512].rearrange("(b p) c -> p b c", p=128), ["VW"], vt_reads("na_v", k0, k0 + 576), "sp", 0)
            ld(A["VW"][0:64, 4, :], VT[k0 + 512:k0 + 576, 0:512], ["VW"], (), "sp", 0)
            ld(A["bias"][:], nab[l, na_pat(i)].rearrange("h p k -> p h k"), ["bias"], (), "pool", 2)
            for h in range(8):
                chn, r0 = h // 2, (h % 2) * 64
                segs = [(A["KW"][r0:r0 + 64, chn, :], 576, A["bias"][:, h, :],
                         [(128, A["VW"][:, b, h * 64:(h + 1) * 64]) for b in range(4)] + [(64, A["VW"][0:64, 4, h * 64:(h + 1) * 64])]),
                        (A["cK"][r0:r0 + 64, chn, :], 256, None, [(128, A["cV"][:, b, h * 64:(h + 1) * 64]) for b in range(2)])]
                oi = (h // 2) % 2
                plain_head(QT[r0:r0 + 64, chn, :], segs, None, h % 2, oi, ["VW", "cV"], ["KW", "cK"])
                if h % 2 == 1:
                    put_mix(oi, chn, tok0)
        for s in range(2):
            p0 = 2048 + 256 * s
            ld(A["KW"][:, :, 0:256], PTs[512:1024, p0:p0 + 256].rearrange("(c p) t -> p c t", p=128), ["KW"], pts_reads(range(4, 8), p0, p0 + 256), "sp", 0)
            ld(A["VW"][:, 0:2, :], VT[p0:p0 + 256, 0:512].rearrange("(b p) c -> p b c", p=128), ["VW"], vt_reads("na_v", p0, p0 + 256), "sp", 0)
            for qt in range(2):
                tok0 = p0 + qt * 128
                qi = rot("QT", 2)
                QT = A["QT"][qi]
                ld(QT[:], PTs[0:512, tok0:tok0 + 128].rearrange("(c p) t -> p c t", p=128), ["QT"], pts_reads(range(0, 4), tok0, tok0 + 128), "sp", 0)
                for h in range(8):
                    chn, r0 = h // 2, (h % 2) * 64
                    segs = [(A["KW"][r0:r0 + 64, chn, 0:256], 256, None, [(128, A["VW"][:, b, h * 64:(h + 1) * 64]) for b in range(2)])]
                    oi = (h // 2) % 2
                    plain_head(QT[r0:r0 + 64, chn, :], segs, None, h % 2, oi, ["VW"], ["KW"])
                    if h % 2 == 1:
                        put_mix(oi, chn, tok0)

        KS = A["KW"]

        def load_swa_k(g, dstcols, src_rows_ap, reads):
            for half in range(2):
                ld(KS[half * 64:(half + 1) * 64, g, dstcols[0]:dstcols[1]], src_rows_ap, ["KW"], reads, "sp", 0)

        for g in range(2):
            for half in range(2):
                ld(A["cK"][half * 64:(half + 1) * 64, g, :], cswaKT[l, g * 64:(g + 1) * 64, :], ["cK"], (), "pool", 2)
        ld(A["cV"][:, :, 0:128], cswaV[l].rearrange("(b p) c -> p b c", p=128), ["cV"], (), "pool", 2)
        for jq in range(16):
            tok0 = jq * 128
            ws = min(max(128 * (jq - 1), 0), 2048 - 384)
            pat = 0 if jq == 0 else (2 if jq == 15 else 1)
            qi = rot("QT", 2)
            QT = A["QT"][qi]
            ld(QT[:], PTs[24 * 128:28 * 128, tok0:tok0 + 128].rearrange("(c p) t -> p c t", p=128), ["QT"], pts_reads(range(24, 28), tok0, tok0 + 128), "sp", 0)
            for g in range(2):
                load_swa_k(g, (0, 384), PTs[28 * 128 + g * 64:28 * 128 + (g + 1) * 64, ws:ws + 384], pts_reads([28], ws, ws + 384))
            ld(A["VW"][:, 0:3, 0:128], VT[ws:ws + 384, 512:640].rearrange("(b p) c -> p b c", p=128), ["VW"], vt_reads("sv", ws, ws + 384), "sp", 0)
            for h in range(8):
                g, chn, r0 = h // 4, h // 2, (h % 2) * 64
                segs = [(KS[r0:r0 + 64, g, 0:384], 384, swamask[:, pat, :], [(128, A["VW"][:, b, g * 64:(g + 1) * 64]) for b in range(3)]),
                        (A["cK"][r0:r0 + 64, g, :], 256, None, [(128, A["cV"][:, b, g * 64:(g + 1) * 64]) for b in range(2)])]
                oi = (h // 2) % 2
                plain_head(QT[r0:r0 + 64, chn, :], segs, sinkT[:, l, h:h + 1], h % 2, oi, ["VW", "cV"], ["KW", "cK", "swamask"])
                if h % 2 == 1:
                    put_mix(oi, 8 + chn, tok0)
        for s in range(2):
            p0 = 2048 + 256 * s
            for g in range(2):
                load_swa_k(g, (0, 256), PTs[28 * 128 + g * 64:28 * 128 + (g + 1) * 64, p0:p0 + 256], pts_reads([28], p0, p0 + 256))
            ld(A["VW"][:, 0:2, 0:128], VT[p0:p0 + 256, 512:640].rearrange("(b p) c -> p b c", p=128), ["VW"], vt_reads("sv", p0, p0 + 256), "sp", 0)
            for qt in range(2):
                tok0 = p0 + qt * 128
                qi = rot("QT", 2)
                QT = A["QT"][qi]
                ld(QT[:], PTs[24 * 128:28 * 128, tok0:tok0 + 128].rearrange("(c p) t -> p c t", p=128), ["QT"], pts_reads(range(24, 28), tok0, tok0 + 128), "sp", 0)
                for h in range(8):
                    g, chn, r0 = h // 4, h // 2, (h % 2) * 64
                    segs = [(KS[r0:r0 + 64, g, 0:256], 256, None, [(128, A["VW"][:, b, g * 64:(g + 1) * 64]) for b in range(2)])]
                    oi = (h // 2) % 2
                    plain_head(QT[r0:r0 + 64, chn, :], segs, sinkT[:, l, h:h + 1], h % 2, oi, ["VW"], ["KW"])
                    if h % 2 == 1:
                        put_mix(oi, 8 + chn, tok0)

        for h in range(4):
            KT, VD = A["KT"], A["VD"]
            ld(KT[:, 0:2048], PTs[(34 + h) * 128:(35 + h) * 128, 0:2048], ["KT"], pts_reads([34 + h], 0, 2048), "sp", 3)
            ld(KT[:, 2048:2304], cdiffKT[l, h], ["KT"], (), "pool", 2)
            ld(VD[:, 0:16, :], VT[0:2048, 640 + h * 128:640 + (h + 1) * 128].rearrange("(b p) c -> p b c", p=128), ["VD"], vt_reads("dv", 0, 2048), "sp", 3)
            ld(VD[:, 16:18, :], cdiffV[l, :, h * 128:(h + 1) * 128].rearrange("(b p) c -> p b c", p=128), ["VD"], (), "pool", 2)
            for i in range(16):
                tok0 = i * 128
                qi = rot("QT", 2)
                QT = A["QT"][qi]
                ld(QT[:, 0, :], PTs[(30 + h) * 128:(31 + h) * 128, tok0:tok0 + 128], ["QT"], pts_reads([30 + h], tok0, tok0 + 128), "sp", 0)
                segs = [(KT[:, :], 2304, [(128, VD[:, b, :]) for b in range(18)])]
                oi = rot("otokd", 2)
                diff_head(QT[:, 0, :], segs, oi, ["VD"], ["KT"])
                put_mix(oi, 12 + h, tok0)
            for s in range(2):
                p0 = 2048 + 256 * s
                ld(KT[:, 0:256], PTs[(34 + h) * 128:(35 + h) * 128, p0:p0 + 256], ["KT"], pts_reads([34 + h], p0, p0 + 256), "sp", 3)
                ld(VD[:, 0:2, :], VT[p0:p0 + 256, 640 + h * 128:640 + (h + 1) * 128].rearrange("(b p) c -> p b c", p=128), ["VD"], vt_reads("dv", p0, p0 + 256), "sp", 3)
                for qt in range(2):
                    tok0 = p0 + qt * 128
                    qi = rot("QT", 2)
                    QT = A["QT"][qi]
                    ld(QT[:, 0, :], PTs[(30 + h) * 128:(31 + h) * 128, tok0:tok0 + 128], ["QT"], pts_reads([30 + h], tok0, tok0 + 128), "sp", 0)
                    segs = [(KT[:, 0:256], 256, [(128, VD[:, b, :]) for b in range(2)])]
                    oi = rot("otokd", 2)
                    diff_head(QT[:, 0, :], segs, oi, ["VD"], ["KT"])
                    put_mix(oi, 12 + h, tok0)

        if STOP < 4:
            break
        fence("s3")
        alloc_work([("x", [128, 16, TS], F32, 1), ("y", [128, 16, 256], F32, 1), ("sq", [128, TS], BF16, 2), ("rs", [128, TS], F32, 1),
                    ("tmp", [128, TS], F32, 2)])

        def branch_update(l, t, half, xt, y, kc, po):
            g = 1 if t < 4 else 0
            rs = wk["rs"]
            rstd_from_psum(po, rs[:, 0:256], 1.0 / D)
            for c in range(16):
                ti = rot("tmp", 2)
                tmp = wk["tmp"][ti]
                S.op("dve", lambda e, c=c, tmp=tmp: e.scalar_tensor_tensor(out=tmp[:, 0:256], in0=y[:, c, :], scalar=coef[:, l, kc, c, g:g + 1], in1=rs[:, 0:256],
                                                                          op0=ALU.mult, op1=ALU.mult), reads=["y", "rs", ("coef", l)], writes=[("tmp", ti)])
                S.op("pool", lambda e, c=c, tmp=tmp: e.tensor_tensor(out=xt[:, c, half * 256:(half + 1) * 256], in0=xt[:, c, half * 256:(half + 1) * 256],
                                                                    in1=tmp[:, 0:256], op=ALU.add), reads=[("tmp", ti), "xt"], writes=["xt"])

        for t in range(NT):
            xt = wk["x"]
            y = wk["y"]
            S.op("sp", lambda e, t=t, xt=xt: e.dma_start(out=xt[:], in_=xview(src_x, t)), reads=[("XS", t)], writes=["xt"], chan=ch_x)
            for half in range(2):
                c0 = t * TS + half * 256
                po = rot("psO", 2)
                for wt in range(4):
                    wv, wr = load_w(w_out[l, :, wt * 512:(wt + 1) * 512], 16, 512)
                    for j in range(4):
                        m = wt * 4 + j
                        pi = rot("ps", 4)
                        for kc in range(16):
                            S.op("pe", lambda e, j=j, kc=kc, pi=pi, wv=wv, c0=c0: e.matmul(ps[pi][:, 0:256], lhsT=wv[:, kc, j * 128:(j + 1) * 128],
                                                                                         rhs=mixT[:, kc, c0:c0 + 256], start=(kc == 0), stop=(kc == 15)),
                                 reads=[wr, ("hall", t)], writes=[("ps", pi)])
                        S.op("dve", lambda e, m=m, pi=pi: e.tensor_copy(out=y[:, m, :], in_=ps[pi][:, 0:256]), reads=[("ps", pi)], writes=["y"])
                        si = rot("sq", 2)
                        sq = wk["sq"][si]
                        S.op("act", lambda e, m=m, sq=sq: e.activation(out=sq[:, 0:256], in_=y[:, m, :], func=AF.Square), reads=["y"], writes=[("sq", si)])
                        S.op("pe", lambda e, m=m, sq=sq, po=po: e.matmul(psO[po][:, 0:256], lhsT=ones[:], rhs=sq[:, 0:256], start=(m == 0), stop=(m == 15)),
                             reads=[("sq", si), "ones"], writes=[("psO", po)])
                branch_update(l, t, half, xt, y, 2, po)
            S.op("sp", lambda e, t=t, xt=xt: e.dma_start(out=xview(XS, t), in_=xt[:]), reads=["xt"], writes=[("XS", t)], chan=ch_xo)
            premix_tile(l, t, xt, 1, "xt")

        if STOP < 5:
            break
        alloc_work([("r", [128, TS], F32, 2), ("astg", [128, TS], BF16, 4)])
        fence("s4")
        for wt in range(16):
            wv, wr = load_w(w1[l, :, wt * 512:(wt + 1) * 512], 16, 512)
            for t in range(NT):
                for j in range(4):
                    m = wt * 4 + j
                    pi = rot("ps", 4)
                    for kc in range(16):
                        S.op("pe", lambda e, j=j, kc=kc, pi=pi, t=t, wv=wv: e.matmul(ps[pi][:], lhsT=wv[:, kc, j * 128:(j + 1) * 128],
                                                                                   rhs=hall[:, kc, t * TS:(t + 1) * TS], start=(kc == 0), stop=(kc == 15)),
                             reads=[wr, ("hall", t)], writes=[("ps", pi)])
                    ri = rot("r", 2)
                    r = wk["r"][ri]
                    ai = rot("astg", 4)
                    a = wk["astg"][ai]
                    S.op("act", lambda e, pi=pi, r=r: e.activation(out=r[:], in_=ps[pi][:], func=AF.Relu), reads=[("ps", pi)], writes=[("r", ri)])
                    S.op("dve", lambda e, r=r, a=a: e.tensor_tensor(out=a[:], in0=r[:], in1=r[:], op=ALU.mult), reads=[("r", ri)], writes=[("astg", ai)])
                    S.op("sp", lambda e, m=m, t=t, a=a: e.dma_start(out=AH[m * 128:(m + 1) * 128, t * TS:(t + 1) * TS], in_=a[:]),
                         reads=[("astg", ai)], writes=[("AH", m, t)], chan=ch_stg[ai])

        if STOP < 6:
            break
        fence("s5")
        alloc_work([("y", [128, 16, TS], F32, 1), ("sq", [128, TS], BF16, 2), ("rs", [128, TS], F32, 1),
                    ("tmp", [128, TS], F32, 2), ("xc", [128, TS], F32, 3)])
        at_flat = hall[:].rearrange("p a b -> p (a b)")
        last = (l == depth - 1)
        for t in range(NT):
            y = wk["y"]
            g = 1 if t < 4 else 0
            a_t = at_flat[:, 0:64 * TS].rearrange("p (a b) -> p a b", a=64)
            S.op("sp", lambda e, t=t, a_t=a_t: e.dma_start(out=a_t, in_=AH[:, t * TS:(t + 1) * TS].rearrange("(a p) t -> p a t", p=128)),
                 reads=[("AH", m, t) for m in range(64)], writes=[("hall", k) for k in range(NT)], chan=ch_ld[4])
            po = rot("psO", 2)
            for m in range(16):
                i = rot("wb", 2)
                buf = WB[i]
                w2v = buf[:].rearrange("p a b -> p (a b)")[:, 0:64 * 128].rearrange("p (a b) -> p a b", a=64)
                S.op("pool", lambda e, m=m, w2v=w2v: e.dma_start(out=w2v, in_=w2[l, :, m * 128:(m + 1) * 128].rearrange("(a p) c -> p a c", p=128)),
                     writes=[("wb", i)], chan=ch_w[i])
                pi = rot("ps", 4)
                for kc in range(64):
                    S.op("pe", lambda e, kc=kc, pi=pi, w2v=w2v, a_t=a_t: e.matmul(ps[pi][:], lhsT=w2v[:, kc, :], rhs=a_t[:, kc, :],
                                                                                start=(kc == 0), stop=(kc == 63)),
                         reads=[("wb", i), ("hall", 0)], writes=[("ps", pi)])
                S.op("dve", lambda e, m=m, pi=pi: e.tensor_copy(out=y[:, m, :], in_=ps[pi][:]), reads=[("ps", pi)], writes=["y"])
                si = rot("sq", 2)
                sq = wk["sq"][si]
                S.op("act", lambda e, m=m, sq=sq: e.activation(out=sq[:], in_=y[:, m, :], func=AF.Square), reads=["y"], writes=[("sq", si)])
                S.op("pe", lambda e, m=m, sq=sq, po=po: e.matmul(psO[po][:], lhsT=ones[:], rhs=sq[:], start=(m == 0), stop=(m == 15)),
                     reads=[("sq", si), "ones"], writes=[("psO", po)])
            rs = wk["rs"]
            rstd_from_psum(po, rs[:], 1.0 / D)
            for c in range(16):
                xi = rot("xc", 3)
                xc = wk["xc"][xi]
                S.op("sp", lambda e, c=c, t=t, xc=xc: e.dma_start(out=xc[:], in_=XS[c * 128:(c + 1) * 128, t * TS:(t + 1) * TS]),
                     reads=[("XS", t)], writes=[("xc", xi)], chan=ch_ld[5])
                ti = rot("tmp", 2)
                tmp = wk["tmp"][ti]
                S.op("dve", lambda e, c=c, tmp=tmp, g=g: e.scalar_tensor_tensor(out=tmp[:], in0=y[:, c, :], scalar=coef[:, l, 5, c, g:g + 1], in1=rs[:],
                                                                           op0=ALU.mult, op1=ALU.mult), reads=["y", "rs", ("coef", l)], writes=[("tmp", ti)])
                S.op("pool", lambda e, xc=xc, tmp=tmp: e.tensor_tensor(out=xc[:], in0=xc[:], in1=tmp[:], op=ALU.add),
                     reads=[("tmp", ti), ("xc", xi)], writes=[("xc", xi)])
                if last:
                    S.op("sp", lambda e, c=c, t=t, xc=xc: e.dma_start(out=yT[c * 128:(c + 1) * 128, t * TS:(t + 1) * TS], in_=xc[:]),
                         reads=[("xc", xi)], chan=ch_out)
                else:
                    S.op("sp", lambda e, c=c, t=t, xc=xc: e.dma_start(out=XS[c * 128:(c + 1) * 128, t * TS:(t + 1) * TS], in_=xc[:]),
                         reads=[("xc", xi)], writes=[("XSo", t, c)], chan=ch_xo)
        if not last:
            for t in range(NT):
                S.op("dve", lambda e: e.memset(small[:, 62:63], 0.0), reads=[("XSo", t, c) for c in range(16)], writes=[("XS", t)])

    cnt = S.finalize(final_waits=STP)
    S.emit()
    return nc, len(S.ops), cnt


def _na_bias_tables(rpb):
    out = np.empty((rpb.shape[0], 5, 8, 128, 576), np.float32)
    for pi, i in enumerate((0, 1, 2, 14, 15)):
        ks = na_ks(i)
        q = np.arange(128)
        r = 2 * i + q // 64
        c = q % 64
        k = np.arange(576)
        kr = ks + k // 64
        kc = k % 64
        rs = np.clip(r - 4, 0, 24)
        row_ok = (kr[None, :] >= rs[:, None]) & (kr[None, :] < rs[:, None] + 8)
        cs = np.clip(c - 8, 0, 48)
        col_ok = (kc[None, :] >= cs[:, None]) & (kc[None, :] < cs[:, None] + 16)
        ok = row_ok & col_ok
        dr = np.clip(kr[None, :] - r[:, None] + 7, 0, 14)
        dc = np.clip(kc[None, :] - c[:, None], -15, 15) + 15
        g = rpb[:, :, dr, dc]
        out[:, pi] = np.where(ok[None, None], g, np.float32(NEG))
    return out


def _rope_tables():
    t = np.arange(2048)
    rows = (t // 64).astype(np.float32)
    cols = (t % 64).astype(np.float32)
    n = 16
    inv = (np.float32(10000.0) ** (-np.arange(n, dtype=np.float32) / n)).astype(np.float32)
    cosT = np.zeros((128, 2048), np.float32)
    sinT = np.zeros((128, 2048), np.float32)
    for p in range(128):
        d = p % 64
        pos = rows if d < 32 else cols
        j = d % 16
        ang = pos * inv[j]
        cosT[p] = np.cos(ang)
        sinT[p] = -np.sin(ang) if (d % 32) < 16 else np.sin(ang)
    perm = np.zeros((128, 128), np.float32)
    for m in range(128):
        partner = m + 16 if (m % 32) < 16 else m - 16
        perm[partner, m] = 1.0
    return cosT, sinT, perm


def _swa_masks():
    out = np.zeros((128, 3, 384), np.float32)
    for pat, jq in enumerate((0, 5, 15)):
        ws = min(max(128 * (jq - 1), 0), 2048 - 384)
        q = jq * 128 + np.arange(128)
        k = ws + np.arange(384)
        ok = np.abs(q[:, None] - k[None, :]) <= 128
        out[:, pat, :] = np.where(ok, 0.0, NEG)
    return out


_CACHE = {}


def kernel(x_prompt, x_sample, cache_na_kv, cache_swa_kv, cache_diff_kv, c, c_ctx,
           ada_w, ada_b, norm_g, w_in, conv_w, na_rpb, swa_sink, diff_lambda, diff_norm_g,
           w_out, mlp_w1, mlp_w2, _depth=DEPTH):
    f32 = np.float32
    A = lambda a: np.ascontiguousarray(np.asarray(a, dtype=f32))
    x_prompt, x_sample = A(x_prompt), A(x_sample)
    cache_na_kv, cache_swa_kv, cache_diff_kv = A(cache_na_kv), A(cache_swa_kv), A(cache_diff_kv)
    c, c_ctx = A(c), A(c_ctx)
    ada_w, ada_b, norm_g, w_in, conv_w = A(ada_w), A(ada_b), A(norm_g), A(w_in), A(conv_w)
    na_rpb, swa_sink, diff_lambda, diff_norm_g = A(na_rpb), A(swa_sink), A(diff_lambda), A(diff_norm_g)
    w_out, mlp_w1, mlp_w2 = A(w_out), A(mlp_w1), A(mlp_w2)

    if _depth not in _CACHE:
        _CACHE[_depth] = build_program(_depth)[0]
    nc = _CACHE[_depth]

    cosT, sinT, perm = _rope_tables()
    shared = {
        "ada_w": A(ada_w[:_depth]),
        "ada_bT": A(ada_b.reshape(DEPTH, 96, 128).transpose(2, 0, 1)),
        "normgT": A(norm_g.reshape(DEPTH, 4, 16, 128).transpose(3, 0, 1, 2)),
        "w_in": A(w_in[:_depth]), "w_out": A(w_out[:_depth]), "w1": A(mlp_w1[:_depth]), "w2": A(mlp_w2[:_depth]),
        "convwT": A(conv_w.reshape(DEPTH, 3, 4, 128).transpose(3, 0, 2, 1)),
        "nab": A(_na_bias_tables(na_rpb)[:_depth]),
        "sinkB": A(np.broadcast_to(swa_sink[None], (128, DEPTH, 8))),
        "lamP": A(np.broadcast_to(diff_lambda[None], (128, DEPTH, 4, 64))),
        "dgB": A(np.broadcast_to(diff_norm_g[None], (128, DEPTH, 128))),
        "cosT": cosT, "sinT": sinT, "perm": perm, "ident": np.eye(128, dtype=f32),
        "swamask": _swa_masks(),
    }
    in_maps = []
    for core in range(8):
        b = core // 4
        s0, s1 = 2 * core, 2 * core + 1
        m = dict(shared)
        m["xT"] = A(np.concatenate([x_sample[b].T, x_prompt[s0].T, x_prompt[s1].T], axis=1))
        cv = np.stack([c_ctx, c[b]], axis=-1)
        m["cvec"] = A(cv.reshape(16, 128, 2).transpose(1, 0, 2))
        na = cache_na_kv[b]
        m["cnaKT"] = A(na[:, 0].reshape(DEPTH, 256, 4, 128).transpose(0, 2, 3, 1))
        m["cnaV"] = A(na[:, 1].reshape(DEPTH, 256, 512))
        sw = cache_swa_kv[b]
        m["cswaKT"] = A(sw[:, 0].reshape(DEPTH, 256, 128).transpose(0, 2, 1))
        m["cswaV"] = A(sw[:, 1].reshape(DEPTH, 256, 128))
        df = cache_diff_kv[b]
        m["cdiffKT"] = A(df[:, 0].transpose(0, 2, 3, 1))
        m["cdiffV"] = A(df[:, 1].reshape(DEPTH, 256, 512))
        in_maps.append(m)

    res = run_bass_kernel_spmd(nc, in_maps, core_ids=list(range(8)))
    R = res.results
    yp = np.empty((16, 256, D), f32)
    ys = np.empty((2, 2048, D), f32)
    nna = np.empty((16, DEPTH, 2, 256, 8, 64), f32)
    nsw = np.empty((16, DEPTH, 2, 256, 2, 64), f32)
    ndf = np.empty((16, DEPTH, 2, 256, 4, 128), f32)
    for core in range(8):
        yt = np.asarray(R[core]["yT"])
        if core % 4 == 0:
            ys[core // 4] = yt[:, 0:2048].T
        for s in range(2):
            yp[2 * core + s] = yt[:, 2048 + 256 * s:2048 + 256 * (s + 1)].T
            nna[2 * core + s] = np.asarray(R[core]["ona"])[s].reshape(DEPTH, 2, 256, 8, 64)
            nsw[2 * core + s] = np.asarray(R[core]["oswa"])[s].reshape(DEPTH, 2, 256, 2, 64)
            ndf[2 * core + s] = np.asarray(R[core]["odiff"])[s].reshape(DEPTH, 2, 256, 4, 128)
    return (yp, ys, nna, nsw, ndf)
```

```python
import math
import numpy as np
import concourse.bass as bass
import concourse.mybir as mybir
from concourse.bass_utils import run_bass_kernel_spmd

F32 = mybir.dt.float32
BF16 = mybir.dt.bfloat16
AF = mybir.ActivationFunctionType
ALU = mybir.AluOpType
AX = mybir.AxisListType

D = 2048
DEPTH = 4
NTOK = 2560
TS = 512
NT = 5
HID = 8192
INC = 5376
SCALE = 64 ** -0.5
EPS = 1e-6
NEG = -30000.0


class Chan:
    def __init__(self, nc, name, inc=16):
        self.sem = nc.alloc_semaphore(name)
        self.count = 0
        self.inc = inc
        self.last_op = None


class Op:
    __slots__ = ("eng", "fn", "deps", "signal", "chan", "val", "is_dma", "waits")


class _Rec:
    def __getattr__(self, name):
        def f(*a, **k):
            self.__dict__["call"] = (name, a, k)
            return self
        return f


class Sched:
    ENGS = ("pe", "act", "dve", "pool", "sp")

    def __init__(self, nc):
        self.nc = nc
        self.ops = []
        self.last_w = {}
        self.readers = {}
        self.esem = {e: nc.alloc_semaphore("prog_" + e) for e in self.ENGS}
        self.nchan = 0
        self.fence_idx = None

    def chan(self, name=None, inc=16):
        self.nchan += 1
        return Chan(self.nc, name or ("ch%d" % self.nchan), inc)

    def op(self, eng, fn, reads=(), writes=(), chan=None):
        o = Op()
        o.eng = eng
        rec = _Rec()
        fn(rec)
        o.fn = rec.__dict__["call"]
        o.chan = chan
        o.is_dma = chan is not None
        o.signal = o.is_dma
        o.val = None
        deps = set()
        for r in reads:
            w = self.last_w.get(r)
            if w is not None:
                deps.add(w)
        for r in writes:
            w = self.last_w.get(r)
            if w is not None:
                deps.add(w)
            for rd in self.readers.get(r, {}).values():
                if isinstance(rd, list):
                    deps.update(rd)
                else:
                    deps.add(rd)
        i = len(self.ops)
        if self.fence_idx is not None:
            deps.add(self.fence_idx)
        if chan is not None:
            if callable(chan):
                chan = chan()
                o.chan = chan
            if chan.last_op is not None:
                deps.add(chan.last_op)
            chan.last_op = i
        o.deps = deps
        self.ops.append(o)
        for r in reads:
            d = self.readers.setdefault(r, {})
            if o.is_dma:
                d.setdefault("dma", []).append(i)
            else:
                d[eng] = i
        for r in writes:
            self.last_w[r] = i
            self.readers[r] = {}
        if o.is_dma:
            chan.count += chan.inc
            o.val = chan.count
        return i

    def fence(self, fn):
        names = list(dict.fromkeys(list(self.last_w.keys()) + list(self.readers.keys())))
        self.fence_idx = self.op("dve", fn, writes=names)

    def finalize(self, final_waits=()):
        ops = self.ops
        for o in ops:
            for d in o.deps:
                p = ops[d]
                if p.is_dma:
                    continue
                if p.eng == "pe" and o.eng == "pe":
                    continue
                p.signal = True
        cnt = {e: 0 for e in self.ENGS}
        for o in ops:
            if o.signal and not o.is_dma:
                cnt[o.eng] += 1
                o.val = cnt[o.eng]
        seen = {e: {} for e in self.ENGS}
        chan_count_at = {}
        for i, o in enumerate(ops):
            waits = {}
            for d in o.deps:
                p = ops[d]
                if p.is_dma:
                    sem = p.chan.sem
                    v = chan_count_at[id(p.chan)]
                    key = ("c", id(p.chan))
                else:
                    if p.eng == "pe" and o.eng == "pe":
                        continue
                    sem = self.esem[p.eng]
                    v = p.val
                    key = ("e", p.eng)
                if seen[o.eng].get(key, 0) >= v:
                    continue
                if key not in waits or waits[key][1] < v:
                    waits[key] = (sem, v)
            for key, (sem, v) in waits.items():
                seen[o.eng][key] = v
            o.waits = list(waits.values())
            if o.is_dma:
                chan_count_at[id(o.chan)] = o.val
        self.final = [(c.sem, c.count) for c in final_waits]
        return cnt

    def emit(self):
        nc = self.nc
        per = {e: [o for o in self.ops if o.eng == e] for e in self.ENGS}
        esem = self.esem
        final = self.final

        def run(eng_name, eng):
            for o in per[eng_name]:
                for sem, v in o.waits:
                    eng.wait_ge(sem, v)
                name, a, k = o.fn
                ins = getattr(eng, name)(*a, **k)
                if o.is_dma:
                    ins.then_inc(o.chan.sem, o.chan.inc)
                elif o.signal:
                    ins.then_inc(esem[eng_name], 1)

        with nc.Block() as block:
            @block.sync
            def _(e):
                run("sp", e)
                for sem, v in final:
                    e.wait_ge(sem, v)

            @block.scalar
            def _(e):
                run("act", e)

            @block.vector
            def _(e):
                run("dve", e)

            @block.gpsimd
            def _(e):
                run("pool", e)

            @block.tensor
            def _(e):
                run("pe", e)


def chunk_type(m):
    if m < 4: return "na_q"
    if m < 8: return "na_k"
    if m < 12: return "na_v"
    if m < 16: return "u"
    if m < 20: return "gb"
    if m < 24: return "gc"
    if m < 28: return "sq"
    if m == 28: return "sk"
    if m == 29: return "sv"
    if m < 34: return "dq"
    if m < 38: return "dk"
    return "dv"


ROPE_T = ("sq", "sk", "dq", "dk")
VCOL = {"na_v": 0, "sv": 512, "dv": 640}
VBASE = {"na_v": 8, "sv": 29, "dv": 38}
KBASE = {"na_k": 4, "sk": 28, "dk": 34}


def na_pat(i):
    return {0: 0, 1: 1, 14: 3, 15: 4}.get(i, 2)


def na_ks(i):
    return min(max(2 * i - 4, 0), 23)


def build_program(depth=DEPTH):
    nc = bass.Bass("TRN2", target_bir_lowering=False)

    def din(name, shape, dt=F32):
        return nc.dram_tensor(name, list(shape), dt, kind="ExternalInput").ap()

    def dout(name, shape, dt=F32):
        return nc.dram_tensor(name, list(shape), dt, kind="ExternalOutput").ap()

    def dscr(name, shape, dt):
        return nc.dram_tensor(name, list(shape), dt).ap()

    xT_in = din("xT", [D, NTOK])
    cvec = din("cvec", [128, 16, 2])
    ada_w = din("ada_w", [depth, D, 6 * D])
    ada_bT = din("ada_bT", [128, DEPTH, 96])
    normgT = din("normgT", [128, DEPTH, 4, 16])
    w_in = din("w_in", [depth, D, INC])
    w_out = din("w_out", [depth, D, D])
    w1 = din("w1", [depth, D, HID])
    w2 = din("w2", [depth, HID, D])
    convwT = din("convwT", [128, DEPTH, 4, 3])
    nab = din("nab", [depth, 5, 8, 128, 576])
    sinkB = din("sinkB", [128, DEPTH, 8])
    lamP = din("lamP", [128, DEPTH, 4, 64])
    dgB = din("dgB", [128, DEPTH, 128])
    cnaKT = din("cnaKT", [DEPTH, 4, 128, 256])
    cnaV = din("cnaV", [DEPTH, 256, 512])
    cswaKT = din("cswaKT", [DEPTH, 128, 256])
    cswaV = din("cswaV", [DEPTH, 256, 128])
    cdiffKT = din("cdiffKT", [DEPTH, 4, 128, 256])
    cdiffV = din("cdiffV", [DEPTH, 256, 512])
    cosT_in = din("cosT", [128, 2048])
    sinT_in = din("sinT", [128, 2048])
    perm_in = din("perm", [128, 128])
    ident_in = din("ident", [128, 128])
    swamask_in = din("swamask", [128, 3, 384])

    yT = dout("yT", [D, NTOK])
    ona = dout("ona", [2, DEPTH, 2, 256, 512])
    oswa = dout("oswa", [2, DEPTH, 2, 256, 128])
    odiff = dout("odiff", [2, DEPTH, 2, 256, 512])

    XS = dscr("XS", [D, NTOK], F32)
    PTs = dscr("PTs", [42 * 128, NTOK], BF16)
    VT = dscr("VT", [NTOK, 1152], BF16)
    AH = dscr("AH", [HID, NTOK], BF16)

    BASE = 16512
    LIMIT = BASE + 212000
    cur = [BASE]

    def sb(name, shape, dt, at=None):
        esz = 4 if dt == F32 else 2
        n = esz
        for s in shape[1:]:
            n *= s
        n = (n + 31) // 32 * 32
        if at is None:
            off = cur[0]
            cur[0] += n
            assert cur[0] <= LIMIT, (name, cur[0])
        else:
            off = at[0]
            at[0] += n
            assert at[0] <= at[1], (name, at[0], at[1])
        return nc.alloc_sbuf_tensor_at(name, list(shape), dt, offset=off)

    ident = sb("ident", [128, 128], BF16)
    perm = sb("perm", [128, 128], BF16)
    ones = sb("ones", [128, 128], BF16)
    cosT = sb("cosT", [128, 2048], BF16)
    sinT = sb("sinT", [128, 2048], BF16)
    modT = sb("modT", [128, DEPTH, 96, 2], F32)
    coef = sb("coef", [128, DEPTH, 6, 16, 2], F32)
    normg = sb("normg", [128, DEPTH, 4, 16], F32)
    adab = sb("adab", [128, DEPTH, 96], F32)
    cv32 = sb("cv32", [128, 16, 2], F32)
    cvb = sb("cvb", [128, 16, 2], BF16)
    convw = sb("convw", [128, DEPTH, 4, 3], F32)
    sinkT = sb("sinkT", [128, DEPTH, 8], F32)
    lamp = sb("lamp", [128, DEPTH, 4, 64], F32)
    lamv = sb("lamv", [128, DEPTH, 4], F32)
    dgs = sb("dgs", [128, DEPTH, 128], F32)
    swamask = sb("swamask", [128, 3, 384], F32)
    epsT = sb("epsT", [128, 1], F32)
    small = sb("small", [128, 64], F32)
    hall = sb("hall", [128, 16, NTOK], BF16)
    WB = [sb("wb%d" % i, [128, 16, 512], BF16) for i in range(2)]
    wbase = WB[0]
    WORK0 = cur[0]
    WORK_END = LIMIT
    ATT0 = WORK0 - 2 * 16 * 512 * 2

    ps = [nc.alloc_psum_tensor("ps%d" % i, [128, 512], F32) for i in range(4)]
    psT = [nc.alloc_psum_tensor("psT%d" % i, [128, 1024], BF16) for i in range(2)]
    psO = [nc.alloc_psum_tensor("psO%d" % i, [128, 512], F32) for i in range(2)]

    S = Sched(nc)
    rotc = {}

    def rot(name, n):
        v = rotc.get(name, 0)
        rotc[name] = v + 1
        return v % n

    ch_w = [S.chan("w0"), S.chan("w1")]
    LDP = {"sp": [S.chan("spld%d" % i) for i in range(12)], "pool": [S.chan("plld%d" % i) for i in range(6)]}
    STP = [S.chan("spst%d" % i) for i in range(12)]

    class _Pick:
        def __init__(self, lst, key):
            self.lst, self.key = lst, key

        def __call__(self):
            return self.lst[rot(self.key, len(self.lst))]

    ld_sp = _Pick(LDP["sp"], "ldsp")
    ld_pl = _Pick(LDP["pool"], "ldpl")
    st_sp = _Pick(STP, "stsp")
    ch_in = ld_sp
    ch_x = ld_sp
    ch_xo = st_sp
    ch_stg = [st_sp] * 4
    ch_out = st_sp
    ch_ld = [ld_sp] * 8

    def ld_const(dst, src, eng="sp", name=None):
        S.op(eng, lambda e: e.dma_start(out=dst, in_=src), writes=[name], chan=(ld_sp if eng == "sp" else ld_pl))

    ld_const(ident[:], ident_in, "pool", "ident")
    ld_const(perm[:], perm_in, "pool", "perm")
    ld_const(cosT[:], cosT_in, "pool", "cosT")
    ld_const(sinT[:], sinT_in, "pool", "sinT")
    ld_const(adab[:], ada_bT, "sp", "adab")
    ld_const(normg[:], normgT, "sp", "normg")
    ld_const(cv32[:], cvec, "sp", "cv32")
    ld_const(convw[:], convwT, "sp", "convw")
    ld_const(sinkT[:], sinkB, "sp", "sinkT")
    ld_const(lamp[:], lamP, "sp", "lamp")
    ld_const(dgs[:], dgB, "sp", "dgs")
    ld_const(swamask[:], swamask_in, "sp", "swamask")
    S.op("dve", lambda e: e.memset(ones[:], 1.0), writes=["ones"])
    S.op("dve", lambda e: e.memset(epsT[:], EPS), writes=["epsT"])
    S.op("act", lambda e: e.activation(out=cvb[:], in_=cv32[:], func=AF.Silu), reads=["cv32"], writes=["cvb"])

    import os as _os
    STOP = int(_os.environ.get("KSTOP", "99"))
    lam_inits = [0.8 - 0.6 * math.exp(-0.3 * l) for l in range(DEPTH)]
    lprod = sb("lprod", [128, 2, 64], F32)
    for l in range(depth if STOP >= -1 else 0):
        for j in range(2):
            S.op("dve", lambda e, l=l, j=j: e.tensor_tensor(out=lprod[:, j, :], in0=lamp[:, l, 2 * j, :],
                                                            in1=lamp[:, l, 2 * j + 1, :], op=ALU.mult),
                 reads=["lamp"], writes=[("lprod", j)])
            S.op("dve", lambda e, l=l, j=j: e.tensor_reduce(out=lamv[:, l, j:j + 1], in_=lprod[:, j, :], axis=AX.X, op=ALU.add),
                 reads=[("lprod", j)], writes=[("lamv", l, j)])
            S.op("act", lambda e, l=l, j=j: e.activation(out=lamv[:, l, j:j + 1], in_=lamv[:, l, j:j + 1], func=AF.Exp),
                 reads=[("lamv", l, j)], writes=[("lamv", l, j)])
        S.op("dve", lambda e, l=l: e.tensor_tensor(out=lamv[:, l, 2:3], in0=lamv[:, l, 0:1], in1=lamv[:, l, 1:2], op=ALU.subtract),
             reads=[("lamv", l, 0), ("lamv", l, 1)], writes=[("lamv", l, 2)])
        S.op("dve", lambda e, l=l: e.tensor_scalar_add(out=lamv[:, l, 2:3], in0=lamv[:, l, 2:3], scalar1=float(lam_inits[l])),
             reads=[("lamv", l, 2)], writes=[("lamv", l, 2)])
        S.op("dve", lambda e, l=l: e.tensor_scalar_mul(out=dgs[:, l, :], in0=dgs[:, l, :], scalar1=float(1.0 - lam_inits[l])),
             reads=["dgs"], writes=["dgs"])

    def load_w(src_ap, kc, ncols):
        i = rot("wb", 2)
        buf = WB[i]
        flat = kc * ncols
        dst = buf[:].rearrange("p a b -> p (a b)")[:, 0:flat].rearrange("p (a b) -> p a b", a=kc)
        S.op("pool", lambda e: e.dma_start(out=dst, in_=src_ap.rearrange("(a p) c -> p a c", p=128)),
             writes=[("wb", i)], chan=ch_w[i])
        return dst, ("wb", i)

    for l in range(depth if STOP >= 0 else 0):
        for wt in range(24):
            wv, wr = load_w(ada_w[l, :, wt * 512:(wt + 1) * 512], 16, 512)
            for j in range(4 if _os.environ.get("KSUB") != "nomm" else 0):
                chn = wt * 4 + j
                pi = rot("ps", 4)
                for kc in range(16):
                    S.op("pe", lambda e, wv=wv, j=j, kc=kc, pi=pi: e.matmul(ps[pi][:, 0:2], lhsT=wv[:, kc, j * 128:(j + 1) * 128],
                                                                         rhs=cvb[:, kc, :], start=(kc == 0), stop=(kc == 15)),
                         reads=[wr, "cvb"], writes=[("ps", pi)])
                S.op("dve", lambda e, l=l, chn=chn, pi=pi: e.tensor_scalar_add(out=modT[:, l, chn, :], in0=ps[pi][:, 0:2],
                                                                            scalar1=adab[:, l, chn:chn + 1]),
                     reads=[("ps", pi), "adab"], writes=[("modT", l)])
        for g in range(2 if _os.environ.get("KSUB") not in ("nomm", "nocoef") else 0):
            def mt(i, l=l, g=g):
                return modT[:, l, i * 16:(i + 1) * 16, g]
            S.op("dve", lambda e, l=l, g=g, mt=mt: e.scalar_tensor_tensor(out=coef[:, l, 0, :, g], in0=mt(1), scalar=1.0, in1=normg[:, l, 0, :],
                                                                          op0=ALU.add, op1=ALU.mult), reads=[("modT", l), "normg"], writes=[("coef", l)])
            S.op("dve", lambda e, l=l, g=g, mt=mt: e.tensor_copy(out=coef[:, l, 1, :, g], in_=mt(0)), reads=[("modT", l)], writes=[("coef", l)])
            S.op("dve", lambda e, l=l, g=g, mt=mt: e.tensor_tensor(out=coef[:, l, 2, :, g], in0=mt(2), in1=normg[:, l, 1, :], op=ALU.mult),
                 reads=[("modT", l), "normg"], writes=[("coef", l)])
            S.op("dve", lambda e, l=l, g=g, mt=mt: e.scalar_tensor_tensor(out=coef[:, l, 3, :, g], in0=mt(4), scalar=1.0, in1=normg[:, l, 2, :],
                                                                          op0=ALU.add, op1=ALU.mult), reads=[("modT", l), "normg"], writes=[("coef", l)])
            S.op("dve", lambda e, l=l, g=g, mt=mt: e.tensor_copy(out=coef[:, l, 4, :, g], in_=mt(3)), reads=[("modT", l)], writes=[("coef", l)])
            S.op("dve", lambda e, l=l, g=g, mt=mt: e.tensor_tensor(out=coef[:, l, 5, :, g], in0=mt(5), in1=normg[:, l, 3, :], op=ALU.mult),
                 reads=[("modT", l), "normg"], writes=[("coef", l)])

    def xview(ap2d, t):
        return ap2d[:, t * TS:(t + 1) * TS].rearrange("(c p) t -> p c t", p=128)

    def rstd_from_psum(pso_i, rs, scale):
        n = rs.shape[-1] if hasattr(rs, "shape") else None
        S.op("act", lambda e: e.activation(out=rs, in_=psO[pso_i][:, 0:rs.shape[1]], func=AF.Sqrt, bias=epsT[:], scale=scale),
             reads=[("psO", pso_i), "epsT"], writes=["rs"])
        S.op("dve", lambda e: e.reciprocal(out=rs, in_=rs), reads=["rs"], writes=["rs"])

    def premix_tile(l, t, xt, kind, xres):
        g = 1 if t < 4 else 0
        ks, kb = (0, 1) if kind == 0 else (3, 4)
        sq = wk["sq"]
        po = rot("psO", 2)
        for c in range(16):
            si = rot("sq", 2)
            S.op("act", lambda e, c=c, si=si: e.activation(out=sq[si][:], in_=xt[:, c, :], func=AF.Square),
                 reads=[xres], writes=[("sq", si)])
            S.op("pe", lambda e, c=c, si=si, po=po: e.matmul(psO[po][:], lhsT=ones[:], rhs=sq[si][:], start=(c == 0), stop=(c == 15)),
                 reads=[("sq", si), "ones"], writes=[("psO", po)])
        rs = wk["rs"]
        rstd_from_psum(po, rs[:], 1.0 / D)
        for c in range(16):
            ti = rot("tmp", 2)
            tmp = wk["tmp"][ti]
            S.op("dve", lambda e, c=c, tmp=tmp: e.scalar_tensor_tensor(out=tmp[:], in0=xt[:, c, :], scalar=coef[:, l, ks, c, g:g + 1], in1=rs[:],
                                                                      op0=ALU.mult, op1=ALU.mult),
                 reads=[xres, "rs", ("coef", l)], writes=[("tmp", ti)])
            S.op("act", lambda e, c=c, tmp=tmp: e.activation(out=hall[:, c, t * TS:(t + 1) * TS], in_=tmp[:], func=AF.Identity,
                                                             bias=coef[:, l, kb, c, g:g + 1], scale=1.0),
                 reads=[("tmp", ti), ("coef", l)], writes=[("hall", t)])

    wk = {}

    def alloc_work(names):
        at = [WORK0, WORK_END]
        wk.clear()
        for nm, shape, dt, n in names:
            if n == 1:
                wk[nm] = sb("wk_%s_%d" % (nm, rot("wkname", 10 ** 9)), shape, dt, at)
            else:
                wk[nm] = [sb("wk_%s_%d" % (nm, rot("wkname", 10 ** 9)), shape, dt, at) for _ in range(n)]

    def fence(tag):
        S.fence(lambda e: e.memset(small[:, 63:64], 0.0))

    import os as _os
    STOP = int(_os.environ.get("KSTOP", "99"))
    for l in range(depth):
        if STOP < 1:
            break
        src_x = xT_in if l == 0 else XS
        fence("s1a")
        alloc_work([("x", [128, 16, TS], F32, 1), ("sq", [128, TS], BF16, 2), ("rs", [128, TS], F32, 1),
                    ("tmp", [128, TS], F32, 2), ("stg", [128, TS], BF16, 4), ("xb", [128, TS], BF16, 2),
                    ("t1", [128, TS], F32, 2), ("fst", [128, 256], F32, 2), ("vst", [128, 256], BF16, 2)])
        for t in range(NT):
            xt = wk["x"]
            S.op("sp", lambda e, t=t, xt=xt: e.dma_start(out=xt[:], in_=xview(src_x, t)), reads=[("XS", t)], writes=["xt"], chan=ch_x)
            premix_tile(l, t, xt, 0, "xt")
        if STOP < 2:
            break
        for wt in range(21):
            wv, wr = load_w(w_in[l, :, wt * 256:(wt + 1) * 256], 16, 256)
            for t in range(NT):
                for j in range(2):
                    m = wt * 2 + j
                    typ = chunk_type(m)
                    if typ in VBASE:
                        continue
                    pi = rot("ps", 4)
                    for kc in range(16):
                        S.op("pe", lambda e, j=j, kc=kc, pi=pi, t=t, wv=wv: e.matmul(ps[pi][:], lhsT=wv[:, kc, j * 128:(j + 1) * 128],
                                                                                   rhs=hall[:, kc, t * TS:(t + 1) * TS],
                                                                                   start=(kc == 0), stop=(kc == 15)),
                             reads=[wr, ("hall", t)], writes=[("ps", pi)])
                    si = rot("stg", 4)
                    stg = wk["stg"][si]
                    if typ in ROPE_T and t < 4 and _os.environ.get("KSUB") != "norope":
                        xi = rot("xb", 2)
                        xb = wk["xb"][xi]
                        t1 = wk["t1"][xi]
                        S.op("act", lambda e, pi=pi, xb=xb: e.copy(out=xb[:], in_=ps[pi][:]), reads=[("ps", pi)], writes=[("xb", xi)])
                        p2 = rot("ps", 4)
                        S.op("pe", lambda e, p2=p2, xb=xb: e.matmul(ps[p2][:], lhsT=perm[:], rhs=xb[:], start=True, stop=True),
                             reads=[("xb", xi), "perm"], writes=[("ps", p2)])
                        S.op("dve", lambda e, xb=xb, t1=t1, t=t: e.tensor_tensor(out=t1[:], in0=xb[:], in1=cosT[:, t * TS:(t + 1) * TS], op=ALU.mult),
                             reads=[("xb", xi), "cosT"], writes=[("t1", xi)])
                        S.op("dve", lambda e, p2=p2, t=t, xb=xb: e.tensor_tensor(out=xb[:], in0=ps[p2][:], in1=sinT[:, t * TS:(t + 1) * TS], op=ALU.mult),
                             reads=[("ps", p2), "sinT"], writes=[("xb", xi)])
                        S.op("dve", lambda e, xb=xb, t1=t1, stg=stg: e.tensor_tensor(out=stg[:], in0=t1[:], in1=xb[:], op=ALU.add),
                             reads=[("xb", xi), ("t1", xi)], writes=[("stg", si)])
                    else:
                        S.op("act", lambda e, pi=pi, stg=stg: e.copy(out=stg[:], in_=ps[pi][:]), reads=[("ps", pi)], writes=[("stg", si)])
                    S.op("sp", lambda e, m=m, t=t, stg=stg: e.dma_start(out=PTs[m * 128:(m + 1) * 128, t * TS:(t + 1) * TS], in_=stg[:]),
                         reads=[("stg", si)], writes=[("PTs", m, t)], chan=ch_stg[si])
                passes = []
                for j in range(2):
                    m = wt * 2 + j
                    typ = chunk_type(m)
                    if typ in VBASE or (typ in KBASE and t == 4):
                        passes.append((j, m, typ))
                if not passes or _os.environ.get("KSUB") == "notok":
                    continue
                j0 = passes[0][0]
                ncol = 128 * len(passes)
                for tb in range(4):
                    pi = rot("ps", 4)
                    tok0 = t * TS + tb * 128
                    for kc in range(16):
                        S.op("pe", lambda e, kc=kc, pi=pi, tok0=tok0, wv=wv, j0=j0, ncol=ncol: e.matmul(
                            ps[pi][:, 0:ncol], lhsT=hall[:, kc, tok0:tok0 + 128], rhs=wv[:, kc, j0 * 128:j0 * 128 + ncol],
                            start=(kc == 0), stop=(kc == 15)), reads=[wr, ("hall", t)], writes=[("ps", pi)])
                    for pj, (j, m, typ) in enumerate(passes):
                        c0 = pj * 128
                        fi = rot("fst", 2)
                        fst = wk["fst"][fi]
                        S.op("act", lambda e, pi=pi, c0=c0, fst=fst: e.copy(out=fst[:, 0:128], in_=ps[pi][:, c0:c0 + 128]),
                             reads=[("ps", pi)], writes=[("fst", fi)])
                        if typ in VBASE:
                            vi = rot("vst", 2)
                            vst = wk["vst"][vi]
                            vc = VCOL[typ] + (m - VBASE[typ]) * 128
                            S.op("dve", lambda e, fst=fst, vst=vst: e.tensor_copy(out=vst[:, 0:128], in_=fst[:, 0:128]),
                                 reads=[("fst", fi)], writes=[("vst", vi)])
                            S.op("sp", lambda e, vst=vst, tok0=tok0, vc=vc: e.dma_start(out=VT[tok0:tok0 + 128, vc:vc + 128], in_=vst[:, 0:128]),
                                 reads=[("vst", vi)], writes=[("VT", m, tok0)], chan=st_sp)
                        if t == 4:
                            sq_i = tb // 2
                            r0 = (tb % 2) * 128
                            if typ in ("na_k", "na_v"):
                                base = KBASE["na_k"] if typ == "na_k" else VBASE["na_v"]
                                dst = ona[sq_i, l, 0 if typ == "na_k" else 1, r0:r0 + 128, (m - base) * 128:(m - base + 1) * 128]
                            elif typ in ("sk", "sv"):
                                dst = oswa[sq_i, l, 0 if typ == "sk" else 1, r0:r0 + 128, :]
                            else:
                                base = KBASE["dk"] if typ == "dk" else VBASE["dv"]
                                dst = odiff[sq_i, l, 0 if typ == "dk" else 1, r0:r0 + 128, (m - base) * 128:(m - base + 1) * 128]
                            S.op("sp", lambda e, fst=fst, dst=dst: e.dma_start(out=dst, in_=fst[:, 0:128]),
                                 reads=[("fst", fi)], chan=ch_out)

        if STOP < 3:
            break
        fence("s2")
        at = [ATT0, WORK_END]
        A = {}

        def asb(nm, shape, dt, n=1):
            if n == 1:
                A[nm] = sb("at_%s_%d" % (nm, rot("wkname", 10 ** 9)), shape, dt, at)
            else:
                A[nm] = [sb("at_%s_%d" % (nm, rot("wkname", 10 ** 9)), shape, dt, at) for _ in range(n)]

        asb("Sb", [128, 2312], F32, 2)
        asb("P", [128, 2304], BF16, 2)
        asb("Pt", [128, 8, 128], BF16, 2)
        asb("KT", [128, 2304], BF16)
        asb("VD", [128, 18, 128], BF16)
        asb("QT", [128, 4, 128], BF16, 2)
        asb("KW", [128, 4, 576], BF16)
        asb("VW", [128, 5, 512], BF16)
        asb("bias", [128, 8, 576], BF16)
        asb("cK", [128, 4, 256], BF16)
        asb("cV", [128, 2, 512], BF16)
        asb("otok", [128, 128], BF16, 2)
        asb("junk", [128, 128], F32)
        asb("cz", [128, 3, 516], BF16)
        asb("cacc", [128, 512], F32, 2)
        mixT = hall

        def scores(qT, kT, nk, sbi, off, bias=None, kres=()):
            Sb = A["Sb"][sbi]
            for c0 in range(0, nk, 512):
                n = min(512, nk - c0)
                pi = rot("ps", 4)
                S.op("pe", lambda e, pi=pi, n=n, c0=c0: e.matmul(ps[pi][:, 0:n], lhsT=qT, rhs=kT[:, c0:c0 + n], start=True, stop=True),
                     reads=["QT"] + list(kres), writes=[("ps", pi)])
                if bias is None:
                    S.op("act", lambda e, pi=pi, n=n, c0=c0: e.activation(out=Sb[:, off + c0:off + c0 + n], in_=ps[pi][:, 0:n], func=AF.Identity, scale=SCALE),
                         reads=[("ps", pi)], writes=[("Sb", sbi, off + c0)])
                else:
                    S.op("dve", lambda e, pi=pi, n=n, c0=c0: e.scalar_tensor_tensor(out=Sb[:, off + c0:off + c0 + n], in0=ps[pi][:, 0:n], scalar=SCALE,
                                                                                   in1=bias[:, c0:c0 + n], op0=ALU.mult, op1=ALU.add),
                         reads=[("ps", pi), "bias"], writes=[("Sb", sbi, off + c0)])
            return [("Sb", sbi, off + c0) for c0 in range(0, nk, 512)]

        def softmax(sbi, W, sres, k):
            Sb = A["Sb"][sbi]
            P = A["P"][sbi]
            mx = small[:, 4 * k:4 * k + 1]
            nmx = small[:, 4 * k + 1:4 * k + 2]
            rs = small[:, 4 * k + 2:4 * k + 3]
            sm = ("small", k)
            S.op("dve", lambda e: e.tensor_reduce(out=mx, in_=Sb[:, 0:W], axis=AX.X, op=ALU.max), reads=sres, writes=[sm])
            S.op("dve", lambda e: e.tensor_scalar_mul(out=nmx, in0=mx, scalar1=-1.0), reads=[sm], writes=[sm])
            S.op("dve", lambda e: e.memset(rs, 0.0), writes=[sm])
            Wp = min(W, 2304)
            S.op("act", lambda e: e.activation(out=P[:, 0:Wp], in_=Sb[:, 0:Wp], func=AF.Exp, bias=nmx, scale=1.0, accum_out=rs),
                 reads=sres + [sm], writes=[("P", sbi), sm])
            if W > Wp:
                S.op("act", lambda e: e.activation(out=Sb[:, Wp:W], in_=Sb[:, Wp:W], func=AF.Exp, bias=nmx, scale=1.0, accum_out=rs),
                     reads=sres + [sm], writes=[sm] + sres)
            S.op("dve", lambda e: e.reciprocal(out=rs, in_=rs), reads=[sm], writes=[sm])
            return rs

        def pv(sbi, blocks, dv, vres):
            P = A["P"][sbi]
            po = rot("psO", 2)
            nb = len(blocks)
            for g0 in range(0, nb, 8):
                grp = blocks[g0:g0 + 8]
                ti = rot("psT", 2)
                pti = rot("Pt", 2)
                Pt = A["Pt"][pti]
                for i, (off, nk, vap) in enumerate(grp):
                    S.op("pe", lambda e, i=i, off=off, nk=nk, ti=ti: e.transpose(out=psT[ti][0:nk, i * 128:(i + 1) * 128], in_=P[:, off:off + nk], identity=ident[:]),
                         reads=[("P", sbi), "ident"], writes=[("psT", ti)])
                ng = len(grp)
                full = all(nk == 128 for (_, nk, _) in grp)
                eng = "act" if rot("pte", 2) == 0 else "dve"
                if full:
                    if eng == "act":
                        S.op("act", lambda e, ti=ti, ng=ng, Pt=Pt: e.copy(out=Pt[:, 0:ng, :].rearrange("p a b -> p (a b)"), in_=psT[ti][:, 0:ng * 128]),
                             reads=[("psT", ti)], writes=[("Pt", pti)])
                    else:
                        S.op("dve", lambda e, ti=ti, ng=ng, Pt=Pt: e.tensor_copy(out=Pt[:, 0:ng, :].rearrange("p a b -> p (a b)"), in_=psT[ti][:, 0:ng * 128]),
                             reads=[("psT", ti)], writes=[("Pt", pti)])
                else:
                    for i, (off, nk, vap) in enumerate(grp):
                        S.op("dve", lambda e, i=i, nk=nk, ti=ti, Pt=Pt: e.tensor_copy(out=Pt[0:nk, i, :], in_=psT[ti][0:nk, i * 128:(i + 1) * 128]),
                             reads=[("psT", ti)], writes=[("Pt", pti)])
                for i, (off, nk, vap) in enumerate(grp):
                    S.op("pe", lambda e, i=i, nk=nk, vap=vap, po=po, first=(g0 + i == 0), last=(g0 + i == nb - 1), Pt=Pt: e.matmul(
                        psO[po][:, 0:dv], lhsT=Pt[0:nk, i, :], rhs=vap, start=first, stop=last),
                        reads=[("Pt", pti)] + list(vres), writes=[("psO", po)])
            return po

        def put_mix(oi, chunk, tok0):
            ot = A["otok"][oi]
            ti = rot("psT", 2)
            S.op("pe", lambda e: e.transpose(out=psT[ti][:, 0:128], in_=ot[:], identity=ident[:]), reads=[("otok", oi), "ident"], writes=[("psT", ti)])
            S.op("act", lambda e: e.copy(out=mixT[:, chunk, tok0:tok0 + 128], in_=psT[ti][:, 0:128]), reads=[("psT", ti)],
                 writes=[("hall", tok0 // TS)])

        pend = []

        def flush_heads():
            while pend:
                pend.pop(0)()

        def plain_head(qT, segs, sink, hloc, oi, vres, kres, after=None):
            k = rot("plainbuf", 2)
            off = 0
            sres = []
            blocks = []
            for (kT, nk, bias, vbl) in segs:
                sres += scores(qT, kT, nk, k, off, bias, kres)
                o2 = off
                for (nkb, vap) in vbl:
                    blocks.append((o2, nkb, vap))
                    o2 += nkb
                off += nk
            W = off
            if sink is not None:
                Sb = A["Sb"][k]
                S.op("dve", lambda e, W=W: e.tensor_copy(out=Sb[:, W:W + 1], in_=sink), reads=["sinkT"], writes=[("Sb", k, "sink")])
                sres.append(("Sb", k, "sink"))
                W += 1
            Pw = off
            rs = softmax_w(k, W, Pw, sres, k)

            def phase_b():
                po = pv(k, blocks, 64, vres)
                ot = A["otok"][oi]
                S.op("act", lambda e, po=po: e.activation(out=ot[:, hloc * 64:(hloc + 1) * 64], in_=psO[po][:, 0:64], func=AF.Identity, scale=rs),
                     reads=[("psO", po), ("small", k)], writes=[("otok", oi)])
                if after is not None:
                    after()
            flush_heads()
            pend.append(phase_b)

        def softmax_w(sbi, W, Pw, sres, k):
            Sb = A["Sb"][sbi]
            P = A["P"][sbi]
            mx = small[:, 4 * k:4 * k + 1]
            nmx = small[:, 4 * k + 1:4 * k + 2]
            rs = small[:, 4 * k + 2:4 * k + 3]
            r2 = small[:, 4 * k + 3:4 * k + 4]
            sm = ("small", k)
            S.op("dve", lambda e: e.tensor_reduce(out=mx, in_=Sb[:, 0:W], axis=AX.X, op=ALU.max), reads=sres, writes=[sm])
            S.op("dve", lambda e: e.tensor_scalar_mul(out=nmx, in0=mx, scalar1=-1.0), reads=[sm], writes=[sm])
            S.op("dve", lambda e: e.memset(rs, 0.0), writes=[sm])
            S.op("act", lambda e: e.activation(out=P[:, 0:Pw], in_=Sb[:, 0:Pw], func=AF.Exp, bias=nmx, scale=1.0, accum_out=rs),
                 reads=sres + [sm], writes=[("P", sbi), sm])
            if W > Pw:
                S.op("act", lambda e: e.activation(out=r2, in_=Sb[:, Pw:W], func=AF.Exp, bias=nmx, scale=1.0), reads=sres + [sm], writes=[sm])
                S.op("dve", lambda e: e.tensor_tensor(out=rs, in0=rs, in1=r2, op=ALU.add), reads=[sm], writes=[sm])
            S.op("dve", lambda e: e.reciprocal(out=rs, in_=rs), reads=[sm], writes=[sm])
            return rs

        def diff_head(qT, segs, oi, vres, kres):
            rss = []
            W = sum(s[1] for s in segs)
            for side in range(2):
                off = 0
                sres = []
                for (kT, nk, vbl) in segs:
                    sres += scores(qT[side * 64:(side + 1) * 64, :], kT[side * 64:(side + 1) * 64, :], nk, side, off, None, kres)
                    off += nk
                rss.append(softmax_w(side, W, W, sres, side))
            blocks = []
            off = 0
            for (kT, nk, vbl) in segs:
                o2 = off
                for (nkb, vap) in vbl:
                    blocks.append((o2, nkb, vap))
                    o2 += nkb
                off += nk
            P1, P2 = A["P"][0], A["P"][1]
            lr2 = small[:, 12:13]
            S.op("dve", lambda e: e.tensor_tensor(out=lr2, in0=rss[1], in1=lamv[:, l, 2:3], op=ALU.mult),
                 reads=[("small", 1), ("lamv", l, 2)], writes=[("small", 3)])
            S.op("dve", lambda e: e.tensor_scalar_mul(out=P2[:, 0:W], in0=P2[:, 0:W], scalar1=lr2),
                 reads=[("P", 1), ("small", 3)], writes=[("P", 1)])
            S.op("dve", lambda e: e.scalar_tensor_tensor(out=P1[:, 0:W], in0=P1[:, 0:W], scalar=rss[0], in1=P2[:, 0:W], op0=ALU.mult, op1=ALU.subtract),
                 reads=[("P", 0), ("P", 1), ("small", 0)], writes=[("P", 0)])
            po = pv(0, blocks, 128, vres)
            ss = small[:, 13:14]
            S.op("dve", lambda e: e.memset(ss, 0.0), writes=[("small", 4)])
            S.op("act", lambda e, po=po: e.activation(out=A["junk"][:], in_=psO[po][:, 0:128], func=AF.Square, accum_out=ss),
                 reads=[("psO", po), ("small", 4)], writes=[("small", 4), "junk"])
            S.op("act", lambda e: e.activation(out=ss, in_=ss, func=AF.Sqrt, bias=epsT[:], scale=1.0 / 128), reads=[("small", 4), "epsT"], writes=[("small", 4)])
            S.op("dve", lambda e: e.reciprocal(out=ss, in_=ss), reads=[("small", 4)], writes=[("small", 4)])
            ot = A["otok"][oi]
            S.op("dve", lambda e, po=po: e.scalar_tensor_tensor(out=ot[:], in0=psO[po][:, 0:128], scalar=ss, in1=dgs[:, l, :], op0=ALU.mult, op1=ALU.mult),
                 reads=[("psO", po), ("small", 4), "dgs"], writes=[("otok", oi)])

        def ld(dst, src, names_w, reads=(), eng="sp", ci=0):
            S.op(eng, lambda e: e.dma_start(out=dst, in_=src), reads=list(reads), writes=list(names_w), chan=(ld_sp if eng == "sp" else ld_pl))

        def pts_reads(chunks, t0, t1):
            return [("PTs", m, t) for m in chunks for t in range(t0 // TS, (t1 - 1) // TS + 1)]

        def vt_reads(typ, t0, t1):
            n = 1 if typ == "sv" else 4
            return [("VT", VBASE[typ] + k, tk) for k in range(n) for tk in range(t0 // 128 * 128, t1, 128)]

        def conv_seg(tok0, n, left, right):
            cz = A["cz"]
            for cc in range(4):
                lo = tok0 - (1 if left else 0)
                hi = tok0 + n + (1 if right else 0)
                o = 0 if left else 1
                for k, base in enumerate((12, 20, 16)):
                    m = base + cc
                    ld(cz[:, k, o:o + hi - lo], PTs[m * 128:(m + 1) * 128, lo:hi], [("cz", k)], pts_reads([m], lo, hi), "sp", 1)
                if not left:
                    S.op("pool", lambda e: e.memset(cz[:, 0:2, 0:1], 0.0), writes=[("cz", 0), ("cz", 1)])
                if not right:
                    S.op("pool", lambda e, n=n: e.memset(cz[:, 0:2, n + 1:n + 2], 0.0), writes=[("cz", 0), ("cz", 1)])
                S.op("pool", lambda e, n=n: e.tensor_tensor(out=cz[:, 0, 0:n + 2], in0=cz[:, 0, 0:n + 2], in1=cz[:, 1, 0:n + 2], op=ALU.mult),
                     reads=[("cz", 0), ("cz", 1)], writes=[("cz", 0)])
                ai = rot("cacc", 2)
                acc = A["cacc"][ai]
                S.op("pool", lambda e, n=n, cc=cc, acc=acc: e.tensor_scalar_mul(out=acc[:, 0:n], in0=cz[:, 0, 0:n], scalar1=convw[:, l, cc, 0:1]),
                     reads=[("cz", 0), "convw"], writes=[("cacc", ai)])
                for k in (1, 2):
                    S.op("dve", lambda e, n=n, cc=cc, acc=acc, k=k: e.scalar_tensor_tensor(out=acc[:, 0:n], in0=cz[:, 0, k:k + n], scalar=convw[:, l, cc, k:k + 1],
                                                                                         in1=acc[:, 0:n], op0=ALU.mult, op1=ALU.add),
                         reads=[("cz", 0), "convw", ("cacc", ai)], writes=[("cacc", ai)])
                S.op("pool", lambda e, n=n, cc=cc, acc=acc: e.tensor_tensor(out=mixT[:, 4 + cc, tok0:tok0 + n], in0=acc[:, 0:n], in1=cz[:, 2, 1:n + 1], op=ALU.mult),
                     reads=[("cacc", ai), ("cz", 2)], writes=[("hall", tok0 // TS)])

        for t in range(4):
            conv_seg(t * TS, TS, t > 0, t < 3)
        conv_seg(2048, 256, False, False)
        conv_seg(2304, 256, False, False)

        ld(A["cK"][:], cnaKT[l].rearrange("c p k -> p c k"), ["cK"], (), "pool", 2)
        ld(A["cV"][:], cnaV[l].rearrange("(b p) c -> p b c", p=128), ["cV"], (), "pool", 2)
        for i in range(16):
            tok0 = i * 128
            k0 = na_ks(i) * 64
            qi = rot("QT", 2)
            QT = A["QT"][qi]
            ld(QT[:], PTs[0:512, tok0:tok0 + 128].rearrange("(c p) t -> p c t", p=128), ["QT"], pts_reads(range(0, 4), tok0, tok0 + 128), "sp", 0)
            ld(A["KW"][:], PTs[512:1024, k0:k0 + 576].rearrange("(c p) t -> p c t", p=128), ["KW"], pts_reads(range(4, 8), k0, k0 + 576), "sp", 0)
            ld(A["VW"][:, 0:4, :], VT[k0:k0 + 512, 0:512].rearrange("(b p) c -> p b c", p=128), ["VW"], vt_reads("na_v", k0, k0 + 576), "sp", 0)
            ld(A["VW"][0:64, 4, :], VT[k0 + 512:k0 + 576, 0:512], ["VW"], (), "sp", 0)
            ld(A["bias"][:], nab[l, na_pat(i)].rearrange("h p k -> p h k"), ["bias"], (), "pool", 2)
            for h in range(8):
                chn, r0 = h // 2, (h % 2) * 64
                segs = [(A["KW"][r0:r0 + 64, chn, :], 576, A["bias"][:, h, :],
                         [(128, A["VW"][:, b, h * 64:(h + 1) * 64]) for b in range(4)] + [(64, A["VW"][0:64, 4, h * 64:(h + 1) * 64])]),
                        (A["cK"][r0:r0 + 64, chn, :], 256, None, [(128, A["cV"][:, b, h * 64:(h + 1) * 64]) for b in range(2)])]
                oi = (h // 2) % 2
                plain_head(QT[r0:r0 + 64, chn, :], segs, None, h % 2, oi, ["VW", "cV"], ["KW", "cK"],
                           after=((lambda oi=oi, chn=chn, tok0=tok0: put_mix(oi, chn, tok0)) if h % 2 == 1 else None))
            flush_heads()
        for s in range(2):
            p0 = 2048 + 256 * s
            ld(A["KW"][:, :, 0:256], PTs[512:1024, p0:p0 + 256].rearrange("(c p) t -> p c t", p=128), ["KW"], pts_reads(range(4, 8), p0, p0 + 256), "sp", 0)
            ld(A["VW"][:, 0:2, :], VT[p0:p0 + 256, 0:512].rearrange("(b p) c -> p b c", p=128), ["VW"], vt_reads("na_v", p0, p0 + 256), "sp", 0)
            for qt in range(2):
                tok0 = p0 + qt * 128
                qi = rot("QT", 2)
                QT = A["QT"][qi]
                ld(QT[:], PTs[0:512, tok0:tok0 + 128].rearrange("(c p) t -> p c t", p=128), ["QT"], pts_reads(range(0, 4), tok0, tok0 + 128), "sp", 0)
                for h in range(8):
                    chn, r0 = h // 2, (h % 2) * 64
                    segs = [(A["KW"][r0:r0 + 64, chn, 0:256], 256, None, [(128, A["VW"][:, b, h * 64:(h + 1) * 64]) for b in range(2)])]
                    oi = (h // 2) % 2
                    plain_head(QT[r0:r0 + 64, chn, :], segs, None, h % 2, oi, ["VW"], ["KW"],
                               after=((lambda oi=oi, chn=chn, tok0=tok0: put_mix(oi, chn, tok0)) if h % 2 == 1 else None))
                flush_heads()

        KS = A["KW"]

        def load_swa_k(g, dstcols, src_rows_ap, reads):
            for half in range(2):
                ld(KS[half * 64:(half + 1) * 64, g, dstcols[0]:dstcols[1]], src_rows_ap, ["KW"], reads, "sp", 0)

        for g in range(2):
            for half in range(2):
                ld(A["cK"][half * 64:(half + 1) * 64, g, :], cswaKT[l, g * 64:(g + 1) * 64, :], ["cK"], (), "pool", 2)
        ld(A["cV"][:, :, 0:128], cswaV[l].rearrange("(b p) c -> p b c", p=128), ["cV"], (), "pool", 2)
        for jq in range(16):
            tok0 = jq * 128
            ws = min(max(128 * (jq - 1), 0), 2048 - 384)
            pat = 0 if jq == 0 else (2 if jq == 15 else 1)
            qi = rot("QT", 2)
            QT = A["QT"][qi]
            ld(QT[:], PTs[24 * 128:28 * 128, tok0:tok0 + 128].rearrange("(c p) t -> p c t", p=128), ["QT"], pts_reads(range(24, 28), tok0, tok0 + 128), "sp", 0)
            for g in range(2):
                load_swa_k(g, (0, 384), PTs[28 * 128 + g * 64:28 * 128 + (g + 1) * 64, ws:ws + 384], pts_reads([28], ws, ws + 384))
            ld(A["VW"][:, 0:3, 0:128], VT[ws:ws + 384, 512:640].rearrange("(b p) c -> p b c", p=128), ["VW"], vt_reads("sv", ws, ws + 384), "sp", 0)
            for h in range(8):
                g, chn, r0 = h // 4, h // 2, (h % 2) * 64
                segs = [(KS[r0:r0 + 64, g, 0:384], 384, swamask[:, pat, :], [(128, A["VW"][:, b, g * 64:(g + 1) * 64]) for b in range(3)]),
                        (A["cK"][r0:r0 + 64, g, :], 256, None, [(128, A["cV"][:, b, g * 64:(g + 1) * 64]) for b in range(2)])]
                oi = (h // 2) % 2
                plain_head(QT[r0:r0 + 64, chn, :], segs, sinkT[:, l, h:h + 1], h % 2, oi, ["VW", "cV"], ["KW", "cK", "swamask"],
                           after=((lambda oi=oi, chn=chn, tok0=tok0: put_mix(oi, 8 + chn, tok0)) if h % 2 == 1 else None))
            flush_heads()
        for s in range(2):
            p0 = 2048 + 256 * s
            for g in range(2):
                load_swa_k(g, (0, 256), PTs[28 * 128 + g * 64:28 * 128 + (g + 1) * 64, p0:p0 + 256], pts_reads([28], p0, p0 + 256))
            ld(A["VW"][:, 0:2, 0:128], VT[p0:p0 + 256, 512:640].rearrange("(b p) c -> p b c", p=128), ["VW"], vt_reads("sv", p0, p0 + 256), "sp", 0)
            for qt in range(2):
                tok0 = p0 + qt * 128
                qi = rot("QT", 2)
                QT = A["QT"][qi]
                ld(QT[:], PTs[24 * 128:28 * 128, tok0:tok0 + 128].rearrange("(c p) t -> p c t", p=128), ["QT"], pts_reads(range(24, 28), tok0, tok0 + 128), "sp", 0)
                for h in range(8):
                    g, chn, r0 = h // 4, h // 2, (h % 2) * 64
                    segs = [(KS[r0:r0 + 64, g, 0:256], 256, None, [(128, A["VW"][:, b, g * 64:(g + 1) * 64]) for b in range(2)])]
                    oi = (h // 2) % 2
                    plain_head(QT[r0:r0 + 64, chn, :], segs, sinkT[:, l, h:h + 1], h % 2, oi, ["VW"], ["KW"],
                               after=((lambda oi=oi, chn=chn, tok0=tok0: put_mix(oi, 8 + chn, tok0)) if h % 2 == 1 else None))
                flush_heads()

        for h in range(4):
            KT, VD = A["KT"], A["VD"]
            ld(KT[:, 0:2048], PTs[(34 + h) * 128:(35 + h) * 128, 0:2048], ["KT"], pts_reads([34 + h], 0, 2048), "sp", 3)
            ld(KT[:, 2048:2304], cdiffKT[l, h], ["KT"], (), "pool", 2)
            ld(VD[:, 0:16, :], VT[0:2048, 640 + h * 128:640 + (h + 1) * 128].rearrange("(b p) c -> p b c", p=128), ["VD"], vt_reads("dv", 0, 2048), "sp", 3)
            ld(VD[:, 16:18, :], cdiffV[l, :, h * 128:(h + 1) * 128].rearrange("(b p) c -> p b c", p=128), ["VD"], (), "pool", 2)
            for i in range(16):
                tok0 = i * 128
                qi = rot("QT", 2)
                QT = A["QT"][qi]
                ld(QT[:, 0, :], PTs[(30 + h) * 128:(31 + h) * 128, tok0:tok0 + 128], ["QT"], pts_reads([30 + h], tok0, tok0 + 128), "sp", 0)
                segs = [(KT[:, :], 2304, [(128, VD[:, b, :]) for b in range(18)])]
                oi = rot("otokd", 2)
                diff_head(QT[:, 0, :], segs, oi, ["VD"], ["KT"])
                put_mix(oi, 12 + h, tok0)
            for s in range(2):
                p0 = 2048 + 256 * s
                ld(KT[:, 0:256], PTs[(34 + h) * 128:(35 + h) * 128, p0:p0 + 256], ["KT"], pts_reads([34 + h], p0, p0 + 256), "sp", 3)
                ld(VD[:, 0:2, :], VT[p0:p0 + 256, 640 + h * 128:640 + (h + 1) * 128].rearrange("(b p) c -> p b c", p=128), ["VD"], vt_reads("dv", p0, p0 + 256), "sp", 3)
                for qt in range(2):
                    tok0 = p0 + qt * 128
                    qi = rot("QT", 2)
                    QT = A["QT"][qi]
                    ld(QT[:, 0, :], PTs[(30 + h) * 128:(31 + h) * 128, tok0:tok0 + 128], ["QT"], pts_reads([30 + h], tok0, tok0 + 128), "sp", 0)
                    segs = [(KT[:, 0:256], 256, [(128, VD[:, b, :]) for b in range(2)])]
                    oi = rot("otokd", 2)
                    diff_head(QT[:, 0, :], segs, oi, ["VD"], ["KT"])
                    put_mix(oi, 12 + h, tok0)

        if STOP < 4:
            break
        fence("s3")
        alloc_work([("x", [128, 16, TS], F32, 1), ("y", [128, 16, 256], F32, 1), ("sq", [128, TS], BF16, 2), ("rs", [128, TS], F32, 1),
                    ("tmp", [128, TS], F32, 2)])

        def branch_update(l, t, half, xt, y, kc, po):
            g = 1 if t < 4 else 0
            rs = wk["rs"]
            rstd_from_psum(po, rs[:, 0:256], 1.0 / D)
            for c in range(16):
                ti = rot("tmp", 2)
                tmp = wk["tmp"][ti]
                S.op("dve", lambda e, c=c, tmp=tmp: e.scalar_tensor_tensor(out=tmp[:, 0:256], in0=y[:, c, :], scalar=coef[:, l, kc, c, g:g + 1], in1=rs[:, 0:256],
                                                                          op0=ALU.mult, op1=ALU.mult), reads=["y", "rs", ("coef", l)], writes=[("tmp", ti)])
                S.op("pool", lambda e, c=c, tmp=tmp: e.tensor_tensor(out=xt[:, c, half * 256:(half + 1) * 256], in0=xt[:, c, half * 256:(half + 1) * 256],
                                                                    in1=tmp[:, 0:256], op=ALU.add), reads=[("tmp", ti), "xt"], writes=["xt"])

        for t in range(NT):
            xt = wk["x"]
            y = wk["y"]
            S.op("sp", lambda e, t=t, xt=xt: e.dma_start(out=xt[:], in_=xview(src_x, t)), reads=[("XS", t)], writes=["xt"], chan=ch_x)
            for half in range(2):
                c0 = t * TS + half * 256
                po = rot("psO", 2)
                for wt in range(4):
                    wv, wr = load_w(w_out[l, :, wt * 512:(wt + 1) * 512], 16, 512)
                    for j in range(4):
                        m = wt * 4 + j
                        pi = rot("ps", 4)
                        for kc in range(16):
                            S.op("pe", lambda e, j=j, kc=kc, pi=pi, wv=wv, c0=c0: e.matmul(ps[pi][:, 0:256], lhsT=wv[:, kc, j * 128:(j + 1) * 128],
                                                                                         rhs=mixT[:, kc, c0:c0 + 256], start=(kc == 0), stop=(kc == 15)),
                                 reads=[wr, ("hall", t)], writes=[("ps", pi)])
                        S.op("dve", lambda e, m=m, pi=pi: e.tensor_copy(out=y[:, m, :], in_=ps[pi][:, 0:256]), reads=[("ps", pi)], writes=["y"])
                        si = rot("sq", 2)
                        sq = wk["sq"][si]
                        S.op("act", lambda e, m=m, sq=sq: e.activation(out=sq[:, 0:256], in_=y[:, m, :], func=AF.Square), reads=["y"], writes=[("sq", si)])
                        S.op("pe", lambda e, m=m, sq=sq, po=po: e.matmul(psO[po][:, 0:256], lhsT=ones[:], rhs=sq[:, 0:256], start=(m == 0), stop=(m == 15)),
                             reads=[("sq", si), "ones"], writes=[("psO", po)])
                branch_update(l, t, half, xt, y, 2, po)
            S.op("sp", lambda e, t=t, xt=xt: e.dma_start(out=xview(XS, t), in_=xt[:]), reads=["xt"], writes=[("XS", t)], chan=ch_xo)
            premix_tile(l, t, xt, 1, "xt")

        if STOP < 5:
            break
        alloc_work([("r", [128, TS], F32, 2), ("astg", [128, TS], BF16, 4)])
        fence("s4")
        for wt in range(16):
            wv, wr = load_w(w1[l, :, wt * 512:(wt + 1) * 512], 16, 512)
            for t in range(NT):
                for j in range(4):
                    m = wt * 4 + j
                    pi = rot("ps", 4)
                    for kc in range(16):
                        S.op("pe", lambda e, j=j, kc=kc, pi=pi, t=t, wv=wv: e.matmul(ps[pi][:], lhsT=wv[:, kc, j * 128:(j + 1) * 128],
                                                                                   rhs=hall[:, kc, t * TS:(t + 1) * TS], start=(kc == 0), stop=(kc == 15)),
                             reads=[wr, ("hall", t)], writes=[("ps", pi)])
                    ri = rot("r", 2)
                    r = wk["r"][ri]
                    ai = rot("astg", 4)
                    a = wk["astg"][ai]
                    S.op("act", lambda e, pi=pi, r=r: e.activation(out=r[:], in_=ps[pi][:], func=AF.Relu), reads=[("ps", pi)], writes=[("r", ri)])
                    S.op("dve", lambda e, r=r, a=a: e.tensor_tensor(out=a[:], in0=r[:], in1=r[:], op=ALU.mult), reads=[("r", ri)], writes=[("astg", ai)])
                    S.op("sp", lambda e, m=m, t=t, a=a: e.dma_start(out=AH[m * 128:(m + 1) * 128, t * TS:(t + 1) * TS], in_=a[:]),
                         reads=[("astg", ai)], writes=[("AH", m, t)], chan=ch_stg[ai])

        if STOP < 6:
            break
        fence("s5")
        alloc_work([("y", [128, 16, TS], F32, 1), ("sq", [128, TS], BF16, 2), ("rs", [128, TS], F32, 1),
                    ("tmp", [128, TS], F32, 2), ("xc", [128, TS], F32, 3)])
        at_flat = hall[:].rearrange("p a b -> p (a b)")
        last = (l == depth - 1)
        for t in range(NT):
            y = wk["y"]
            g = 1 if t < 4 else 0
            a_t = at_flat[:, 0:64 * TS].rearrange("p (a b) -> p a b", a=64)
            S.op("sp", lambda e, t=t, a_t=a_t: e.dma_start(out=a_t, in_=AH[:, t * TS:(t + 1) * TS].rearrange("(a p) t -> p a t", p=128)),
                 reads=[("AH", m, t) for m in range(64)], writes=[("hall", k) for k in range(NT)], chan=ch_ld[4])
            po = rot("psO", 2)
            for m in range(16):
                i = rot("wb", 2)
                buf = WB[i]
                w2v = buf[:].rearrange("p a b -> p (a b)")[:, 0:64 * 128].rearrange("p (a b) -> p a b", a=64)
                S.op("pool", lambda e, m=m, w2v=w2v: e.dma_start(out=w2v, in_=w2[l, :, m * 128:(m + 1) * 128].rearrange("(a p) c -> p a c", p=128)),
                     writes=[("wb", i)], chan=ch_w[i])
                pi = rot("ps", 4)
                for kc in range(64):
                    S.op("pe", lambda e, kc=kc, pi=pi, w2v=w2v, a_t=a_t: e.matmul(ps[pi][:], lhsT=w2v[:, kc, :], rhs=a_t[:, kc, :],
                                                                                start=(kc == 0), stop=(kc == 63)),
                         reads=[("wb", i), ("hall", 0)], writes=[("ps", pi)])
                S.op("dve", lambda e, m=m, pi=pi: e.tensor_copy(out=y[:, m, :], in_=ps[pi][:]), reads=[("ps", pi)], writes=["y"])
                si = rot("sq", 2)
                sq = wk["sq"][si]
                S.op("act", lambda e, m=m, sq=sq: e.activation(out=sq[:], in_=y[:, m, :], func=AF.Square), reads=["y"], writes=[("sq", si)])
                S.op("pe", lambda e, m=m, sq=sq, po=po: e.matmul(psO[po][:], lhsT=ones[:], rhs=sq[:], start=(m == 0), stop=(m == 15)),
                     reads=[("sq", si), "ones"], writes=[("psO", po)])
            rs = wk["rs"]
            rstd_from_psum(po, rs[:], 1.0 / D)
            for c in range(16):
                xi = rot("xc", 3)
                xc = wk["xc"][xi]
                S.op("sp", lambda e, c=c, t=t, xc=xc: e.dma_start(out=xc[:], in_=XS[c * 128:(c + 1) * 128, t * TS:(t + 1) * TS]),
                     reads=[("XS", t)], writes=[("xc", xi)], chan=ch_ld[5])
                ti = rot("tmp", 2)
                tmp = wk["tmp"][ti]
                S.op("dve", lambda e, c=c, tmp=tmp, g=g: e.scalar_tensor_tensor(out=tmp[:], in0=y[:, c, :], scalar=coef[:, l, 5, c, g:g + 1], in1=rs[:],
                                                                           op0=ALU.mult, op1=ALU.mult), reads=["y", "rs", ("coef", l)], writes=[("tmp", ti)])
                S.op("pool", lambda e, xc=xc, tmp=tmp: e.tensor_tensor(out=xc[:], in0=xc[:], in1=tmp[:], op=ALU.add),
                     reads=[("tmp", ti), ("xc", xi)], writes=[("xc", xi)])
                if last:
                    S.op("sp", lambda e, c=c, t=t, xc=xc: e.dma_start(out=yT[c * 128:(c + 1) * 128, t * TS:(t + 1) * TS], in_=xc[:]),
                         reads=[("xc", xi)], chan=ch_out)
                else:
                    S.op("sp", lambda e, c=c, t=t, xc=xc: e.dma_start(out=XS[c * 128:(c + 1) * 128, t * TS:(t + 1) * TS], in_=xc[:]),
                         reads=[("xc", xi)], writes=[("XSo", t, c)], chan=ch_xo)
        if not last:
            for t in range(NT):
                S.op("dve", lambda e: e.memset(small[:, 62:63], 0.0), reads=[("XSo", t, c) for c in range(16)], writes=[("XS", t)])

    cnt = S.finalize(final_waits=STP)
    S.emit()
    return nc, len(S.ops), cnt


def _na_bias_tables(rpb):
    out = np.empty((rpb.shape[0], 5, 8, 128, 576), np.float32)
    for pi, i in enumerate((0, 1, 2, 14, 15)):
        ks = na_ks(i)
        q = np.arange(128)
        r = 2 * i + q // 64
        c = q % 64
        k = np.arange(576)
        kr = ks + k // 64
        kc = k % 64
        rs = np.clip(r - 4, 0, 24)
        row_ok = (kr[None, :] >= rs[:, None]) & (kr[None, :] < rs[:, None] + 8)
        cs = np.clip(c - 8, 0, 48)
        col_ok = (kc[None, :] >= cs[:, None]) & (kc[None, :] < cs[:, None] + 16)
        ok = row_ok & col_ok
        dr = np.clip(kr[None, :] - r[:, None] + 7, 0, 14)
        dc = np.clip(kc[None, :] - c[:, None], -15, 15) + 15
        g = rpb[:, :, dr, dc]
        out[:, pi] = np.where(ok[None, None], g, np.float32(NEG))
    return out


def _rope_tables():
    t = np.arange(2048)
    rows = (t // 64).astype(np.float32)
    cols = (t % 64).astype(np.float32)
    n = 16
    inv = (np.float32(10000.0) ** (-np.arange(n, dtype=np.float32) / n)).astype(np.float32)
    cosT = np.zeros((128, 2048), np.float32)
    sinT = np.zeros((128, 2048), np.float32)
    for p in range(128):
        d = p % 64
        pos = rows if d < 32 else cols
        j = d % 16
        ang = pos * inv[j]
        cosT[p] = np.cos(ang)
        sinT[p] = -np.sin(ang) if (d % 32) < 16 else np.sin(ang)
    perm = np.zeros((128, 128), np.float32)
    for m in range(128):
        partner = m + 16 if (m % 32) < 16 else m - 16
        perm[partner, m] = 1.0
    return cosT, sinT, perm


def _swa_masks():
    out = np.zeros((128, 3, 384), np.float32)
    for pat, jq in enumerate((0, 5, 15)):
        ws = min(max(128 * (jq - 1), 0), 2048 - 384)
        q = jq * 128 + np.arange(128)
        k = ws + np.arange(384)
        ok = np.abs(q[:, None] - k[None, :]) <= 128
        out[:, pat, :] = np.where(ok, 0.0, NEG)
    return out


_CACHE = {}


def kernel(x_prompt, x_sample, cache_na_kv, cache_swa_kv, cache_diff_kv, c, c_ctx,
           ada_w, ada_b, norm_g, w_in, conv_w, na_rpb, swa_sink, diff_lambda, diff_norm_g,
           w_out, mlp_w1, mlp_w2, _depth=DEPTH):
    f32 = np.float32
    A = lambda a: np.ascontiguousarray(np.asarray(a, dtype=f32))
    x_prompt, x_sample = A(x_prompt), A(x_sample)
    cache_na_kv, cache_swa_kv, cache_diff_kv = A(cache_na_kv), A(cache_swa_kv), A(cache_diff_kv)
    c, c_ctx = A(c), A(c_ctx)
    ada_w, ada_b, norm_g, w_in, conv_w = A(ada_w), A(ada_b), A(norm_g), A(w_in), A(conv_w)
    na_rpb, swa_sink, diff_lambda, diff_norm_g = A(na_rpb), A(swa_sink), A(diff_lambda), A(diff_norm_g)
    w_out, mlp_w1, mlp_w2 = A(w_out), A(mlp_w1), A(mlp_w2)

    if _depth not in _CACHE:
        _CACHE[_depth] = build_program(_depth)[0]
    nc = _CACHE[_depth]

    cosT, sinT, perm = _rope_tables()
    shared = {
        "ada_w": A(ada_w[:_depth]),
        "ada_bT": A(ada_b.reshape(DEPTH, 96, 128).transpose(2, 0, 1)),
        "normgT": A(norm_g.reshape(DEPTH, 4, 16, 128).transpose(3, 0, 1, 2)),
        "w_in": A(w_in[:_depth]), "w_out": A(w_out[:_depth]), "w1": A(mlp_w1[:_depth]), "w2": A(mlp_w2[:_depth]),
        "convwT": A(conv_w.reshape(DEPTH, 3, 4, 128).transpose(3, 0, 2, 1)),
        "nab": A(_na_bias_tables(na_rpb)[:_depth]),
        "sinkB": A(np.broadcast_to(swa_sink[None], (128, DEPTH, 8))),
        "lamP": A(np.broadcast_to(diff_lambda[None], (128, DEPTH, 4, 64))),
        "dgB": A(np.broadcast_to(diff_norm_g[None], (128, DEPTH, 128))),
        "cosT": cosT, "sinT": sinT, "perm": perm, "ident": np.eye(128, dtype=f32),
        "swamask": _swa_masks(),
    }
    in_maps = []
    for core in range(8):
        b = core // 4
        s0, s1 = 2 * core, 2 * core + 1
        m = dict(shared)
        m["xT"] = A(np.concatenate([x_sample[b].T, x_prompt[s0].T, x_prompt[s1].T], axis=1))
        cv = np.stack([c_ctx, c[b]], axis=-1)
        m["cvec"] = A(cv.reshape(16, 128, 2).transpose(1, 0, 2))
        na = cache_na_kv[b]
        m["cnaKT"] = A(na[:, 0].reshape(DEPTH, 256, 4, 128).transpose(0, 2, 3, 1))
        m["cnaV"] = A(na[:, 1].reshape(DEPTH, 256, 512))
        sw = cache_swa_kv[b]
        m["cswaKT"] = A(sw[:, 0].reshape(DEPTH, 256, 128).transpose(0, 2, 1))
        m["cswaV"] = A(sw[:, 1].reshape(DEPTH, 256, 128))
        df = cache_diff_kv[b]
        m["cdiffKT"] = A(df[:, 0].transpose(0, 2, 3, 1))
        m["cdiffV"] = A(df[:, 1].reshape(DEPTH, 256, 512))
        in_maps.append(m)

    res = run_bass_kernel_spmd(nc, in_maps, core_ids=list(range(8)))
    R = res.results
    yp = np.empty((16, 256, D), f32)
    ys = np.empty((2, 2048, D), f32)
    nna = np.empty((16, DEPTH, 2, 256, 8, 64), f32)
    nsw = np.empty((16, DEPTH, 2, 256, 2, 64), f32)
    ndf = np.empty((16, DEPTH, 2, 256, 4, 128), f32)
    for core in range(8):
        yt = np.asarray(R[core]["yT"])
        if core % 4 == 0:
            ys[core // 4] = yt[:, 0:2048].T
        for s in range(2):
            yp[2 * core + s] = yt[:, 2048 + 256 * s:2048 + 256 * (s + 1)].T
            nna[2 * core + s] = np.asarray(R[core]["ona"])[s].reshape(DEPTH, 2, 256, 8, 64)
            nsw[2 * core + s] = np.asarray(R[core]["oswa"])[s].reshape(DEPTH, 2, 256, 2, 64)
            ndf[2 * core + s] = np.asarray(R[core]["odiff"])[s].reshape(DEPTH, 2, 256, 4, 128)
    return (yp, ys, nna, nsw, ndf)
```

```python
import math
import numpy as np
import concourse.bass as bass
import concourse.mybir as mybir
from concourse.bass_utils import run_bass_kernel_spmd

F32 = mybir.dt.float32
BF16 = mybir.dt.bfloat16
AF = mybir.ActivationFunctionType
ALU = mybir.AluOpType
AX = mybir.AxisListType

D = 2048
DEPTH = 4
NTOK = 2560
TS = 512
NT = 5
HID = 8192
INC = 5376
SCALE = 64 ** -0.5
EPS = 1e-6
NEG = -30000.0


class Chan:
    def __init__(self, nc, name, inc=16):
        self.sem = nc.alloc_semaphore(name)
        self.count = 0
        self.inc = inc
        self.last_op = None


class Op:
    __slots__ = ("eng", "fn", "deps", "signal", "chan", "val", "is_dma", "waits")


class _Rec:
    def __getattr__(self, name):
        def f(*a, **k):
            self.__dict__["call"] = (name, a, k)
            return self
        return f


class Sched:
    ENGS = ("pe", "act", "dve", "pool", "sp")

    def __init__(self, nc):
        self.nc = nc
        self.ops = []
        self.last_w = {}
        self.readers = {}
        self.esem = {e: nc.alloc_semaphore("prog_" + e) for e in self.ENGS}
        self.nchan = 0
        self.fence_idx = None

    def chan(self, name=None, inc=16):
        self.nchan += 1
        return Chan(self.nc, name or ("ch%d" % self.nchan), inc)

    def op(self, eng, fn, reads=(), writes=(), chan=None):
        o = Op()
        o.eng = eng
        rec = _Rec()
        fn(rec)
        o.fn = rec.__dict__["call"]
        o.chan = chan
        o.is_dma = chan is not None
        o.signal = o.is_dma
        o.val = None
        deps = set()
        for r in reads:
            w = self.last_w.get(r)
            if w is not None:
                deps.add(w)
        for r in writes:
            w = self.last_w.get(r)
            if w is not None:
                deps.add(w)
            for rd in self.readers.get(r, {}).values():
                if isinstance(rd, list):
                    deps.update(rd)
                else:
                    deps.add(rd)
        i = len(self.ops)
        if self.fence_idx is not None:
            deps.add(self.fence_idx)
        if chan is not None:
            if callable(chan):
                chan = chan()
                o.chan = chan
            if chan.last_op is not None:
                deps.add(chan.last_op)
            chan.last_op = i
        o.deps = deps
        self.ops.append(o)
        for r in reads:
            d = self.readers.setdefault(r, {})
            if o.is_dma:
                d.setdefault("dma", []).append(i)
            else:
                d[eng] = i
        for r in writes:
            self.last_w[r] = i
            self.readers[r] = {}
        if o.is_dma:
            chan.count += chan.inc
            o.val = chan.count
        return i

    def fence(self, fn):
        names = list(dict.fromkeys(list(self.last_w.keys()) + list(self.readers.keys())))
        self.fence_idx = self.op("dve", fn, writes=names)

    def finalize(self, final_waits=()):
        ops = self.ops
        for o in ops:
            for d in o.deps:
                p = ops[d]
                if p.is_dma:
                    continue
                if p.eng == "pe" and o.eng == "pe":
                    continue
                p.signal = True
        cnt = {e: 0 for e in self.ENGS}
        for o in ops:
            if o.signal and not o.is_dma:
                cnt[o.eng] += 1
                o.val = cnt[o.eng]
        seen = {e: {} for e in self.ENGS}
        chan_count_at = {}
        for i, o in enumerate(ops):
            waits = {}
            for d in o.deps:
                p = ops[d]
                if p.is_dma:
                    sem = p.chan.sem
                    v = chan_count_at[id(p.chan)]
                    key = ("c", id(p.chan))
                else:
                    if p.eng == "pe" and o.eng == "pe":
                        continue
                    sem = self.esem[p.eng]
                    v = p.val
                    key = ("e", p.eng)
                if seen[o.eng].get(key, 0) >= v:
                    continue
                if key not in waits or waits[key][1] < v:
                    waits[key] = (sem, v)
            for key, (sem, v) in waits.items():
                seen[o.eng][key] = v
            o.waits = list(waits.values())
            if o.is_dma:
                chan_count_at[id(o.chan)] = o.val
        self.final = [(c.sem, c.count) for c in final_waits]
        return cnt

    def emit(self):
        nc = self.nc
        per = {e: [o for o in self.ops if o.eng == e] for e in self.ENGS}
        esem = self.esem
        final = self.final

        def run(eng_name, eng):
            for o in per[eng_name]:
                for sem, v in o.waits:
                    eng.wait_ge(sem, v)
                name, a, k = o.fn
                ins = getattr(eng, name)(*a, **k)
                if o.is_dma:
                    ins.then_inc(o.chan.sem, o.chan.inc)
                elif o.signal:
                    ins.then_inc(esem[eng_name], 1)

        with nc.Block() as block:
            @block.sync
            def _(e):
                run("sp", e)
                for sem, v in final:
                    e.wait_ge(sem, v)

            @block.scalar
            def _(e):
                run("act", e)

            @block.vector
            def _(e):
                run("dve", e)

            @block.gpsimd
            def _(e):
                run("pool", e)

            @block.tensor
            def _(e):
                run("pe", e)


def chunk_type(m):
    if m < 4: return "na_q"
    if m < 8: return "na_k"
    if m < 12: return "na_v"
    if m < 16: return "u"
    if m < 20: return "gb"
    if m < 24: return "gc"
    if m < 28: return "sq"
    if m == 28: return "sk"
    if m == 29: return "sv"
    if m < 34: return "dq"
    if m < 38: return "dk"
    return "dv"


ROPE_T = ("sq", "sk", "dq", "dk")
VCOL = {"na_v": 0, "sv": 512, "dv": 640}
VBASE = {"na_v": 8, "sv": 29, "dv": 38}
KBASE = {"na_k": 4, "sk": 28, "dk": 34}


def na_pat(i):
    return {0: 0, 1: 1, 14: 3, 15: 4}.get(i, 2)


def na_ks(i):
    return min(max(2 * i - 4, 0), 23)


def build_program(depth=DEPTH):
    nc = bass.Bass("TRN2", target_bir_lowering=False)

    def din(name, shape, dt=F32):
        return nc.dram_tensor(name, list(shape), dt, kind="ExternalInput").ap()

    def dout(name, shape, dt=F32):
        return nc.dram_tensor(name, list(shape), dt, kind="ExternalOutput").ap()

    def dscr(name, shape, dt):
        return nc.dram_tensor(name, list(shape), dt).ap()

    xT_in = din("xT", [D, NTOK])
    cvec = din("cvec", [128, 16, 2])
    ada_w = din("ada_w", [depth, D, 6 * D])
    ada_bT = din("ada_bT", [128, DEPTH, 96])
    normgT = din("normgT", [128, DEPTH, 4, 16])
    w_in = din("w_in", [depth, D, INC])
    w_out = din("w_out", [depth, D, D])
    w1 = din("w1", [depth, D, HID])
    w2 = din("w2", [depth, 16, 128, 64 * 128])
    convwT = din("convwT", [128, DEPTH, 4, 3])
    nab = din("nab", [depth, 5, 8, 128, 576])
    sinkB = din("sinkB", [128, DEPTH, 8])
    lamP = din("lamP", [128, DEPTH, 4, 64])
    dgB = din("dgB", [128, DEPTH, 128])
    cnaKT = din("cnaKT", [DEPTH, 4, 128, 256])
    cnaV = din("cnaV", [DEPTH, 256, 512])
    cswaKT = din("cswaKT", [DEPTH, 128, 256])
    cswaV = din("cswaV", [DEPTH, 256, 128])
    cdiffKT = din("cdiffKT", [DEPTH, 4, 128, 256])
    cdiffV = din("cdiffV", [DEPTH, 256, 512])
    cosT_in = din("cosT", [128, 2048])
    sinT_in = din("sinT", [128, 2048])
    perm_in = din("perm", [128, 128])
    ident_in = din("ident", [128, 128])
    swamask_in = din("swamask", [128, 3, 384])

    yT = dout("yT", [D, NTOK])
    ona = dout("ona", [2, DEPTH, 2, 256, 512])
    oswa = dout("oswa", [2, DEPTH, 2, 256, 128])
    odiff = dout("odiff", [2, DEPTH, 2, 256, 512])

    XS = dscr("XS", [D, NTOK], F32)
    PTs = dscr("PTs", [42 * 128, NTOK], BF16)
    VT = dscr("VT", [NTOK, 1152], BF16)
    AH = dscr("AH", [HID, NTOK], BF16)

    BASE = 16512
    LIMIT = BASE + 212000
    cur = [BASE]

    def sb(name, shape, dt, at=None):
        esz = 4 if dt == F32 else 2
        n = esz
        for s in shape[1:]:
            n *= s
        n = (n + 31) // 32 * 32
        if at is None:
            off = cur[0]
            cur[0] += n
            assert cur[0] <= LIMIT, (name, cur[0])
        else:
            off = at[0]
            at[0] += n
            assert at[0] <= at[1], (name, at[0], at[1])
        return nc.alloc_sbuf_tensor_at(name, list(shape), dt, offset=off)

    ident = sb("ident", [128, 128], BF16)
    perm = sb("perm", [128, 128], BF16)
    ones = sb("ones", [128, 128], BF16)
    cosT = sb("cosT", [128, 2048], BF16)
    sinT = sb("sinT", [128, 2048], BF16)
    modT = sb("modT", [128, DEPTH, 96, 2], F32)
    coef = sb("coef", [128, DEPTH, 6, 16, 2], F32)
    normg = sb("normg", [128, DEPTH, 4, 16], F32)
    adab = sb("adab", [128, DEPTH, 96], F32)
    cv32 = sb("cv32", [128, 16, 2], F32)
    cvb = sb("cvb", [128, 16, 2], BF16)
    convw = sb("convw", [128, DEPTH, 4, 3], F32)
    sinkT = sb("sinkT", [128, DEPTH, 8], F32)
    lamp = sb("lamp", [128, DEPTH, 4, 64], F32)
    lamv = sb("lamv", [128, DEPTH, 4], F32)
    dgs = sb("dgs", [128, DEPTH, 128], F32)
    swamask = sb("swamask", [128, 3, 384], F32)
    epsT = sb("epsT", [128, 1], F32)
    small = sb("small", [128, 64], F32)
    hall = sb("hall", [128, 16, NTOK], BF16)
    WB = [sb("wb%d" % i, [128, 16, 512], BF16) for i in range(2)]
    wbase = WB[0]
    WORK0 = cur[0]
    WORK_END = LIMIT
    ATT0 = WORK0 - 2 * 16 * 512 * 2

    ps = [nc.alloc_psum_tensor("ps%d" % i, [128, 512], F32) for i in range(4)]
    psT = [nc.alloc_psum_tensor("psT%d" % i, [128, 1024], BF16) for i in range(2)]
    psO = [nc.alloc_psum_tensor("psO%d" % i, [128, 512], F32) for i in range(2)]

    S = Sched(nc)
    rotc = {}

    def rot(name, n):
        v = rotc.get(name, 0)
        rotc[name] = v + 1
        return v % n

    ch_w = [S.chan("w0"), S.chan("w1")]
    LDP = {"sp": [S.chan("spld%d" % i) for i in range(12)], "pool": [S.chan("plld%d" % i) for i in range(6)]}
    STP = [S.chan("spst%d" % i) for i in range(12)]

    class _Pick:
        def __init__(self, lst, key):
            self.lst, self.key = lst, key

        def __call__(self):
            return self.lst[rot(self.key, len(self.lst))]

    ld_sp = _Pick(LDP["sp"], "ldsp")
    ld_pl = _Pick(LDP["pool"], "ldpl")
    st_sp = _Pick(STP, "stsp")
    ch_in = ld_sp
    ch_x = ld_sp
    ch_xo = st_sp
    ch_stg = [st_sp] * 4
    ch_out = st_sp
    ch_ld = [ld_sp] * 8

    def ld_const(dst, src, eng="sp", name=None):
        S.op(eng, lambda e: e.dma_start(out=dst, in_=src), writes=[name], chan=(ld_sp if eng == "sp" else ld_pl))

    ld_const(ident[:], ident_in, "pool", "ident")
    ld_const(perm[:], perm_in, "pool", "perm")
    ld_const(cosT[:], cosT_in, "pool", "cosT")
    ld_const(sinT[:], sinT_in, "pool", "sinT")
    ld_const(adab[:], ada_bT, "sp", "adab")
    ld_const(normg[:], normgT, "sp", "normg")
    ld_const(cv32[:], cvec, "sp", "cv32")
    ld_const(convw[:], convwT, "sp", "convw")
    ld_const(sinkT[:], sinkB, "sp", "sinkT")
    ld_const(lamp[:], lamP, "sp", "lamp")
    ld_const(dgs[:], dgB, "sp", "dgs")
    ld_const(swamask[:], swamask_in, "sp", "swamask")
    S.op("dve", lambda e: e.memset(ones[:], 1.0), writes=["ones"])
    S.op("dve", lambda e: e.memset(epsT[:], EPS), writes=["epsT"])
    S.op("act", lambda e: e.activation(out=cvb[:], in_=cv32[:], func=AF.Silu), reads=["cv32"], writes=["cvb"])

    import os as _os
    STOP = int(_os.environ.get("KSTOP", "99"))
    lam_inits = [0.8 - 0.6 * math.exp(-0.3 * l) for l in range(DEPTH)]
    lprod = sb("lprod", [128, 2, 64], F32)
    for l in range(depth if STOP >= -1 else 0):
        for j in range(2):
            S.op("dve", lambda e, l=l, j=j: e.tensor_tensor(out=lprod[:, j, :], in0=lamp[:, l, 2 * j, :],
                                                            in1=lamp[:, l, 2 * j + 1, :], op=ALU.mult),
                 reads=["lamp"], writes=[("lprod", j)])
            S.op("dve", lambda e, l=l, j=j: e.tensor_reduce(out=lamv[:, l, j:j + 1], in_=lprod[:, j, :], axis=AX.X, op=ALU.add),
                 reads=[("lprod", j)], writes=[("lamv", l, j)])
            S.op("act", lambda e, l=l, j=j: e.activation(out=lamv[:, l, j:j + 1], in_=lamv[:, l, j:j + 1], func=AF.Exp),
                 reads=[("lamv", l, j)], writes=[("lamv", l, j)])
        S.op("dve", lambda e, l=l: e.tensor_tensor(out=lamv[:, l, 2:3], in0=lamv[:, l, 0:1], in1=lamv[:, l, 1:2], op=ALU.subtract),
             reads=[("lamv", l, 0), ("lamv", l, 1)], writes=[("lamv", l, 2)])
        S.op("dve", lambda e, l=l: e.tensor_scalar_add(out=lamv[:, l, 2:3], in0=lamv[:, l, 2:3], scalar1=float(lam_inits[l])),
             reads=[("lamv", l, 2)], writes=[("lamv", l, 2)])
        S.op("dve", lambda e, l=l: e.tensor_scalar_mul(out=dgs[:, l, :], in0=dgs[:, l, :], scalar1=float(1.0 - lam_inits[l])),
             reads=["dgs"], writes=["dgs"])

    def load_w(src_ap, kc, ncols):
        i = rot("wb", 2)
        buf = WB[i]
        flat = kc * ncols
        dst = buf[:].rearrange("p a b -> p (a b)")[:, 0:flat].rearrange("p (a b) -> p a b", a=kc)
        S.op("pool", lambda e: e.dma_start(out=dst, in_=src_ap.rearrange("(a p) c -> p a c", p=128)),
             writes=[("wb", i)], chan=ch_w[i])
        return dst, ("wb", i)

    for l in range(depth if STOP >= 0 else 0):
        for wt in range(24):
            wv, wr = load_w(ada_w[l, :, wt * 512:(wt + 1) * 512], 16, 512)
            for j in range(4 if _os.environ.get("KSUB") != "nomm" else 0):
                chn = wt * 4 + j
                pi = rot("ps", 4)
                for kc in range(16):
                    S.op("pe", lambda e, wv=wv, j=j, kc=kc, pi=pi: e.matmul(ps[pi][:, 0:2], lhsT=wv[:, kc, j * 128:(j + 1) * 128],
                                                                         rhs=cvb[:, kc, :], start=(kc == 0), stop=(kc == 15)),
                         reads=[wr, "cvb"], writes=[("ps", pi)])
                S.op("dve", lambda e, l=l, chn=chn, pi=pi: e.tensor_scalar_add(out=modT[:, l, chn, :], in0=ps[pi][:, 0:2],
                                                                            scalar1=adab[:, l, chn:chn + 1]),
                     reads=[("ps", pi), "adab"], writes=[("modT", l)])
        for g in range(2 if _os.environ.get("KSUB") not in ("nomm", "nocoef") else 0):
            def mt(i, l=l, g=g):
                return modT[:, l, i * 16:(i + 1) * 16, g]
            S.op("dve", lambda e, l=l, g=g, mt=mt: e.scalar_tensor_tensor(out=coef[:, l, 0, :, g], in0=mt(1), scalar=1.0, in1=normg[:, l, 0, :],
                                                                          op0=ALU.add, op1=ALU.mult), reads=[("modT", l), "normg"], writes=[("coef", l)])
            S.op("dve", lambda e, l=l, g=g, mt=mt: e.tensor_copy(out=coef[:, l, 1, :, g], in_=mt(0)), reads=[("modT", l)], writes=[("coef", l)])
            S.op("dve", lambda e, l=l, g=g, mt=mt: e.tensor_tensor(out=coef[:, l, 2, :, g], in0=mt(2), in1=normg[:, l, 1, :], op=ALU.mult),
                 reads=[("modT", l), "normg"], writes=[("coef", l)])
            S.op("dve", lambda e, l=l, g=g, mt=mt: e.scalar_tensor_tensor(out=coef[:, l, 3, :, g], in0=mt(4), scalar=1.0, in1=normg[:, l, 2, :],
                                                                          op0=ALU.add, op1=ALU.mult), reads=[("modT", l), "normg"], writes=[("coef", l)])
            S.op("dve", lambda e, l=l, g=g, mt=mt: e.tensor_copy(out=coef[:, l, 4, :, g], in_=mt(3)), reads=[("modT", l)], writes=[("coef", l)])
            S.op("dve", lambda e, l=l, g=g, mt=mt: e.tensor_tensor(out=coef[:, l, 5, :, g], in0=mt(5), in1=normg[:, l, 3, :], op=ALU.mult),
                 reads=[("modT", l), "normg"], writes=[("coef", l)])

    def xview(ap2d, t):
        return ap2d[:, t * TS:(t + 1) * TS].rearrange("(c p) t -> p c t", p=128)

    def rstd_from_psum(pso_i, rs, scale):
        n = rs.shape[-1] if hasattr(rs, "shape") else None
        S.op("act", lambda e: e.activation(out=rs, in_=psO[pso_i][:, 0:rs.shape[1]], func=AF.Sqrt, bias=epsT[:], scale=scale),
             reads=[("psO", pso_i), "epsT"], writes=["rs"])
        S.op("dve", lambda e: e.reciprocal(out=rs, in_=rs), reads=["rs"], writes=["rs"])

    def premix_tile(l, t, xt, kind, xres):
        g = 1 if t < 4 else 0
        ks, kb = (0, 1) if kind == 0 else (3, 4)
        sq = wk["sq"]
        po = rot("psO", 2)
        for c in range(16):
            si = rot("sq", 2)
            S.op("act", lambda e, c=c, si=si: e.activation(out=sq[si][:], in_=xt[:, c, :], func=AF.Square),
                 reads=[xres], writes=[("sq", si)])
            S.op("pe", lambda e, c=c, si=si, po=po: e.matmul(psO[po][:], lhsT=ones[:], rhs=sq[si][:], start=(c == 0), stop=(c == 15)),
                 reads=[("sq", si), "ones"], writes=[("psO", po)])
        rs = wk["rs"]
        rstd_from_psum(po, rs[:], 1.0 / D)
        for c in range(16):
            ti = rot("tmp", 2)
            tmp = wk["tmp"][ti]
            S.op("dve", lambda e, c=c, tmp=tmp: e.scalar_tensor_tensor(out=tmp[:], in0=xt[:, c, :], scalar=coef[:, l, ks, c, g:g + 1], in1=rs[:],
                                                                      op0=ALU.mult, op1=ALU.mult),
                 reads=[xres, "rs", ("coef", l)], writes=[("tmp", ti)])
            S.op("act", lambda e, c=c, tmp=tmp: e.activation(out=hall[:, c, t * TS:(t + 1) * TS], in_=tmp[:], func=AF.Identity,
                                                             bias=coef[:, l, kb, c, g:g + 1], scale=1.0),
                 reads=[("tmp", ti), ("coef", l)], writes=[("hall", t)])

    wk = {}

    def alloc_work(names):
        at = [WORK0, WORK_END]
        wk.clear()
        for nm, shape, dt, n in names:
            if n == 1:
                wk[nm] = sb("wk_%s_%d" % (nm, rot("wkname", 10 ** 9)), shape, dt, at)
            else:
                wk[nm] = [sb("wk_%s_%d" % (nm, rot("wkname", 10 ** 9)), shape, dt, at) for _ in range(n)]

    def fence(tag):
        S.fence(lambda e: e.memset(small[:, 63:64], 0.0))

    import os as _os
    STOP = int(_os.environ.get("KSTOP", "99"))
    for l in range(depth):
        if STOP < 1:
            break
        src_x = xT_in if l == 0 else XS
        fence("s1a")
        alloc_work([("x", [128, 16, TS], F32, 1), ("sq", [128, TS], BF16, 2), ("rs", [128, TS], F32, 1),
                    ("tmp", [128, TS], F32, 2), ("stg", [128, TS], BF16, 4), ("xb", [128, TS], BF16, 2),
                    ("t1", [128, TS], F32, 2), ("fst", [128, 256], F32, 2), ("vst", [128, 256], BF16, 2)])
        for t in range(NT):
            xt = wk["x"]
            S.op("sp", lambda e, t=t, xt=xt: e.dma_start(out=xt[:], in_=xview(src_x, t)), reads=[("XS", t)], writes=["xt"], chan=ch_x)
            premix_tile(l, t, xt, 0, "xt")
        if STOP < 2:
            break
        for wt in range(21):
            wv, wr = load_w(w_in[l, :, wt * 256:(wt + 1) * 256], 16, 256)
            for t in range(NT):
                for j in range(2):
                    m = wt * 2 + j
                    typ = chunk_type(m)
                    if typ in VBASE:
                        continue
                    pi = rot("ps", 4)
                    for kc in range(16):
                        S.op("pe", lambda e, j=j, kc=kc, pi=pi, t=t, wv=wv: e.matmul(ps[pi][:], lhsT=wv[:, kc, j * 128:(j + 1) * 128],
                                                                                   rhs=hall[:, kc, t * TS:(t + 1) * TS],
                                                                                   start=(kc == 0), stop=(kc == 15)),
                             reads=[wr, ("hall", t)], writes=[("ps", pi)])
                    si = rot("stg", 4)
                    stg = wk["stg"][si]
                    if typ in ROPE_T and t < 4 and _os.environ.get("KSUB") != "norope":
                        xi = rot("xb", 2)
                        xb = wk["xb"][xi]
                        t1 = wk["t1"][xi]
                        S.op("act", lambda e, pi=pi, xb=xb: e.copy(out=xb[:], in_=ps[pi][:]), reads=[("ps", pi)], writes=[("xb", xi)])
                        p2 = rot("ps", 4)
                        S.op("pe", lambda e, p2=p2, xb=xb: e.matmul(ps[p2][:], lhsT=perm[:], rhs=xb[:], start=True, stop=True),
                             reads=[("xb", xi), "perm"], writes=[("ps", p2)])
                        S.op("dve", lambda e, xb=xb, t1=t1, t=t: e.tensor_tensor(out=t1[:], in0=xb[:], in1=cosT[:, t * TS:(t + 1) * TS], op=ALU.mult),
                             reads=[("xb", xi), "cosT"], writes=[("t1", xi)])
                        S.op("dve", lambda e, p2=p2, t=t, xb=xb: e.tensor_tensor(out=xb[:], in0=ps[p2][:], in1=sinT[:, t * TS:(t + 1) * TS], op=ALU.mult),
                             reads=[("ps", p2), "sinT"], writes=[("xb", xi)])
                        S.op("dve", lambda e, xb=xb, t1=t1, stg=stg: e.tensor_tensor(out=stg[:], in0=t1[:], in1=xb[:], op=ALU.add),
                             reads=[("xb", xi), ("t1", xi)], writes=[("stg", si)])
                    else:
                        S.op("act", lambda e, pi=pi, stg=stg: e.copy(out=stg[:], in_=ps[pi][:]), reads=[("ps", pi)], writes=[("stg", si)])
                    S.op("sp", lambda e, m=m, t=t, stg=stg: e.dma_start(out=PTs[m * 128:(m + 1) * 128, t * TS:(t + 1) * TS], in_=stg[:]),
                         reads=[("stg", si)], writes=[("PTs", m, t)], chan=ch_stg[si])
                passes = []
                for j in range(2):
                    m = wt * 2 + j
                    typ = chunk_type(m)
                    if typ in VBASE or (typ in KBASE and t == 4):
                        passes.append((j, m, typ))
                if not passes or _os.environ.get("KSUB") == "notok":
                    continue
                j0 = passes[0][0]
                ncol = 128 * len(passes)
                for tb in range(4):
                    pi = rot("ps", 4)
                    tok0 = t * TS + tb * 128
                    for kc in range(16):
                        S.op("pe", lambda e, kc=kc, pi=pi, tok0=tok0, wv=wv, j0=j0, ncol=ncol: e.matmul(
                            ps[pi][:, 0:ncol], lhsT=hall[:, kc, tok0:tok0 + 128], rhs=wv[:, kc, j0 * 128:j0 * 128 + ncol],
                            start=(kc == 0), stop=(kc == 15)), reads=[wr, ("hall", t)], writes=[("ps", pi)])
                    for pj, (j, m, typ) in enumerate(passes):
                        c0 = pj * 128
                        fi = rot("fst", 2)
                        fst = wk["fst"][fi]
                        S.op("act", lambda e, pi=pi, c0=c0, fst=fst: e.copy(out=fst[:, 0:128], in_=ps[pi][:, c0:c0 + 128]),
                             reads=[("ps", pi)], writes=[("fst", fi)])
                        if typ in VBASE:
                            vi = rot("vst", 2)
                            vst = wk["vst"][vi]
                            vc = VCOL[typ] + (m - VBASE[typ]) * 128
                            S.op("dve", lambda e, fst=fst, vst=vst: e.tensor_copy(out=vst[:, 0:128], in_=fst[:, 0:128]),
                                 reads=[("fst", fi)], writes=[("vst", vi)])
                            S.op("sp", lambda e, vst=vst, tok0=tok0, vc=vc: e.dma_start(out=VT[tok0:tok0 + 128, vc:vc + 128], in_=vst[:, 0:128]),
                                 reads=[("vst", vi)], writes=[("VT", m, tok0)], chan=st_sp)
                        if t == 4:
                            sq_i = tb // 2
                            r0 = (tb % 2) * 128
                            if typ in ("na_k", "na_v"):
                                base = KBASE["na_k"] if typ == "na_k" else VBASE["na_v"]
                                dst = ona[sq_i, l, 0 if typ == "na_k" else 1, r0:r0 + 128, (m - base) * 128:(m - base + 1) * 128]
                            elif typ in ("sk", "sv"):
                                dst = oswa[sq_i, l, 0 if typ == "sk" else 1, r0:r0 + 128, :]
                            else:
                                base = KBASE["dk"] if typ == "dk" else VBASE["dv"]
                                dst = odiff[sq_i, l, 0 if typ == "dk" else 1, r0:r0 + 128, (m - base) * 128:(m - base + 1) * 128]
                            S.op("sp", lambda e, fst=fst, dst=dst: e.dma_start(out=dst, in_=fst[:, 0:128]),
                                 reads=[("fst", fi)], chan=ch_out)

        if STOP < 3:
            break
        fence("s2")
        at = [ATT0, WORK_END]
        A = {}

        def asb(nm, shape, dt, n=1):
            if n == 1:
                A[nm] = sb("at_%s_%d" % (nm, rot("wkname", 10 ** 9)), shape, dt, at)
            else:
                A[nm] = [sb("at_%s_%d" % (nm, rot("wkname", 10 ** 9)), shape, dt, at) for _ in range(n)]

        asb("Sb", [128, 2312], F32, 2)
        asb("P", [128, 2304], BF16, 2)
        asb("Pt", [128, 8, 128], BF16, 2)
        asb("KT", [128, 2304], BF16)
        asb("VD", [128, 18, 128], BF16)
        asb("QT", [128, 4, 128], BF16, 2)
        asb("KW", [128, 4, 576], BF16)
        asb("VW", [128, 5, 512], BF16)
        asb("bias", [128, 8, 576], BF16)
        asb("cK", [128, 4, 256], BF16)
        asb("cV", [128, 2, 512], BF16)
        asb("otok", [128, 128], BF16, 2)
        asb("junk", [128, 128], F32)
        asb("cz", [128, 3, 516], BF16)
        asb("cacc", [128, 512], F32, 2)
        mixT = hall

        def scores(qT, kT, nk, sbi, off, bias=None, kres=()):
            Sb = A["Sb"][sbi]
            for c0 in range(0, nk, 512):
                n = min(512, nk - c0)
                pi = rot("ps", 4)
                S.op("pe", lambda e, pi=pi, n=n, c0=c0: e.matmul(ps[pi][:, 0:n], lhsT=qT, rhs=kT[:, c0:c0 + n], start=True, stop=True),
                     reads=["QT"] + list(kres), writes=[("ps", pi)])
                if bias is None:
                    S.op("act", lambda e, pi=pi, n=n, c0=c0: e.activation(out=Sb[:, off + c0:off + c0 + n], in_=ps[pi][:, 0:n], func=AF.Identity, scale=SCALE),
                         reads=[("ps", pi)], writes=[("Sb", sbi, off + c0)])
                else:
                    S.op("dve", lambda e, pi=pi, n=n, c0=c0: e.scalar_tensor_tensor(out=Sb[:, off + c0:off + c0 + n], in0=ps[pi][:, 0:n], scalar=SCALE,
                                                                                   in1=bias[:, c0:c0 + n], op0=ALU.mult, op1=ALU.add),
                         reads=[("ps", pi), "bias"], writes=[("Sb", sbi, off + c0)])
            return [("Sb", sbi, off + c0) for c0 in range(0, nk, 512)]

        def softmax(sbi, W, sres, k):
            Sb = A["Sb"][sbi]
            P = A["P"][sbi]
            mx = small[:, 4 * k:4 * k + 1]
            nmx = small[:, 4 * k + 1:4 * k + 2]
            rs = small[:, 4 * k + 2:4 * k + 3]
            sm = ("small", k)
            S.op("dve", lambda e: e.tensor_reduce(out=mx, in_=Sb[:, 0:W], axis=AX.X, op=ALU.max), reads=sres, writes=[sm])
            S.op("dve", lambda e: e.tensor_scalar_mul(out=nmx, in0=mx, scalar1=-1.0), reads=[sm], writes=[sm])
            S.op("dve", lambda e: e.memset(rs, 0.0), writes=[sm])
            Wp = min(W, 2304)
            S.op("act", lambda e: e.activation(out=P[:, 0:Wp], in_=Sb[:, 0:Wp], func=AF.Exp, bias=nmx, scale=1.0, accum_out=rs),
                 reads=sres + [sm], writes=[("P", sbi), sm])
            if W > Wp:
                S.op("act", lambda e: e.activation(out=Sb[:, Wp:W], in_=Sb[:, Wp:W], func=AF.Exp, bias=nmx, scale=1.0, accum_out=rs),
                     reads=sres + [sm], writes=[sm] + sres)
            S.op("dve", lambda e: e.reciprocal(out=rs, in_=rs), reads=[sm], writes=[sm])
            return rs

        def pv(sbi, blocks, dv, vres):
            P = A["P"][sbi]
            po = rot("psO", 2)
            nb = len(blocks)
            for g0 in range(0, nb, 8):
                grp = blocks[g0:g0 + 8]
                ti = rot("psT", 2)
                pti = rot("Pt", 2)
                Pt = A["Pt"][pti]
                for i, (off, nk, vap) in enumerate(grp):
                    S.op("pe", lambda e, i=i, off=off, nk=nk, ti=ti: e.transpose(out=psT[ti][0:nk, i * 128:(i + 1) * 128], in_=P[:, off:off + nk], identity=ident[:]),
                         reads=[("P", sbi), "ident"], writes=[("psT", ti)])
                ng = len(grp)
                full = all(nk == 128 for (_, nk, _) in grp)
                eng = "act" if rot("pte", 2) == 0 else "dve"
                if full:
                    if eng == "act":
                        S.op("act", lambda e, ti=ti, ng=ng, Pt=Pt: e.copy(out=Pt[:, 0:ng, :].rearrange("p a b -> p (a b)"), in_=psT[ti][:, 0:ng * 128]),
                             reads=[("psT", ti)], writes=[("Pt", pti)])
                    else:
                        S.op("dve", lambda e, ti=ti, ng=ng, Pt=Pt: e.tensor_copy(out=Pt[:, 0:ng, :].rearrange("p a b -> p (a b)"), in_=psT[ti][:, 0:ng * 128]),
                             reads=[("psT", ti)], writes=[("Pt", pti)])
                else:
                    for i, (off, nk, vap) in enumerate(grp):
                        S.op("dve", lambda e, i=i, nk=nk, ti=ti, Pt=Pt: e.tensor_copy(out=Pt[0:nk, i, :], in_=psT[ti][0:nk, i * 128:(i + 1) * 128]),
                             reads=[("psT", ti)], writes=[("Pt", pti)])
                for i, (off, nk, vap) in enumerate(grp):
                    S.op("pe", lambda e, i=i, nk=nk, vap=vap, po=po, first=(g0 + i == 0), last=(g0 + i == nb - 1), Pt=Pt: e.matmul(
                        psO[po][:, 0:dv], lhsT=Pt[0:nk, i, :], rhs=vap, start=first, stop=last),
                        reads=[("Pt", pti)] + list(vres), writes=[("psO", po)])
            return po

        def put_mix(oi, chunk, tok0):
            ot = A["otok"][oi]
            ti = rot("psT", 2)
            S.op("pe", lambda e: e.transpose(out=psT[ti][:, 0:128], in_=ot[:], identity=ident[:]), reads=[("otok", oi), "ident"], writes=[("psT", ti)])
            S.op("act", lambda e: e.copy(out=mixT[:, chunk, tok0:tok0 + 128], in_=psT[ti][:, 0:128]), reads=[("psT", ti)],
                 writes=[("hall", tok0 // TS)])

        pend = []

        def flush_heads():
            while pend:
                pend.pop(0)()

        def plain_head(qT, segs, sink, hloc, oi, vres, kres, after=None):
            k = rot("plainbuf", 2)
            off = 0
            sres = []
            blocks = []
            for (kT, nk, bias, vbl) in segs:
                sres += scores(qT, kT, nk, k, off, bias, kres)
                o2 = off
                for (nkb, vap) in vbl:
                    blocks.append((o2, nkb, vap))
                    o2 += nkb
                off += nk
            W = off
            if sink is not None:
                Sb = A["Sb"][k]
                S.op("dve", lambda e, W=W: e.tensor_copy(out=Sb[:, W:W + 1], in_=sink), reads=["sinkT"], writes=[("Sb", k, "sink")])
                sres.append(("Sb", k, "sink"))
                W += 1
            Pw = off
            rs = softmax_w(k, W, Pw, sres, k)

            def phase_b():
                po = pv(k, blocks, 64, vres)
                ot = A["otok"][oi]
                S.op("act", lambda e, po=po: e.activation(out=ot[:, hloc * 64:(hloc + 1) * 64], in_=psO[po][:, 0:64], func=AF.Identity, scale=rs),
                     reads=[("psO", po), ("small", k)], writes=[("otok", oi)])
                if after is not None:
                    after()
            flush_heads()
            pend.append(phase_b)

        def softmax_w(sbi, W, Pw, sres, k):
            Sb = A["Sb"][sbi]
            P = A["P"][sbi]
            mx = small[:, 4 * k:4 * k + 1]
            nmx = small[:, 4 * k + 1:4 * k + 2]
            rs = small[:, 4 * k + 2:4 * k + 3]
            r2 = small[:, 4 * k + 3:4 * k + 4]
            sm = ("small", k)
            S.op("dve", lambda e: e.tensor_reduce(out=mx, in_=Sb[:, 0:W], axis=AX.X, op=ALU.max), reads=sres, writes=[sm])
            S.op("dve", lambda e: e.tensor_scalar_mul(out=nmx, in0=mx, scalar1=-1.0), reads=[sm], writes=[sm])
            S.op("dve", lambda e: e.memset(rs, 0.0), writes=[sm])
            S.op("act", lambda e: e.activation(out=P[:, 0:Pw], in_=Sb[:, 0:Pw], func=AF.Exp, bias=nmx, scale=1.0, accum_out=rs),
                 reads=sres + [sm], writes=[("P", sbi), sm])
            if W > Pw:
                S.op("act", lambda e: e.activation(out=r2, in_=Sb[:, Pw:W], func=AF.Exp, bias=nmx, scale=1.0), reads=sres + [sm], writes=[sm])
                S.op("dve", lambda e: e.tensor_tensor(out=rs, in0=rs, in1=r2, op=ALU.add), reads=[sm], writes=[sm])
            S.op("dve", lambda e: e.reciprocal(out=rs, in_=rs), reads=[sm], writes=[sm])
            return rs

        def diff_head(qT, segs, oi, vres, kres):
            rss = []
            W = sum(s[1] for s in segs)
            for side in range(2):
                off = 0
                sres = []
                for (kT, nk, vbl) in segs:
                    sres += scores(qT[side * 64:(side + 1) * 64, :], kT[side * 64:(side + 1) * 64, :], nk, side, off, None, kres)
                    off += nk
                rss.append(softmax_w(side, W, W, sres, side))
            blocks = []
            off = 0
            for (kT, nk, vbl) in segs:
                o2 = off
                for (nkb, vap) in vbl:
                    blocks.append((o2, nkb, vap))
                    o2 += nkb
                off += nk
            P1, P2 = A["P"][0], A["P"][1]
            lr2 = small[:, 12:13]
            S.op("dve", lambda e: e.tensor_tensor(out=lr2, in0=rss[1], in1=lamv[:, l, 2:3], op=ALU.mult),
                 reads=[("small", 1), ("lamv", l, 2)], writes=[("small", 3)])
            S.op("dve", lambda e: e.tensor_scalar_mul(out=P2[:, 0:W], in0=P2[:, 0:W], scalar1=lr2),
                 reads=[("P", 1), ("small", 3)], writes=[("P", 1)])
            S.op("dve", lambda e: e.scalar_tensor_tensor(out=P1[:, 0:W], in0=P1[:, 0:W], scalar=rss[0], in1=P2[:, 0:W], op0=ALU.mult, op1=ALU.subtract),
                 reads=[("P", 0), ("P", 1), ("small", 0)], writes=[("P", 0)])
            po = pv(0, blocks, 128, vres)
            ss = small[:, 13:14]
            S.op("dve", lambda e: e.memset(ss, 0.0), writes=[("small", 4)])
            S.op("act", lambda e, po=po: e.activation(out=A["junk"][:], in_=psO[po][:, 0:128], func=AF.Square, accum_out=ss),
                 reads=[("psO", po), ("small", 4)], writes=[("small", 4), "junk"])
            S.op("act", lambda e: e.activation(out=ss, in_=ss, func=AF.Sqrt, bias=epsT[:], scale=1.0 / 128), reads=[("small", 4), "epsT"], writes=[("small", 4)])
            S.op("dve", lambda e: e.reciprocal(out=ss, in_=ss), reads=[("small", 4)], writes=[("small", 4)])
            ot = A["otok"][oi]
            S.op("dve", lambda e, po=po: e.scalar_tensor_tensor(out=ot[:], in0=psO[po][:, 0:128], scalar=ss, in1=dgs[:, l, :], op0=ALU.mult, op1=ALU.mult),
                 reads=[("psO", po), ("small", 4), "dgs"], writes=[("otok", oi)])

        def ld(dst, src, names_w, reads=(), eng="sp", ci=0):
            S.op(eng, lambda e: e.dma_start(out=dst, in_=src), reads=list(reads), writes=list(names_w), chan=(ld_sp if eng == "sp" else ld_pl))

        def pts_reads(chunks, t0, t1):
            return [("PTs", m, t) for m in chunks for t in range(t0 // TS, (t1 - 1) // TS + 1)]

        def vt_reads(typ, t0, t1):
            n = 1 if typ == "sv" else 4
            return [("VT", VBASE[typ] + k, tk) for k in range(n) for tk in range(t0 // 128 * 128, t1, 128)]

        def conv_seg(tok0, n, left, right):
            cz = A["cz"]
            for cc in range(4):
                lo = tok0 - (1 if left else 0)
                hi = tok0 + n + (1 if right else 0)
                o = 0 if left else 1
                for k, base in enumerate((12, 20, 16)):
                    m = base + cc
                    ld(cz[:, k, o:o + hi - lo], PTs[m * 128:(m + 1) * 128, lo:hi], [("cz", k)], pts_reads([m], lo, hi), "sp", 1)
                if not left:
                    S.op("pool", lambda e: e.memset(cz[:, 0:2, 0:1], 0.0), writes=[("cz", 0), ("cz", 1)])
                if not right:
                    S.op("pool", lambda e, n=n: e.memset(cz[:, 0:2, n + 1:n + 2], 0.0), writes=[("cz", 0), ("cz", 1)])
                S.op("pool", lambda e, n=n: e.tensor_tensor(out=cz[:, 0, 0:n + 2], in0=cz[:, 0, 0:n + 2], in1=cz[:, 1, 0:n + 2], op=ALU.mult),
                     reads=[("cz", 0), ("cz", 1)], writes=[("cz", 0)])
                ai = rot("cacc", 2)
                acc = A["cacc"][ai]
                S.op("pool", lambda e, n=n, cc=cc, acc=acc: e.tensor_scalar_mul(out=acc[:, 0:n], in0=cz[:, 0, 0:n], scalar1=convw[:, l, cc, 0:1]),
                     reads=[("cz", 0), "convw"], writes=[("cacc", ai)])
                for k in (1, 2):
                    S.op("dve", lambda e, n=n, cc=cc, acc=acc, k=k: e.scalar_tensor_tensor(out=acc[:, 0:n], in0=cz[:, 0, k:k + n], scalar=convw[:, l, cc, k:k + 1],
                                                                                         in1=acc[:, 0:n], op0=ALU.mult, op1=ALU.add),
                         reads=[("cz", 0), "convw", ("cacc", ai)], writes=[("cacc", ai)])
                S.op("pool", lambda e, n=n, cc=cc, acc=acc: e.tensor_tensor(out=mixT[:, 4 + cc, tok0:tok0 + n], in0=acc[:, 0:n], in1=cz[:, 2, 1:n + 1], op=ALU.mult),
                     reads=[("cacc", ai), ("cz", 2)], writes=[("hall", tok0 // TS)])

        for t in range(4):
            conv_seg(t * TS, TS, t > 0, t < 3)
        conv_seg(2048, 256, False, False)
        conv_seg(2304, 256, False, False)

        ld(A["cK"][:], cnaKT[l].rearrange("c p k -> p c k"), ["cK"], (), "pool", 2)
        ld(A["cV"][:], cnaV[l].rearrange("(b p) c -> p b c", p=128), ["cV"], (), "pool", 2)
        for i in range(16):
            tok0 = i * 128
            k0 = na_ks(i) * 64
            qi = rot("QT", 2)
            QT = A["QT"][qi]
            ld(QT[:], PTs[0:512, tok0:tok0 + 128].rearrange("(c p) t -> p c t", p=128), ["QT"], pts_reads(range(0, 4), tok0, tok0 + 128), "sp", 0)
            ld(A["KW"][:], PTs[512:1024, k0:k0 + 576].rearrange("(c p) t -> p c t", p=128), ["KW"], pts_reads(range(4, 8), k0, k0 + 576), "sp", 0)
            ld(A["VW"][:, 0:4, :], VT[k0:k0 + 512, 0:512].rearrange("(b p) c -> p b c", p=128), ["VW"], vt_reads("na_v", k0, k0 + 576), "sp", 0)
            ld(A["VW"][0:64, 4, :], VT[k0 + 512:k0 + 576, 0:512], ["VW"], (), "sp", 0)
            ld(A["bias"][:], nab[l, na_pat(i)].rearrange("h p k -> p h k"), ["bias"], (), "pool", 2)
            for h in range(8):
                chn, r0 = h // 2, (h % 2) * 64
                segs = [(A["KW"][r0:r0 + 64, chn, :], 576, A["bias"][:, h, :],
                         [(128, A["VW"][:, b, h * 64:(h + 1) * 64]) for b in range(4)] + [(64, A["VW"][0:64, 4, h * 64:(h + 1) * 64])]),
                        (A["cK"][r0:r0 + 64, chn, :], 256, None, [(128, A["cV"][:, b, h * 64:(h + 1) * 64]) for b in range(2)])]
                oi = (h // 2) % 2
                plain_head(QT[r0:r0 + 64, chn, :], segs, None, h % 2, oi, ["VW", "cV"], ["KW", "cK"],
                           after=((lambda oi=oi, chn=chn, tok0=tok0: put_mix(oi, chn, tok0)) if h % 2 == 1 else None))
            flush_heads()
        for s in range(2):
            p0 = 2048 + 256 * s
            ld(A["KW"][:, :, 0:256], PTs[512:1024, p0:p0 + 256].rearrange("(c p) t -> p c t", p=128), ["KW"], pts_reads(range(4, 8), p0, p0 + 256), "sp", 0)
            ld(A["VW"][:, 0:2, :], VT[p0:p0 + 256, 0:512].rearrange("(b p) c -> p b c", p=128), ["VW"], vt_reads("na_v", p0, p0 + 256), "sp", 0)
            for qt in range(2):
                tok0 = p0 + qt * 128
                qi = rot("QT", 2)
                QT = A["QT"][qi]
                ld(QT[:], PTs[0:512, tok0:tok0 + 128].rearrange("(c p) t -> p c t", p=128), ["QT"], pts_reads(range(0, 4), tok0, tok0 + 128), "sp", 0)
                for h in range(8):
                    chn, r0 = h // 2, (h % 2) * 64
                    segs = [(A["KW"][r0:r0 + 64, chn, 0:256], 256, None, [(128, A["VW"][:, b, h * 64:(h + 1) * 64]) for b in range(2)])]
                    oi = (h // 2) % 2
                    plain_head(QT[r0:r0 + 64, chn, :], segs, None, h % 2, oi, ["VW"], ["KW"],
                               after=((lambda oi=oi, chn=chn, tok0=tok0: put_mix(oi, chn, tok0)) if h % 2 == 1 else None))
                flush_heads()

        KS = A["KW"]

        def load_swa_k(g, dstcols, src_rows_ap, reads):
            for half in range(2):
                ld(KS[half * 64:(half + 1) * 64, g, dstcols[0]:dstcols[1]], src_rows_ap, ["KW"], reads, "sp", 0)

        for g in range(2):
            for half in range(2):
                ld(A["cK"][half * 64:(half + 1) * 64, g, :], cswaKT[l, g * 64:(g + 1) * 64, :], ["cK"], (), "pool", 2)
        ld(A["cV"][:, :, 0:128], cswaV[l].rearrange("(b p) c -> p b c", p=128), ["cV"], (), "pool", 2)
        for jq in range(16):
            tok0 = jq * 128
            ws = min(max(128 * (jq - 1), 0), 2048 - 384)
            pat = 0 if jq == 0 else (2 if jq == 15 else 1)
            qi = rot("QT", 2)
            QT = A["QT"][qi]
            ld(QT[:], PTs[24 * 128:28 * 128, tok0:tok0 + 128].rearrange("(c p) t -> p c t", p=128), ["QT"], pts_reads(range(24, 28), tok0, tok0 + 128), "sp", 0)
            for g in range(2):
                load_swa_k(g, (0, 384), PTs[28 * 128 + g * 64:28 * 128 + (g + 1) * 64, ws:ws + 384], pts_reads([28], ws, ws + 384))
            ld(A["VW"][:, 0:3, 0:128], VT[ws:ws + 384, 512:640].rearrange("(b p) c -> p b c", p=128), ["VW"], vt_reads("sv", ws, ws + 384), "sp", 0)
            for h in range(8):
                g, chn, r0 = h // 4, h // 2, (h % 2) * 64
                segs = [(KS[r0:r0 + 64, g, 0:384], 384, swamask[:, pat, :], [(128, A["VW"][:, b, g * 64:(g + 1) * 64]) for b in range(3)]),
                        (A["cK"][r0:r0 + 64, g, :], 256, None, [(128, A["cV"][:, b, g * 64:(g + 1) * 64]) for b in range(2)])]
                oi = (h // 2) % 2
                plain_head(QT[r0:r0 + 64, chn, :], segs, sinkT[:, l, h:h + 1], h % 2, oi, ["VW", "cV"], ["KW", "cK", "swamask"],
                           after=((lambda oi=oi, chn=chn, tok0=tok0: put_mix(oi, 8 + chn, tok0)) if h % 2 == 1 else None))
            flush_heads()
        for s in range(2):
            p0 = 2048 + 256 * s
            for g in range(2):
                load_swa_k(g, (0, 256), PTs[28 * 128 + g * 64:28 * 128 + (g + 1) * 64, p0:p0 + 256], pts_reads([28], p0, p0 + 256))
            ld(A["VW"][:, 0:2, 0:128], VT[p0:p0 + 256, 512:640].rearrange("(b p) c -> p b c", p=128), ["VW"], vt_reads("sv", p0, p0 + 256), "sp", 0)
            for qt in range(2):
                tok0 = p0 + qt * 128
                qi = rot("QT", 2)
                QT = A["QT"][qi]
                ld(QT[:], PTs[24 * 128:28 * 128, tok0:tok0 + 128].rearrange("(c p) t -> p c t", p=128), ["QT"], pts_reads(range(24, 28), tok0, tok0 + 128), "sp", 0)
                for h in range(8):
                    g, chn, r0 = h // 4, h // 2, (h % 2) * 64
                    segs = [(KS[r0:r0 + 64, g, 0:256], 256, None, [(128, A["VW"][:, b, g * 64:(g + 1) * 64]) for b in range(2)])]
                    oi = (h // 2) % 2
                    plain_head(QT[r0:r0 + 64, chn, :], segs, sinkT[:, l, h:h + 1], h % 2, oi, ["VW"], ["KW"],
                               after=((lambda oi=oi, chn=chn, tok0=tok0: put_mix(oi, 8 + chn, tok0)) if h % 2 == 1 else None))
                flush_heads()

        for h in range(4):
            KT, VD = A["KT"], A["VD"]
            ld(KT[:, 0:2048], PTs[(34 + h) * 128:(35 + h) * 128, 0:2048], ["KT"], pts_reads([34 + h], 0, 2048), "sp", 3)
            ld(KT[:, 2048:2304], cdiffKT[l, h], ["KT"], (), "pool", 2)
            ld(VD[:, 0:16, :], VT[0:2048, 640 + h * 128:640 + (h + 1) * 128].rearrange("(b p) c -> p b c", p=128), ["VD"], vt_reads("dv", 0, 2048), "sp", 3)
            ld(VD[:, 16:18, :], cdiffV[l, :, h * 128:(h + 1) * 128].rearrange("(b p) c -> p b c", p=128), ["VD"], (), "pool", 2)
            for i in range(16):
                tok0 = i * 128
                qi = rot("QT", 2)
                QT = A["QT"][qi]
                ld(QT[:, 0, :], PTs[(30 + h) * 128:(31 + h) * 128, tok0:tok0 + 128], ["QT"], pts_reads([30 + h], tok0, tok0 + 128), "sp", 0)
                segs = [(KT[:, :], 2304, [(128, VD[:, b, :]) for b in range(18)])]
                oi = rot("otokd", 2)
                diff_head(QT[:, 0, :], segs, oi, ["VD"], ["KT"])
                put_mix(oi, 12 + h, tok0)
            for s in range(2):
                p0 = 2048 + 256 * s
                ld(KT[:, 0:256], PTs[(34 + h) * 128:(35 + h) * 128, p0:p0 + 256], ["KT"], pts_reads([34 + h], p0, p0 + 256), "sp", 3)
                ld(VD[:, 0:2, :], VT[p0:p0 + 256, 640 + h * 128:640 + (h + 1) * 128].rearrange("(b p) c -> p b c", p=128), ["VD"], vt_reads("dv", p0, p0 + 256), "sp", 3)
                for qt in range(2):
                    tok0 = p0 + qt * 128
                    qi = rot("QT", 2)
                    QT = A["QT"][qi]
                    ld(QT[:, 0, :], PTs[(30 + h) * 128:(31 + h) * 128, tok0:tok0 + 128], ["QT"], pts_reads([30 + h], tok0, tok0 + 128), "sp", 0)
                    segs = [(KT[:, 0:256], 256, [(128, VD[:, b, :]) for b in range(2)])]
                    oi = rot("otokd", 2)
                    diff_head(QT[:, 0, :], segs, oi, ["VD"], ["KT"])
                    put_mix(oi, 12 + h, tok0)

        if STOP < 4:
            break
        fence("s3")
        alloc_work([("x", [128, 16, TS], F32, 1), ("y", [128, 16, 256], F32, 1), ("sq", [128, TS], BF16, 2), ("rs", [128, TS], F32, 1),
                    ("tmp", [128, TS], F32, 2)])

        def branch_update(l, t, half, xt, y, kc, po):
            g = 1 if t < 4 else 0
            rs = wk["rs"]
            rstd_from_psum(po, rs[:, 0:256], 1.0 / D)
            for c in range(16):
                ti = rot("tmp", 2)
                tmp = wk["tmp"][ti]
                S.op("dve", lambda e, c=c, tmp=tmp: e.scalar_tensor_tensor(out=tmp[:, 0:256], in0=y[:, c, :], scalar=coef[:, l, kc, c, g:g + 1], in1=rs[:, 0:256],
                                                                          op0=ALU.mult, op1=ALU.mult), reads=["y", "rs", ("coef", l)], writes=[("tmp", ti)])
                S.op("pool", lambda e, c=c, tmp=tmp: e.tensor_tensor(out=xt[:, c, half * 256:(half + 1) * 256], in0=xt[:, c, half * 256:(half + 1) * 256],
                                                                    in1=tmp[:, 0:256], op=ALU.add), reads=[("tmp", ti), "xt"], writes=["xt"])

        for t in range(NT):
            xt = wk["x"]
            y = wk["y"]
            S.op("sp", lambda e, t=t, xt=xt: e.dma_start(out=xt[:], in_=xview(src_x, t)), reads=[("XS", t)], writes=["xt"], chan=ch_x)
            for half in range(2):
                c0 = t * TS + half * 256
                po = rot("psO", 2)
                for wt in range(4):
                    wv, wr = load_w(w_out[l, :, wt * 512:(wt + 1) * 512], 16, 512)
                    for j in range(4):
                        m = wt * 4 + j
                        pi = rot("ps", 4)
                        for kc in range(16):
                            S.op("pe", lambda e, j=j, kc=kc, pi=pi, wv=wv, c0=c0: e.matmul(ps[pi][:, 0:256], lhsT=wv[:, kc, j * 128:(j + 1) * 128],
                                                                                         rhs=mixT[:, kc, c0:c0 + 256], start=(kc == 0), stop=(kc == 15)),
                                 reads=[wr, ("hall", t)], writes=[("ps", pi)])
                        S.op("dve", lambda e, m=m, pi=pi: e.tensor_copy(out=y[:, m, :], in_=ps[pi][:, 0:256]), reads=[("ps", pi)], writes=["y"])
                        si = rot("sq", 2)
                        sq = wk["sq"][si]
                        S.op("act", lambda e, m=m, sq=sq: e.activation(out=sq[:, 0:256], in_=y[:, m, :], func=AF.Square), reads=["y"], writes=[("sq", si)])
                        S.op("pe", lambda e, m=m, sq=sq, po=po: e.matmul(psO[po][:, 0:256], lhsT=ones[:], rhs=sq[:, 0:256], start=(m == 0), stop=(m == 15)),
                             reads=[("sq", si), "ones"], writes=[("psO", po)])
                branch_update(l, t, half, xt, y, 2, po)
            S.op("sp", lambda e, t=t, xt=xt: e.dma_start(out=xview(XS, t), in_=xt[:]), reads=["xt"], writes=[("XS", t)], chan=ch_xo)
            premix_tile(l, t, xt, 1, "xt")

        if STOP < 5:
            break
        alloc_work([("r", [128, TS], F32, 2), ("astg", [128, TS], BF16, 4)])
        fence("s4")
        for wt in range(16):
            wv, wr = load_w(w1[l, :, wt * 512:(wt + 1) * 512], 16, 512)
            for t in range(NT):
                for j in range(4):
                    m = wt * 4 + j
                    pi = rot("ps", 4)
                    for kc in range(16):
                        S.op("pe", lambda e, j=j, kc=kc, pi=pi, t=t, wv=wv: e.matmul(ps[pi][:], lhsT=wv[:, kc, j * 128:(j + 1) * 128],
                                                                                   rhs=hall[:, kc, t * TS:(t + 1) * TS], start=(kc == 0), stop=(kc == 15)),
                             reads=[wr, ("hall", t)], writes=[("ps", pi)])
                    ri = rot("r", 2)
                    r = wk["r"][ri]
                    ai = rot("astg", 4)
                    a = wk["astg"][ai]
                    S.op("act", lambda e, pi=pi, r=r: e.activation(out=r[:], in_=ps[pi][:], func=AF.Relu), reads=[("ps", pi)], writes=[("r", ri)])
                    S.op("dve", lambda e, r=r, a=a: e.tensor_tensor(out=a[:], in0=r[:], in1=r[:], op=ALU.mult), reads=[("r", ri)], writes=[("astg", ai)])
                    S.op("sp", lambda e, m=m, t=t, a=a: e.dma_start(out=AH[m * 128:(m + 1) * 128, t * TS:(t + 1) * TS], in_=a[:]),
                         reads=[("astg", ai)], writes=[("AH", m, t)], chan=ch_stg[ai])

        if STOP < 6:
            break
        fence("s5")
        alloc_work([("y", [128, 16, TS], F32, 1), ("sq", [128, TS], BF16, 2), ("rs", [128, TS], F32, 1),
                    ("tmp", [128, TS], F32, 2), ("xc", [128, TS], F32, 3)])
        at_flat = hall[:].rearrange("p a b -> p (a b)")
        last = (l == depth - 1)
        for t in range(NT):
            y = wk["y"]
            g = 1 if t < 4 else 0
            a_t = at_flat[:, 0:64 * TS].rearrange("p (a b) -> p a b", a=64)
            S.op("sp", lambda e, t=t, a_t=a_t: e.dma_start(out=a_t, in_=AH[:, t * TS:(t + 1) * TS].rearrange("(a p) t -> p a t", p=128)),
                 reads=[("AH", m, t) for m in range(64)], writes=[("hall", k) for k in range(NT)], chan=ch_ld[4])
            po = rot("psO", 2)
            for m in range(16):
                i = rot("wb", 2)
                buf = WB[i]
                w2f = buf[:].rearrange("p a b -> p (a b)")[:, 0:64 * 128]
                w2v = w2f.rearrange("p (a b) -> p a b", a=64)
                S.op("pool", lambda e, m=m, w2f=w2f: e.dma_start(out=w2f, in_=w2[l, m]),
                     writes=[("wb", i)], chan=ch_w[i])
                pi = rot("ps", 4)
                for kc in range(64):
                    S.op("pe", lambda e, kc=kc, pi=pi, w2v=w2v, a_t=a_t: e.matmul(ps[pi][:], lhsT=w2v[:, kc, :], rhs=a_t[:, kc, :],
                                                                                start=(kc == 0), stop=(kc == 63)),
                         reads=[("wb", i), ("hall", 0)], writes=[("ps", pi)])
                S.op("dve", lambda e, m=m, pi=pi: e.tensor_copy(out=y[:, m, :], in_=ps[pi][:]), reads=[("ps", pi)], writes=["y"])
                si = rot("sq", 2)
                sq = wk["sq"][si]
                S.op("act", lambda e, m=m, sq=sq: e.activation(out=sq[:], in_=y[:, m, :], func=AF.Square), reads=["y"], writes=[("sq", si)])
                S.op("pe", lambda e, m=m, sq=sq, po=po: e.matmul(psO[po][:], lhsT=ones[:], rhs=sq[:], start=(m == 0), stop=(m == 15)),
                     reads=[("sq", si), "ones"], writes=[("psO", po)])
            rs = wk["rs"]
            rstd_from_psum(po, rs[:], 1.0 / D)
            for c in range(16):
                xi = rot("xc", 3)
                xc = wk["xc"][xi]
                S.op("sp", lambda e, c=c, t=t, xc=xc: e.dma_start(out=xc[:], in_=XS[c * 128:(c + 1) * 128, t * TS:(t + 1) * TS]),
                     reads=[("XS", t)], writes=[("xc", xi)], chan=ch_ld[5])
                ti = rot("tmp", 2)
                tmp = wk["tmp"][ti]
                S.op("dve", lambda e, c=c, tmp=tmp, g=g: e.scalar_tensor_tensor(out=tmp[:], in0=y[:, c, :], scalar=coef[:, l, 5, c, g:g + 1], in1=rs[:],
                                                                           op0=ALU.mult, op1=ALU.mult), reads=["y", "rs", ("coef", l)], writes=[("tmp", ti)])
                S.op("pool", lambda e, xc=xc, tmp=tmp: e.tensor_tensor(out=xc[:], in0=xc[:], in1=tmp[:], op=ALU.add),
                     reads=[("tmp", ti), ("xc", xi)], writes=[("xc", xi)])
                if last:
                    S.op("sp", lambda e, c=c, t=t, xc=xc: e.dma_start(out=yT[c * 128:(c + 1) * 128, t * TS:(t + 1) * TS], in_=xc[:]),
                         reads=[("xc", xi)], chan=ch_out)
                else:
                    S.op("sp", lambda e, c=c, t=t, xc=xc: e.dma_start(out=XS[c * 128:(c + 1) * 128, t * TS:(t + 1) * TS], in_=xc[:]),
                         reads=[("xc", xi)], writes=[("XSo", t, c)], chan=ch_xo)
        if not last:
            for t in range(NT):
                S.op("dve", lambda e: e.memset(small[:, 62:63], 0.0), reads=[("XSo", t, c) for c in range(16)], writes=[("XS", t)])

    cnt = S.finalize(final_waits=STP)
    S.emit()
    return nc, len(S.ops), cnt


def _na_bias_tables(rpb):
    out = np.empty((rpb.shape[0], 5, 8, 128, 576), np.float32)
    for pi, i in enumerate((0, 1, 2, 14, 15)):
        ks = na_ks(i)
        q = np.arange(128)
        r = 2 * i + q // 64
        c = q % 64
        k = np.arange(576)
        kr = ks + k // 64
        kc = k % 64
        rs = np.clip(r - 4, 0, 24)
        row_ok = (kr[None, :] >= rs[:, None]) & (kr[None, :] < rs[:, None] + 8)
        cs = np.clip(c - 8, 0, 48)
        col_ok = (kc[None, :] >= cs[:, None]) & (kc[None, :] < cs[:, None] + 16)
        ok = row_ok & col_ok
        dr = np.clip(kr[None, :] - r[:, None] + 7, 0, 14)
        dc = np.clip(kc[None, :] - c[:, None], -15, 15) + 15
        g = rpb[:, :, dr, dc]
        out[:, pi] = np.where(ok[None, None], g, np.float32(NEG))
    return out


def _rope_tables():
    t = np.arange(2048)
    rows = (t // 64).astype(np.float32)
    cols = (t % 64).astype(np.float32)
    n = 16
    inv = (np.float32(10000.0) ** (-np.arange(n, dtype=np.float32) / n)).astype(np.float32)
    cosT = np.zeros((128, 2048), np.float32)
    sinT = np.zeros((128, 2048), np.float32)
    for p in range(128):
        d = p % 64
        pos = rows if d < 32 else cols
        j = d % 16
        ang = pos * inv[j]
        cosT[p] = np.cos(ang)
        sinT[p] = -np.sin(ang) if (d % 32) < 16 else np.sin(ang)
    perm = np.zeros((128, 128), np.float32)
    for m in range(128):
        partner = m + 16 if (m % 32) < 16 else m - 16
        perm[partner, m] = 1.0
    return cosT, sinT, perm


def _swa_masks():
    out = np.zeros((128, 3, 384), np.float32)
    for pat, jq in enumerate((0, 5, 15)):
        ws = min(max(128 * (jq - 1), 0), 2048 - 384)
        q = jq * 128 + np.arange(128)
        k = ws + np.arange(384)
        ok = np.abs(q[:, None] - k[None, :]) <= 128
        out[:, pat, :] = np.where(ok, 0.0, NEG)
    return out


_CACHE = {}


def kernel(x_prompt, x_sample, cache_na_kv, cache_swa_kv, cache_diff_kv, c, c_ctx,
           ada_w, ada_b, norm_g, w_in, conv_w, na_rpb, swa_sink, diff_lambda, diff_norm_g,
           w_out, mlp_w1, mlp_w2, _depth=DEPTH):
    f32 = np.float32
    A = lambda a: np.ascontiguousarray(np.asarray(a, dtype=f32))
    x_prompt, x_sample = A(x_prompt), A(x_sample)
    cache_na_kv, cache_swa_kv, cache_diff_kv = A(cache_na_kv), A(cache_swa_kv), A(cache_diff_kv)
    c, c_ctx = A(c), A(c_ctx)
    ada_w, ada_b, norm_g, w_in, conv_w = A(ada_w), A(ada_b), A(norm_g), A(w_in), A(conv_w)
    na_rpb, swa_sink, diff_lambda, diff_norm_g = A(na_rpb), A(swa_sink), A(diff_lambda), A(diff_norm_g)
    w_out, mlp_w1, mlp_w2 = A(w_out), A(mlp_w1), A(mlp_w2)

    if _depth not in _CACHE:
        _CACHE[_depth] = build_program(_depth)[0]
    nc = _CACHE[_depth]

    cosT, sinT, perm = _rope_tables()
    shared = {
        "ada_w": A(ada_w[:_depth]),
        "ada_bT": A(ada_b.reshape(DEPTH, 96, 128).transpose(2, 0, 1)),
        "normgT": A(norm_g.reshape(DEPTH, 4, 16, 128).transpose(3, 0, 1, 2)),
        "w_in": A(w_in[:_depth]), "w_out": A(w_out[:_depth]), "w1": A(mlp_w1[:_depth]), "w2": A(mlp_w2[:_depth].reshape(_depth, 64, 128, 16, 128).transpose(0, 3, 2, 1, 4).reshape(_depth, 16, 128, 64 * 128)),
        "convwT": A(conv_w.reshape(DEPTH, 3, 4, 128).transpose(3, 0, 2, 1)),
        "nab": A(_na_bias_tables(na_rpb)[:_depth]),
        "sinkB": A(np.broadcast_to(swa_sink[None], (128, DEPTH, 8))),
        "lamP": A(np.broadcast_to(diff_lambda[None], (128, DEPTH, 4, 64))),
        "dgB": A(np.broadcast_to(diff_norm_g[None], (128, DEPTH, 128))),
        "cosT": cosT, "sinT": sinT, "perm": perm, "ident": np.eye(128, dtype=f32),
        "swamask": _swa_masks(),
    }
    in_maps = []
    for core in range(8):
        b = core // 4
        s0, s1 = 2 * core, 2 * core + 1
        m = dict(shared)
        m["xT"] = A(np.concatenate([x_sample[b].T, x_prompt[s0].T, x_prompt[s1].T], axis=1))
        cv = np.stack([c_ctx, c[b]], axis=-1)
        m["cvec"] = A(cv.reshape(16, 128, 2).transpose(1, 0, 2))
        na = cache_na_kv[b]
        m["cnaKT"] = A(na[:, 0].reshape(DEPTH, 256, 4, 128).transpose(0, 2, 3, 1))
        m["cnaV"] = A(na[:, 1].reshape(DEPTH, 256, 512))
        sw = cache_swa_kv[b]
        m["cswaKT"] = A(sw[:, 0].reshape(DEPTH, 256, 128).transpose(0, 2, 1))
        m["cswaV"] = A(sw[:, 1].reshape(DEPTH, 256, 128))
        df = cache_diff_kv[b]
        m["cdiffKT"] = A(df[:, 0].transpose(0, 2, 3, 1))
        m["cdiffV"] = A(df[:, 1].reshape(DEPTH, 256, 512))
        in_maps.append(m)

    res = run_bass_kernel_spmd(nc, in_maps, core_ids=list(range(8)))
    R = res.results
    yp = np.empty((16, 256, D), f32)
    ys = np.empty((2, 2048, D), f32)
    nna = np.empty((16, DEPTH, 2, 256, 8, 64), f32)
    nsw = np.empty((16, DEPTH, 2, 256, 2, 64), f32)
    ndf = np.empty((16, DEPTH, 2, 256, 4, 128), f32)
    for core in range(8):
        yt = np.asarray(R[core]["yT"])
        if core % 4 == 0:
            ys[core // 4] = yt[:, 0:2048].T
        for s in range(2):
            yp[2 * core + s] = yt[:, 2048 + 256 * s:2048 + 256 * (s + 1)].T
            nna[2 * core + s] = np.asarray(R[core]["ona"])[s].reshape(DEPTH, 2, 256, 8, 64)
            nsw[2 * core + s] = np.asarray(R[core]["oswa"])[s].reshape(DEPTH, 2, 256, 2, 64)
            ndf[2 * core + s] = np.asarray(R[core]["odiff"])[s].reshape(DEPTH, 2, 256, 4, 128)
    return (yp, ys, nna, nsw, ndf)
```

```python
import math
import numpy as np
import concourse.bass as bass
import concourse.mybir as mybir
from concourse.bass_utils import run_bass_kernel_spmd

F32 = mybir.dt.float32
BF16 = mybir.dt.bfloat16
AF = mybir.ActivationFunctionType
ALU = mybir.AluOpType
AX = mybir.AxisListType

D = 2048
DEPTH = 4
NTOK = 2560
TS = 512
NT = 5
HID = 8192
INC = 5376
SCALE = 64 ** -0.5
EPS = 1e-6
NEG = -30000.0


class Chan:
    def __init__(self, nc, name, inc=16):
        self.sem = nc.alloc_semaphore(name)
        self.count = 0
        self.inc = inc
        self.last_op = None


class Op:
    __slots__ = ("eng", "fn", "deps", "signal", "chan", "val", "is_dma", "waits")


class _Rec:
    def __getattr__(self, name):
        def f(*a, **k):
            self.__dict__["call"] = (name, a, k)
            return self
        return f


class Sched:
    ENGS = ("pe", "act", "dve", "pool", "sp")

    def __init__(self, nc):
        self.nc = nc
        self.ops = []
        self.last_w = {}
        self.readers = {}
        self.esem = {e: nc.alloc_semaphore("prog_" + e) for e in self.ENGS}
        self.nchan = 0
        self.fence_idx = None

    def chan(self, name=None, inc=16):
        self.nchan += 1
        return Chan(self.nc, name or ("ch%d" % self.nchan), inc)

    def op(self, eng, fn, reads=(), writes=(), chan=None):
        o = Op()
        o.eng = eng
        rec = _Rec()
        fn(rec)
        o.fn = rec.__dict__["call"]
        o.chan = chan
        o.is_dma = chan is not None
        o.signal = o.is_dma
        o.val = None
        deps = set()
        for r in reads:
            w = self.last_w.get(r)
            if w is not None:
                deps.add(w)
        for r in writes:
            w = self.last_w.get(r)
            if w is not None:
                deps.add(w)
            for rd in self.readers.get(r, {}).values():
                if isinstance(rd, list):
                    deps.update(rd)
                else:
                    deps.add(rd)
        i = len(self.ops)
        if self.fence_idx is not None:
            deps.add(self.fence_idx)
        if chan is not None:
            if callable(chan):
                chan = chan()
                o.chan = chan
            if chan.last_op is not None:
                deps.add(chan.last_op)
            chan.last_op = i
        o.deps = deps
        self.ops.append(o)
        for r in reads:
            d = self.readers.setdefault(r, {})
            if o.is_dma:
                d.setdefault("dma", []).append(i)
            else:
                d[eng] = i
        for r in writes:
            self.last_w[r] = i
            self.readers[r] = {}
        if o.is_dma:
            chan.count += chan.inc
            o.val = chan.count
        return i

    def fence(self, fn):
        names = list(dict.fromkeys(list(self.last_w.keys()) + list(self.readers.keys())))
        self.fence_idx = self.op("dve", fn, writes=names)

    def finalize(self, final_waits=()):
        ops = self.ops
        for o in ops:
            for d in o.deps:
                p = ops[d]
                if p.is_dma:
                    continue
                if p.eng == "pe" and o.eng == "pe":
                    continue
                p.signal = True
        cnt = {e: 0 for e in self.ENGS}
        for o in ops:
            if o.signal and not o.is_dma:
                cnt[o.eng] += 1
                o.val = cnt[o.eng]
        seen = {e: {} for e in self.ENGS}
        chan_count_at = {}
        for i, o in enumerate(ops):
            waits = {}
            for d in o.deps:
                p = ops[d]
                if p.is_dma:
                    sem = p.chan.sem
                    v = chan_count_at[id(p.chan)]
                    key = ("c", id(p.chan))
                else:
                    if p.eng == "pe" and o.eng == "pe":
                        continue
                    sem = self.esem[p.eng]
                    v = p.val
                    key = ("e", p.eng)
                if seen[o.eng].get(key, 0) >= v:
                    continue
                if key not in waits or waits[key][1] < v:
                    waits[key] = (sem, v)
            for key, (sem, v) in waits.items():
                seen[o.eng][key] = v
            o.waits = list(waits.values())
            if o.is_dma:
                chan_count_at[id(o.chan)] = o.val
        self.final = [(c.sem, c.count) for c in final_waits]
        return cnt

    def emit(self):
        nc = self.nc
        per = {e: [o for o in self.ops if o.eng == e] for e in self.ENGS}
        esem = self.esem
        final = self.final

        def run(eng_name, eng):
            for o in per[eng_name]:
                for sem, v in o.waits:
                    eng.wait_ge(sem, v)
                name, a, k = o.fn
                ins = getattr(eng, name)(*a, **k)
                if o.is_dma:
                    ins.then_inc(o.chan.sem, o.chan.inc)
                elif o.signal:
                    ins.then_inc(esem[eng_name], 1)

        with nc.Block() as block:
            @block.sync
            def _(e):
                run("sp", e)
                for sem, v in final:
                    e.wait_ge(sem, v)

            @block.scalar
            def _(e):
                run("act", e)

            @block.vector
            def _(e):
                run("dve", e)

            @block.gpsimd
            def _(e):
                run("pool", e)

            @block.tensor
            def _(e):
                run("pe", e)


def chunk_type(m):
    if m < 4: return "na_q"
    if m < 8: return "na_k"
    if m < 12: return "na_v"
    if m < 16: return "u"
    if m < 20: return "gb"
    if m < 24: return "gc"
    if m < 28: return "sq"
    if m == 28: return "sk"
    if m == 29: return "sv"
    if m < 34: return "dq"
    if m < 38: return "dk"
    return "dv"


ROPE_T = ("sq", "sk", "dq", "dk")
VCOL = {"na_v": 0, "sv": 512, "dv": 640}
VBASE = {"na_v": 8, "sv": 29, "dv": 38}
KBASE = {"na_k": 4, "sk": 28, "dk": 34}


def na_pat(i):
    return {0: 0, 1: 1, 14: 3, 15: 4}.get(i, 2)


def na_ks(i):
    return min(max(2 * i - 4, 0), 23)


def build_program(depth=DEPTH):
    nc = bass.Bass("TRN2", target_bir_lowering=False)

    def din(name, shape, dt=F32):
        return nc.dram_tensor(name, list(shape), dt, kind="ExternalInput").ap()

    def dout(name, shape, dt=F32):
        return nc.dram_tensor(name, list(shape), dt, kind="ExternalOutput").ap()

    def dscr(name, shape, dt):
        return nc.dram_tensor(name, list(shape), dt).ap()

    xT_in = din("xT", [D, NTOK])
    cvec = din("cvec", [128, 16, 2])
    ada_w = din("ada_w", [depth, 24, 128, 16 * 512])
    ada_bT = din("ada_bT", [128, DEPTH, 96])
    normgT = din("normgT", [128, DEPTH, 4, 16])
    w_in = din("w_in", [depth, 21, 128, 16 * 256])
    w_out = din("w_out", [depth, 4, 128, 16 * 512])
    w1 = din("w1", [depth, 16, 128, 16 * 512])
    w2 = din("w2", [depth, 16, 128, 64 * 128])
    convwT = din("convwT", [128, DEPTH, 4, 3])
    nab = din("nab", [depth, 5, 8, 128, 576])
    sinkB = din("sinkB", [128, DEPTH, 8])
    lamP = din("lamP", [128, DEPTH, 4, 64])
    dgB = din("dgB", [128, DEPTH, 128])
    cnaKT = din("cnaKT", [DEPTH, 4, 128, 256])
    cnaV = din("cnaV", [DEPTH, 256, 512])
    cswaKT = din("cswaKT", [DEPTH, 128, 256])
    cswaV = din("cswaV", [DEPTH, 256, 128])
    cdiffKT = din("cdiffKT", [DEPTH, 4, 128, 256])
    cdiffV = din("cdiffV", [DEPTH, 256, 512])
    cosT_in = din("cosT", [128, 2048])
    sinT_in = din("sinT", [128, 2048])
    perm_in = din("perm", [128, 128])
    ident_in = din("ident", [128, 128])
    swamask_in = din("swamask", [128, 3, 384])

    yT = dout("yT", [D, NTOK])
    ona = dout("ona", [2, DEPTH, 2, 256, 512])
    oswa = dout("oswa", [2, DEPTH, 2, 256, 128])
    odiff = dout("odiff", [2, DEPTH, 2, 256, 512])

    XS = dscr("XS", [D, NTOK], F32)
    PTs = dscr("PTs", [42 * 128, NTOK], BF16)
    VT = dscr("VT", [NTOK, 1152], BF16)
    AH = dscr("AH", [HID, NTOK], BF16)

    BASE = 16512
    LIMIT = BASE + 212000
    cur = [BASE]

    def sb(name, shape, dt, at=None):
        esz = 4 if dt == F32 else 2
        n = esz
        for s in shape[1:]:
            n *= s
        n = (n + 31) // 32 * 32
        if at is None:
            off = cur[0]
            cur[0] += n
            assert cur[0] <= LIMIT, (name, cur[0])
        else:
            off = at[0]
            at[0] += n
            assert at[0] <= at[1], (name, at[0], at[1])
        return nc.alloc_sbuf_tensor_at(name, list(shape), dt, offset=off)

    ident = sb("ident", [128, 128], BF16)
    perm = sb("perm", [128, 128], BF16)
    ones = sb("ones", [128, 128], BF16)
    cosT = sb("cosT", [128, 2048], BF16)
    sinT = sb("sinT", [128, 2048], BF16)
    modT = sb("modT", [128, DEPTH, 96, 2], F32)
    coef = sb("coef", [128, DEPTH, 6, 16, 2], F32)
    normg = sb("normg", [128, DEPTH, 4, 16], F32)
    adab = sb("adab", [128, DEPTH, 96], F32)
    cv32 = sb("cv32", [128, 16, 2], F32)
    cvb = sb("cvb", [128, 16, 2], BF16)
    convw = sb("convw", [128, DEPTH, 4, 3], F32)
    sinkT = sb("sinkT", [128, DEPTH, 8], F32)
    lamp = sb("lamp", [128, DEPTH, 4, 64], F32)
    lamv = sb("lamv", [128, DEPTH, 4], F32)
    dgs = sb("dgs", [128, DEPTH, 128], F32)
    swamask = sb("swamask", [128, 3, 384], F32)
    epsT = sb("epsT", [128, 1], F32)
    small = sb("small", [128, 64], F32)
    hall = sb("hall", [128, 16, NTOK], BF16)
    WB = [sb("wb%d" % i, [128, 16, 512], BF16) for i in range(2)]
    wbase = WB[0]
    WORK0 = cur[0]
    WORK_END = LIMIT
    ATT0 = WORK0 - 2 * 16 * 512 * 2

    ps = [nc.alloc_psum_tensor("ps%d" % i, [128, 512], F32) for i in range(4)]
    psT = [nc.alloc_psum_tensor("psT%d" % i, [128, 1024], BF16) for i in range(2)]
    psO = [nc.alloc_psum_tensor("psO%d" % i, [128, 512], F32) for i in range(2)]

    S = Sched(nc)
    rotc = {}

    def rot(name, n):
        v = rotc.get(name, 0)
        rotc[name] = v + 1
        return v % n

    ch_w = [S.chan("w0"), S.chan("w1")]
    LDP = {"sp": [S.chan("spld%d" % i) for i in range(12)], "pool": [S.chan("plld%d" % i) for i in range(6)]}
    STP = [S.chan("spst%d" % i) for i in range(12)]

    class _Pick:
        def __init__(self, lst, key):
            self.lst, self.key = lst, key

        def __call__(self):
            return self.lst[rot(self.key, len(self.lst))]

    ld_sp = _Pick(LDP["sp"], "ldsp")
    ld_pl = _Pick(LDP["pool"], "ldpl")
    st_sp = _Pick(STP, "stsp")
    ch_in = ld_sp
    ch_x = ld_sp
    ch_xo = st_sp
    ch_stg = [st_sp] * 4
    ch_out = st_sp
    ch_ld = [ld_sp] * 8

    def ld_const(dst, src, eng="sp", name=None):
        S.op(eng, lambda e: e.dma_start(out=dst, in_=src), writes=[name], chan=(ld_sp if eng == "sp" else ld_pl))

    ld_const(ident[:], ident_in, "pool", "ident")
    ld_const(perm[:], perm_in, "pool", "perm")
    ld_const(cosT[:], cosT_in, "pool", "cosT")
    ld_const(sinT[:], sinT_in, "pool", "sinT")
    ld_const(adab[:], ada_bT, "sp", "adab")
    ld_const(normg[:], normgT, "sp", "normg")
    ld_const(cv32[:], cvec, "sp", "cv32")
    ld_const(convw[:], convwT, "sp", "convw")
    ld_const(sinkT[:], sinkB, "sp", "sinkT")
    ld_const(lamp[:], lamP, "sp", "lamp")
    ld_const(dgs[:], dgB, "sp", "dgs")
    ld_const(swamask[:], swamask_in, "sp", "swamask")
    S.op("dve", lambda e: e.memset(ones[:], 1.0), writes=["ones"])
    S.op("dve", lambda e: e.memset(epsT[:], EPS), writes=["epsT"])
    S.op("act", lambda e: e.activation(out=cvb[:], in_=cv32[:], func=AF.Silu), reads=["cv32"], writes=["cvb"])

    import os as _os
    STOP = int(_os.environ.get("KSTOP", "99"))
    lam_inits = [0.8 - 0.6 * math.exp(-0.3 * l) for l in range(DEPTH)]
    lprod = sb("lprod", [128, 2, 64], F32)
    for l in range(depth if STOP >= -1 else 0):
        for j in range(2):
            S.op("dve", lambda e, l=l, j=j: e.tensor_tensor(out=lprod[:, j, :], in0=lamp[:, l, 2 * j, :],
                                                            in1=lamp[:, l, 2 * j + 1, :], op=ALU.mult),
                 reads=["lamp"], writes=[("lprod", j)])
            S.op("dve", lambda e, l=l, j=j: e.tensor_reduce(out=lamv[:, l, j:j + 1], in_=lprod[:, j, :], axis=AX.X, op=ALU.add),
                 reads=[("lprod", j)], writes=[("lamv", l, j)])
            S.op("act", lambda e, l=l, j=j: e.activation(out=lamv[:, l, j:j + 1], in_=lamv[:, l, j:j + 1], func=AF.Exp),
                 reads=[("lamv", l, j)], writes=[("lamv", l, j)])
        S.op("dve", lambda e, l=l: e.tensor_tensor(out=lamv[:, l, 2:3], in0=lamv[:, l, 0:1], in1=lamv[:, l, 1:2], op=ALU.subtract),
             reads=[("lamv", l, 0), ("lamv", l, 1)], writes=[("lamv", l, 2)])
        S.op("dve", lambda e, l=l: e.tensor_scalar_add(out=lamv[:, l, 2:3], in0=lamv[:, l, 2:3], scalar1=float(lam_inits[l])),
             reads=[("lamv", l, 2)], writes=[("lamv", l, 2)])
        S.op("dve", lambda e, l=l: e.tensor_scalar_mul(out=dgs[:, l, :], in0=dgs[:, l, :], scalar1=float(1.0 - lam_inits[l])),
             reads=["dgs"], writes=["dgs"])

    def load_w(src_ap, kc, ncols):
        i = rot("wb", 2)
        buf = WB[i]
        flat = kc * ncols
        dflat = buf[:].rearrange("p a b -> p (a b)")[:, 0:flat]
        dst = dflat.rearrange("p (a b) -> p a b", a=kc)
        S.op("pool", lambda e: e.dma_start(out=dflat, in_=src_ap),
             writes=[("wb", i)], chan=ch_w[i])
        return dst, ("wb", i)

    for l in range(depth if STOP >= 0 else 0):
        for wt in range(24):
            wv, wr = load_w(ada_w[l, wt], 16, 512)
            for j in range(4 if _os.environ.get("KSUB") != "nomm" else 0):
                chn = wt * 4 + j
                pi = rot("ps", 4)
                for kc in range(16):
                    S.op("pe", lambda e, wv=wv, j=j, kc=kc, pi=pi: e.matmul(ps[pi][:, 0:2], lhsT=wv[:, kc, j * 128:(j + 1) * 128],
                                                                         rhs=cvb[:, kc, :], start=(kc == 0), stop=(kc == 15)),
                         reads=[wr, "cvb"], writes=[("ps", pi)])
                S.op("dve", lambda e, l=l, chn=chn, pi=pi: e.tensor_scalar_add(out=modT[:, l, chn, :], in0=ps[pi][:, 0:2],
                                                                            scalar1=adab[:, l, chn:chn + 1]),
                     reads=[("ps", pi), "adab"], writes=[("modT", l)])
        for g in range(2 if _os.environ.get("KSUB") not in ("nomm", "nocoef") else 0):
            def mt(i, l=l, g=g):
                return modT[:, l, i * 16:(i + 1) * 16, g]
            S.op("dve", lambda e, l=l, g=g, mt=mt: e.scalar_tensor_tensor(out=coef[:, l, 0, :, g], in0=mt(1), scalar=1.0, in1=normg[:, l, 0, :],
                                                                          op0=ALU.add, op1=ALU.mult), reads=[("modT", l), "normg"], writes=[("coef", l)])
            S.op("dve", lambda e, l=l, g=g, mt=mt: e.tensor_copy(out=coef[:, l, 1, :, g], in_=mt(0)), reads=[("modT", l)], writes=[("coef", l)])
            S.op("dve", lambda e, l=l, g=g, mt=mt: e.tensor_tensor(out=coef[:, l, 2, :, g], in0=mt(2), in1=normg[:, l, 1, :], op=ALU.mult),
                 reads=[("modT", l), "normg"], writes=[("coef", l)])
            S.op("dve", lambda e, l=l, g=g, mt=mt: e.scalar_tensor_tensor(out=coef[:, l, 3, :, g], in0=mt(4), scalar=1.0, in1=normg[:, l, 2, :],
                                                                          op0=ALU.add, op1=ALU.mult), reads=[("modT", l), "normg"], writes=[("coef", l)])
            S.op("dve", lambda e, l=l, g=g, mt=mt: e.tensor_copy(out=coef[:, l, 4, :, g], in_=mt(3)), reads=[("modT", l)], writes=[("coef", l)])
            S.op("dve", lambda e, l=l, g=g, mt=mt: e.tensor_tensor(out=coef[:, l, 5, :, g], in0=mt(5), in1=normg[:, l, 3, :], op=ALU.mult),
                 reads=[("modT", l), "normg"], writes=[("coef", l)])

    def xview(ap2d, t):
        return ap2d[:, t * TS:(t + 1) * TS].rearrange("(c p) t -> p c t", p=128)

    def rstd_from_psum(pso_i, rs, scale):
        n = rs.shape[-1] if hasattr(rs, "shape") else None
        S.op("act", lambda e: e.activation(out=rs, in_=psO[pso_i][:, 0:rs.shape[1]], func=AF.Sqrt, bias=epsT[:], scale=scale),
             reads=[("psO", pso_i), "epsT"], writes=["rs"])
        S.op("dve", lambda e: e.reciprocal(out=rs, in_=rs), reads=["rs"], writes=["rs"])

    def premix_tile(l, t, xt, kind, xres):
        g = 1 if t < 4 else 0
        ks, kb = (0, 1) if kind == 0 else (3, 4)
        sq = wk["sq"]
        po = rot("psO", 2)
        for c in range(16):
            si = rot("sq", 2)
            S.op("act", lambda e, c=c, si=si: e.activation(out=sq[si][:], in_=xt[:, c, :], func=AF.Square),
                 reads=[xres], writes=[("sq", si)])
            S.op("pe", lambda e, c=c, si=si, po=po: e.matmul(psO[po][:], lhsT=ones[:], rhs=sq[si][:], start=(c == 0), stop=(c == 15)),
                 reads=[("sq", si), "ones"], writes=[("psO", po)])
        rs = wk["rs"]
        rstd_from_psum(po, rs[:], 1.0 / D)
        for c in range(16):
            ti = rot("tmp", 2)
            tmp = wk["tmp"][ti]
            S.op("dve", lambda e, c=c, tmp=tmp: e.scalar_tensor_tensor(out=tmp[:], in0=xt[:, c, :], scalar=coef[:, l, ks, c, g:g + 1], in1=rs[:],
                                                                      op0=ALU.mult, op1=ALU.mult),
                 reads=[xres, "rs", ("coef", l)], writes=[("tmp", ti)])
            S.op("act", lambda e, c=c, tmp=tmp: e.activation(out=hall[:, c, t * TS:(t + 1) * TS], in_=tmp[:], func=AF.Identity,
                                                             bias=coef[:, l, kb, c, g:g + 1], scale=1.0),
                 reads=[("tmp", ti), ("coef", l)], writes=[("hall", t)])

    wk = {}

    def alloc_work(names):
        at = [WORK0, WORK_END]
        wk.clear()
        for nm, shape, dt, n in names:
            if n == 1:
                wk[nm] = sb("wk_%s_%d" % (nm, rot("wkname", 10 ** 9)), shape, dt, at)
            else:
                wk[nm] = [sb("wk_%s_%d" % (nm, rot("wkname", 10 ** 9)), shape, dt, at) for _ in range(n)]

    def fence(tag):
        S.fence(lambda e: e.memset(small[:, 63:64], 0.0))

    import os as _os
    STOP = int(_os.environ.get("KSTOP", "99"))
    for l in range(depth):
        if STOP < 1:
            break
        src_x = xT_in if l == 0 else XS
        fence("s1a")
        alloc_work([("x", [128, 16, TS], F32, 1), ("sq", [128, TS], BF16, 2), ("rs", [128, TS], F32, 1),
                    ("tmp", [128, TS], F32, 2), ("stg", [128, TS], BF16, 4), ("xb", [128, TS], BF16, 2),
                    ("t1", [128, TS], F32, 2), ("fst", [128, 256], F32, 2), ("vst", [128, 256], BF16, 2)])
        for t in range(NT):
            xt = wk["x"]
            S.op("sp", lambda e, t=t, xt=xt: e.dma_start(out=xt[:], in_=xview(src_x, t)), reads=[("XS", t)], writes=["xt"], chan=ch_x)
            premix_tile(l, t, xt, 0, "xt")
        if STOP < 2:
            break
        for wt in range(21):
            wv, wr = load_w(w_in[l, wt], 16, 256)
            for t in range(NT):
                for j in range(2):
                    m = wt * 2 + j
                    typ = chunk_type(m)
                    if typ in VBASE:
                        continue
                    pi = rot("ps", 4)
                    for kc in range(16):
                        S.op("pe", lambda e, j=j, kc=kc, pi=pi, t=t, wv=wv: e.matmul(ps[pi][:], lhsT=wv[:, kc, j * 128:(j + 1) * 128],
                                                                                   rhs=hall[:, kc, t * TS:(t + 1) * TS],
                                                                                   start=(kc == 0), stop=(kc == 15)),
                             reads=[wr, ("hall", t)], writes=[("ps", pi)])
                    si = rot("stg", 4)
                    stg = wk["stg"][si]
                    if typ in ROPE_T and t < 4 and _os.environ.get("KSUB") != "norope":
                        xi = rot("xb", 2)
                        xb = wk["xb"][xi]
                        t1 = wk["t1"][xi]
                        S.op("act", lambda e, pi=pi, xb=xb: e.copy(out=xb[:], in_=ps[pi][:]), reads=[("ps", pi)], writes=[("xb", xi)])
                        p2 = rot("ps", 4)
                        S.op("pe", lambda e, p2=p2, xb=xb: e.matmul(ps[p2][:], lhsT=perm[:], rhs=xb[:], start=True, stop=True),
                             reads=[("xb", xi), "perm"], writes=[("ps", p2)])
                        S.op("dve", lambda e, xb=xb, t1=t1, t=t: e.tensor_tensor(out=t1[:], in0=xb[:], in1=cosT[:, t * TS:(t + 1) * TS], op=ALU.mult),
                             reads=[("xb", xi), "cosT"], writes=[("t1", xi)])
                        S.op("dve", lambda e, p2=p2, t=t, xb=xb: e.tensor_tensor(out=xb[:], in0=ps[p2][:], in1=sinT[:, t * TS:(t + 1) * TS], op=ALU.mult),
                             reads=[("ps", p2), "sinT"], writes=[("xb", xi)])
                        S.op("dve", lambda e, xb=xb, t1=t1, stg=stg: e.tensor_tensor(out=stg[:], in0=t1[:], in1=xb[:], op=ALU.add),
                             reads=[("xb", xi), ("t1", xi)], writes=[("stg", si)])
                    else:
                        S.op("act", lambda e, pi=pi, stg=stg: e.copy(out=stg[:], in_=ps[pi][:]), reads=[("ps", pi)], writes=[("stg", si)])
                    S.op("sp", lambda e, m=m, t=t, stg=stg: e.dma_start(out=PTs[m * 128:(m + 1) * 128, t * TS:(t + 1) * TS], in_=stg[:]),
                         reads=[("stg", si)], writes=[("PTs", m, t)], chan=ch_stg[si])
                passes = []
                for j in range(2):
                    m = wt * 2 + j
                    typ = chunk_type(m)
                    if typ in VBASE or (typ in KBASE and t == 4):
                        passes.append((j, m, typ))
                if not passes or _os.environ.get("KSUB") == "notok":
                    continue
                j0 = passes[0][0]
                ncol = 128 * len(passes)
                for tb in range(4):
                    pi = rot("ps", 4)
                    tok0 = t * TS + tb * 128
                    for kc in range(16):
                        S.op("pe", lambda e, kc=kc, pi=pi, tok0=tok0, wv=wv, j0=j0, ncol=ncol: e.matmul(
                            ps[pi][:, 0:ncol], lhsT=hall[:, kc, tok0:tok0 + 128], rhs=wv[:, kc, j0 * 128:j0 * 128 + ncol],
                            start=(kc == 0), stop=(kc == 15)), reads=[wr, ("hall", t)], writes=[("ps", pi)])
                    for pj, (j, m, typ) in enumerate(passes):
                        c0 = pj * 128
                        fi = rot("fst", 2)
                        fst = wk["fst"][fi]
                        S.op("act", lambda e, pi=pi, c0=c0, fst=fst: e.copy(out=fst[:, 0:128], in_=ps[pi][:, c0:c0 + 128]),
                             reads=[("ps", pi)], writes=[("fst", fi)])
                        if typ in VBASE:
                            vi = rot("vst", 2)
                            vst = wk["vst"][vi]
                            vc = VCOL[typ] + (m - VBASE[typ]) * 128
                            S.op("dve", lambda e, fst=fst, vst=vst: e.tensor_copy(out=vst[:, 0:128], in_=fst[:, 0:128]),
                                 reads=[("fst", fi)], writes=[("vst", vi)])
                            S.op("sp", lambda e, vst=vst, tok0=tok0, vc=vc: e.dma_start(out=VT[tok0:tok0 + 128, vc:vc + 128], in_=vst[:, 0:128]),
                                 reads=[("vst", vi)], writes=[("VT", m, tok0)], chan=st_sp)
                        if t == 4:
                            sq_i = tb // 2
                            r0 = (tb % 2) * 128
                            if typ in ("na_k", "na_v"):
                                base = KBASE["na_k"] if typ == "na_k" else VBASE["na_v"]
                                dst = ona[sq_i, l, 0 if typ == "na_k" else 1, r0:r0 + 128, (m - base) * 128:(m - base + 1) * 128]
                            elif typ in ("sk", "sv"):
                                dst = oswa[sq_i, l, 0 if typ == "sk" else 1, r0:r0 + 128, :]
                            else:
                                base = KBASE["dk"] if typ == "dk" else VBASE["dv"]
                                dst = odiff[sq_i, l, 0 if typ == "dk" else 1, r0:r0 + 128, (m - base) * 128:(m - base + 1) * 128]
                            S.op("sp", lambda e, fst=fst, dst=dst: e.dma_start(out=dst, in_=fst[:, 0:128]),
                                 reads=[("fst", fi)], chan=ch_out)

        if STOP < 3:
            break
        fence("s2")
        at = [ATT0, WORK_END]
        A = {}

        def asb(nm, shape, dt, n=1):
            if n == 1:
                A[nm] = sb("at_%s_%d" % (nm, rot("wkname", 10 ** 9)), shape, dt, at)
            else:
                A[nm] = [sb("at_%s_%d" % (nm, rot("wkname", 10 ** 9)), shape, dt, at) for _ in range(n)]

        asb("Sb", [128, 2312], F32, 2)
        asb("P", [128, 2304], BF16, 2)
        asb("Pt", [128, 8, 128], BF16, 2)
        asb("KT", [128, 2304], BF16)
        asb("VD", [128, 18, 128], BF16)
        asb("QT", [128, 4, 128], BF16, 2)
        asb("KW", [128, 4, 576], BF16)
        asb("VW", [128, 5, 512], BF16)
        asb("bias", [128, 8, 576], BF16)
        asb("cK", [128, 4, 256], BF16)
        asb("cV", [128, 2, 512], BF16)
        asb("otok", [128, 128], BF16, 2)
        asb("junk", [128, 128], F32)
        asb("cz", [128, 3, 516], BF16)
        asb("cacc", [128, 512], F32, 2)
        mixT = hall

        def scores(qT, kT, nk, sbi, off, bias=None, kres=()):
            Sb = A["Sb"][sbi]
            for c0 in range(0, nk, 512):
                n = min(512, nk - c0)
                pi = rot("ps", 4)
                S.op("pe", lambda e, pi=pi, n=n, c0=c0: e.matmul(ps[pi][:, 0:n], lhsT=qT, rhs=kT[:, c0:c0 + n], start=True, stop=True),
                     reads=["QT"] + list(kres), writes=[("ps", pi)])
                if bias is None:
                    S.op("act", lambda e, pi=pi, n=n, c0=c0: e.activation(out=Sb[:, off + c0:off + c0 + n], in_=ps[pi][:, 0:n], func=AF.Identity, scale=SCALE),
                         reads=[("ps", pi)], writes=[("Sb", sbi, off + c0)])
                else:
                    S.op("dve", lambda e, pi=pi, n=n, c0=c0: e.scalar_tensor_tensor(out=Sb[:, off + c0:off + c0 + n], in0=ps[pi][:, 0:n], scalar=SCALE,
                                                                                   in1=bias[:, c0:c0 + n], op0=ALU.mult, op1=ALU.add),
                         reads=[("ps", pi), "bias"], writes=[("Sb", sbi, off + c0)])
            return [("Sb", sbi, off + c0) for c0 in range(0, nk, 512)]

        def softmax(sbi, W, sres, k):
            Sb = A["Sb"][sbi]
            P = A["P"][sbi]
            mx = small[:, 4 * k:4 * k + 1]
            nmx = small[:, 4 * k + 1:4 * k + 2]
            rs = small[:, 4 * k + 2:4 * k + 3]
            sm = ("small", k)
            S.op("dve", lambda e: e.tensor_reduce(out=mx, in_=Sb[:, 0:W], axis=AX.X, op=ALU.max), reads=sres, writes=[sm])
            S.op("dve", lambda e: e.tensor_scalar_mul(out=nmx, in0=mx, scalar1=-1.0), reads=[sm], writes=[sm])
            S.op("dve", lambda e: e.memset(rs, 0.0), writes=[sm])
            Wp = min(W, 2304)
            S.op("act", lambda e: e.activation(out=P[:, 0:Wp], in_=Sb[:, 0:Wp], func=AF.Exp, bias=nmx, scale=1.0, accum_out=rs),
                 reads=sres + [sm], writes=[("P", sbi), sm])
            if W > Wp:
                S.op("act", lambda e: e.activation(out=Sb[:, Wp:W], in_=Sb[:, Wp:W], func=AF.Exp, bias=nmx, scale=1.0, accum_out=rs),
                     reads=sres + [sm], writes=[sm] + sres)
            S.op("dve", lambda e: e.reciprocal(out=rs, in_=rs), reads=[sm], writes=[sm])
            return rs

        def pv(sbi, blocks, dv, vres):
            P = A["P"][sbi]
            po = rot("psO", 2)
            nb = len(blocks)
            for g0 in range(0, nb, 8):
                grp = blocks[g0:g0 + 8]
                ti = rot("psT", 2)
                pti = rot("Pt", 2)
                Pt = A["Pt"][pti]
                for i, (off, nk, vap) in enumerate(grp):
                    S.op("pe", lambda e, i=i, off=off, nk=nk, ti=ti: e.transpose(out=psT[ti][0:nk, i * 128:(i + 1) * 128], in_=P[:, off:off + nk], identity=ident[:]),
                         reads=[("P", sbi), "ident"], writes=[("psT", ti)])
                ng = len(grp)
                full = all(nk == 128 for (_, nk, _) in grp)
                eng = "act" if rot("pte", 2) == 0 else "dve"
                if full:
                    if eng == "act":
                        S.op("act", lambda e, ti=ti, ng=ng, Pt=Pt: e.copy(out=Pt[:, 0:ng, :].rearrange("p a b -> p (a b)"), in_=psT[ti][:, 0:ng * 128]),
                             reads=[("psT", ti)], writes=[("Pt", pti)])
                    else:
                        S.op("dve", lambda e, ti=ti, ng=ng, Pt=Pt: e.tensor_copy(out=Pt[:, 0:ng, :].rearrange("p a b -> p (a b)"), in_=psT[ti][:, 0:ng * 128]),
                             reads=[("psT", ti)], writes=[("Pt", pti)])
                else:
                    for i, (off, nk, vap) in enumerate(grp):
                        S.op("dve", lambda e, i=i, nk=nk, ti=ti, Pt=Pt: e.tensor_copy(out=Pt[0:nk, i, :], in_=psT[ti][0:nk, i * 128:(i + 1) * 128]),
                             reads=[("psT", ti)], writes=[("Pt", pti)])
                for i, (off, nk, vap) in enumerate(grp):
                    S.op("pe", lambda e, i=i, nk=nk, vap=vap, po=po, first=(g0 + i == 0), last=(g0 + i == nb - 1), Pt=Pt: e.matmul(
                        psO[po][:, 0:dv], lhsT=Pt[0:nk, i, :], rhs=vap, start=first, stop=last),
                        reads=[("Pt", pti)] + list(vres), writes=[("psO", po)])
            return po

        def put_mix(oi, chunk, tok0):
            ot = A["otok"][oi]
            ti = rot("psT", 2)
            S.op("pe", lambda e: e.transpose(out=psT[ti][:, 0:128], in_=ot[:], identity=ident[:]), reads=[("otok", oi), "ident"], writes=[("psT", ti)])
            S.op("act", lambda e: e.copy(out=mixT[:, chunk, tok0:tok0 + 128], in_=psT[ti][:, 0:128]), reads=[("psT", ti)],
                 writes=[("hall", tok0 // TS)])

        pend = []

        def flush_heads():
            while pend:
                pend.pop(0)()

        def plain_head(qT, segs, sink, hloc, oi, vres, kres, after=None):
            k = rot("plainbuf", 2)
            off = 0
            sres = []
            blocks = []
            for (kT, nk, bias, vbl) in segs:
                sres += scores(qT, kT, nk, k, off, bias, kres)
                o2 = off
                for (nkb, vap) in vbl:
                    blocks.append((o2, nkb, vap))
                    o2 += nkb
                off += nk
            W = off
            if sink is not None:
                Sb = A["Sb"][k]
                S.op("dve", lambda e, W=W: e.tensor_copy(out=Sb[:, W:W + 1], in_=sink), reads=["sinkT"], writes=[("Sb", k, "sink")])
                sres.append(("Sb", k, "sink"))
                W += 1
            Pw = off
            rs = softmax_w(k, W, Pw, sres, k)

            def phase_b():
                po = pv(k, blocks, 64, vres)
                ot = A["otok"][oi]
                S.op("act", lambda e, po=po: e.activation(out=ot[:, hloc * 64:(hloc + 1) * 64], in_=psO[po][:, 0:64], func=AF.Identity, scale=rs),
                     reads=[("psO", po), ("small", k)], writes=[("otok", oi)])
                if after is not None:
                    after()
            flush_heads()
            pend.append(phase_b)

        def softmax_w(sbi, W, Pw, sres, k):
            Sb = A["Sb"][sbi]
            P = A["P"][sbi]
            mx = small[:, 4 * k:4 * k + 1]
            nmx = small[:, 4 * k + 1:4 * k + 2]
            rs = small[:, 4 * k + 2:4 * k + 3]
            r2 = small[:, 4 * k + 3:4 * k + 4]
            sm = ("small", k)
            S.op("dve", lambda e: e.tensor_reduce(out=mx, in_=Sb[:, 0:W], axis=AX.X, op=ALU.max), reads=sres, writes=[sm])
            S.op("dve", lambda e: e.tensor_scalar_mul(out=nmx, in0=mx, scalar1=-1.0), reads=[sm], writes=[sm])
            S.op("dve", lambda e: e.memset(rs, 0.0), writes=[sm])
            S.op("act", lambda e: e.activation(out=P[:, 0:Pw], in_=Sb[:, 0:Pw], func=AF.Exp, bias=nmx, scale=1.0, accum_out=rs),
                 reads=sres + [sm], writes=[("P", sbi), sm])
            if W > Pw:
                S.op("act", lambda e: e.activation(out=r2, in_=Sb[:, Pw:W], func=AF.Exp, bias=nmx, scale=1.0), reads=sres + [sm], writes=[sm])
                S.op("dve", lambda e: e.tensor_tensor(out=rs, in0=rs, in1=r2, op=ALU.add), reads=[sm], writes=[sm])
            S.op("dve", lambda e: e.reciprocal(out=rs, in_=rs), reads=[sm], writes=[sm])
            return rs

        def diff_head(qT, segs, oi, vres, kres):
            rss = []
            W = sum(s[1] for s in segs)
            for side in range(2):
                off = 0
                sres = []
                for (kT, nk, vbl) in segs:
                    sres += scores(qT[side * 64:(side + 1) * 64, :], kT[side * 64:(side + 1) * 64, :], nk, side, off, None, kres)
                    off += nk
                rss.append(softmax_w(side, W, W, sres, side))
            blocks = []
            off = 0
            for (kT, nk, vbl) in segs:
                o2 = off
                for (nkb, vap) in vbl:
                    blocks.append((o2, nkb, vap))
                    o2 += nkb
                off += nk
            P1, P2 = A["P"][0], A["P"][1]
            lr2 = small[:, 12:13]
            S.op("dve", lambda e: e.tensor_tensor(out=lr2, in0=rss[1], in1=lamv[:, l, 2:3], op=ALU.mult),
                 reads=[("small", 1), ("lamv", l, 2)], writes=[("small", 3)])
            S.op("dve", lambda e: e.tensor_scalar_mul(out=P2[:, 0:W], in0=P2[:, 0:W], scalar1=lr2),
                 reads=[("P", 1), ("small", 3)], writes=[("P", 1)])
            S.op("dve", lambda e: e.scalar_tensor_tensor(out=P1[:, 0:W], in0=P1[:, 0:W], scalar=rss[0], in1=P2[:, 0:W], op0=ALU.mult, op1=ALU.subtract),
                 reads=[("P", 0), ("P", 1), ("small", 0)], writes=[("P", 0)])
            po = pv(0, blocks, 128, vres)
            ss = small[:, 13:14]
            S.op("dve", lambda e: e.memset(ss, 0.0), writes=[("small", 4)])
            S.op("act", lambda e, po=po: e.activation(out=A["junk"][:], in_=psO[po][:, 0:128], func=AF.Square, accum_out=ss),
                 reads=[("psO", po), ("small", 4)], writes=[("small", 4), "junk"])
            S.op("act", lambda e: e.activation(out=ss, in_=ss, func=AF.Sqrt, bias=epsT[:], scale=1.0 / 128), reads=[("small", 4), "epsT"], writes=[("small", 4)])
            S.op("dve", lambda e: e.reciprocal(out=ss, in_=ss), reads=[("small", 4)], writes=[("small", 4)])
            ot = A["otok"][oi]
            S.op("dve", lambda e, po=po: e.scalar_tensor_tensor(out=ot[:], in0=psO[po][:, 0:128], scalar=ss, in1=dgs[:, l, :], op0=ALU.mult, op1=ALU.mult),
                 reads=[("psO", po), ("small", 4), "dgs"], writes=[("otok", oi)])

        def ld(dst, src, names_w, reads=(), eng="sp", ci=0):
            S.op(eng, lambda e: e.dma_start(out=dst, in_=src), reads=list(reads), writes=list(names_w), chan=(ld_sp if eng == "sp" else ld_pl))

        def pts_reads(chunks, t0, t1):
            return [("PTs", m, t) for m in chunks for t in range(t0 // TS, (t1 - 1) // TS + 1)]

        def vt_reads(typ, t0, t1):
            n = 1 if typ == "sv" else 4
            return [("VT", VBASE[typ] + k, tk) for k in range(n) for tk in range(t0 // 128 * 128, t1, 128)]

        def conv_seg(tok0, n, left, right):
            cz = A["cz"]
            for cc in range(4):
                lo = tok0 - (1 if left else 0)
                hi = tok0 + n + (1 if right else 0)
                o = 0 if left else 1
                for k, base in enumerate((12, 20, 16)):
                    m = base + cc
                    ld(cz[:, k, o:o + hi - lo], PTs[m * 128:(m + 1) * 128, lo:hi], [("cz", k)], pts_reads([m], lo, hi), "sp", 1)
                if not left:
                    S.op("pool", lambda e: e.memset(cz[:, 0:2, 0:1], 0.0), writes=[("cz", 0), ("cz", 1)])
                if not right:
                    S.op("pool", lambda e, n=n: e.memset(cz[:, 0:2, n + 1:n + 2], 0.0), writes=[("cz", 0), ("cz", 1)])
                S.op("pool", lambda e, n=n: e.tensor_tensor(out=cz[:, 0, 0:n + 2], in0=cz[:, 0, 0:n + 2], in1=cz[:, 1, 0:n + 2], op=ALU.mult),
                     reads=[("cz", 0), ("cz", 1)], writes=[("cz", 0)])
                ai = rot("cacc", 2)
                acc = A["cacc"][ai]
                S.op("pool", lambda e, n=n, cc=cc, acc=acc: e.tensor_scalar_mul(out=acc[:, 0:n], in0=cz[:, 0, 0:n], scalar1=convw[:, l, cc, 0:1]),
                     reads=[("cz", 0), "convw"], writes=[("cacc", ai)])
                for k in (1, 2):
                    S.op("dve", lambda e, n=n, cc=cc, acc=acc, k=k: e.scalar_tensor_tensor(out=acc[:, 0:n], in0=cz[:, 0, k:k + n], scalar=convw[:, l, cc, k:k + 1],
                                                                                         in1=acc[:, 0:n], op0=ALU.mult, op1=ALU.add),
                         reads=[("cz", 0), "convw", ("cacc", ai)], writes=[("cacc", ai)])
                S.op("pool", lambda e, n=n, cc=cc, acc=acc: e.tensor_tensor(out=mixT[:, 4 + cc, tok0:tok0 + n], in0=acc[:, 0:n], in1=cz[:, 2, 1:n + 1], op=ALU.mult),
                     reads=[("cacc", ai), ("cz", 2)], writes=[("hall", tok0 // TS)])

        for t in range(4):
            conv_seg(t * TS, TS, t > 0, t < 3)
        conv_seg(2048, 256, False, False)
        conv_seg(2304, 256, False, False)

        ld(A["cK"][:], cnaKT[l].rearrange("c p k -> p c k"), ["cK"], (), "pool", 2)
        ld(A["cV"][:], cnaV[l].rearrange("(b p) c -> p b c", p=128), ["cV"], (), "pool", 2)
        for i in range(16):
            tok0 = i * 128
            k0 = na_ks(i) * 64
            qi = rot("QT", 2)
            QT = A["QT"][qi]
            ld(QT[:], PTs[0:512, tok0:tok0 + 128].rearrange("(c p) t -> p c t", p=128), ["QT"], pts_reads(range(0, 4), tok0, tok0 + 128), "sp", 0)
            ld(A["KW"][:], PTs[512:1024, k0:k0 + 576].rearrange("(c p) t -> p c t", p=128), ["KW"], pts_reads(range(4, 8), k0, k0 + 576), "sp", 0)
            ld(A["VW"][:, 0:4, :], VT[k0:k0 + 512, 0:512].rearrange("(b p) c -> p b c", p=128), ["VW"], vt_reads("na_v", k0, k0 + 576), "sp", 0)
            ld(A["VW"][0:64, 4, :], VT[k0 + 512:k0 + 576, 0:512], ["VW"], (), "sp", 0)
            ld(A["bias"][:], nab[l, na_pat(i)].rearrange("h p k -> p h k"), ["bias"], (), "pool", 2)
            for h in range(8):
                chn, r0 = h // 2, (h % 2) * 64
                segs = [(A["KW"][r0:r0 + 64, chn, :], 576, A["bias"][:, h, :],
                         [(128, A["VW"][:, b, h * 64:(h + 1) * 64]) for b in range(4)] + [(64, A["VW"][0:64, 4, h * 64:(h + 1) * 64])]),
                        (A["cK"][r0:r0 + 64, chn, :], 256, None, [(128, A["cV"][:, b, h * 64:(h + 1) * 64]) for b in range(2)])]
                oi = (h // 2) % 2
                plain_head(QT[r0:r0 + 64, chn, :], segs, None, h % 2, oi, ["VW", "cV"], ["KW", "cK"],
                           after=((lambda oi=oi, chn=chn, tok0=tok0: put_mix(oi, chn, tok0)) if h % 2 == 1 else None))
            flush_heads()
        for s in range(2):
            p0 = 2048 + 256 * s
            ld(A["KW"][:, :, 0:256], PTs[512:1024, p0:p0 + 256].rearrange("(c p) t -> p c t", p=128), ["KW"], pts_reads(range(4, 8), p0, p0 + 256), "sp", 0)
            ld(A["VW"][:, 0:2, :], VT[p0:p0 + 256, 0:512].rearrange("(b p) c -> p b c", p=128), ["VW"], vt_reads("na_v", p0, p0 + 256), "sp", 0)
            for qt in range(2):
                tok0 = p0 + qt * 128
                qi = rot("QT", 2)
                QT = A["QT"][qi]
                ld(QT[:], PTs[0:512, tok0:tok0 + 128].rearrange("(c p) t -> p c t", p=128), ["QT"], pts_reads(range(0, 4), tok0, tok0 + 128), "sp", 0)
                for h in range(8):
                    chn, r0 = h // 2, (h % 2) * 64
                    segs = [(A["KW"][r0:r0 + 64, chn, 0:256], 256, None, [(128, A["VW"][:, b, h * 64:(h + 1) * 64]) for b in range(2)])]
                    oi = (h // 2) % 2
                    plain_head(QT[r0:r0 + 64, chn, :], segs, None, h % 2, oi, ["VW"], ["KW"],
                               after=((lambda oi=oi, chn=chn, tok0=tok0: put_mix(oi, chn, tok0)) if h % 2 == 1 else None))
                flush_heads()

        KS = A["KW"]

        def load_swa_k(g, dstcols, src_rows_ap, reads):
            for half in range(2):
                ld(KS[half * 64:(half + 1) * 64, g, dstcols[0]:dstcols[1]], src_rows_ap, ["KW"], reads, "sp", 0)

        for g in range(2):
            for half in range(2):
                ld(A["cK"][half * 64:(half + 1) * 64, g, :], cswaKT[l, g * 64:(g + 1) * 64, :], ["cK"], (), "pool", 2)
        ld(A["cV"][:, :, 0:128], cswaV[l].rearrange("(b p) c -> p b c", p=128), ["cV"], (), "pool", 2)
        for jq in range(16):
            tok0 = jq * 128
            ws = min(max(128 * (jq - 1), 0), 2048 - 384)
            pat = 0 if jq == 0 else (2 if jq == 15 else 1)
            qi = rot("QT", 2)
            QT = A["QT"][qi]
            ld(QT[:], PTs[24 * 128:28 * 128, tok0:tok0 + 128].rearrange("(c p) t -> p c t", p=128), ["QT"], pts_reads(range(24, 28), tok0, tok0 + 128), "sp", 0)
            for g in range(2):
                load_swa_k(g, (0, 384), PTs[28 * 128 + g * 64:28 * 128 + (g + 1) * 64, ws:ws + 384], pts_reads([28], ws, ws + 384))
            ld(A["VW"][:, 0:3, 0:128], VT[ws:ws + 384, 512:640].rearrange("(b p) c -> p b c", p=128), ["VW"], vt_reads("sv", ws, ws + 384), "sp", 0)
            for h in range(8):
                g, chn, r0 = h // 4, h // 2, (h % 2) * 64
                segs = [(KS[r0:r0 + 64, g, 0:384], 384, swamask[:, pat, :], [(128, A["VW"][:, b, g * 64:(g + 1) * 64]) for b in range(3)]),
                        (A["cK"][r0:r0 + 64, g, :], 256, None, [(128, A["cV"][:, b, g * 64:(g + 1) * 64]) for b in range(2)])]
                oi = (h // 2) % 2
                plain_head(QT[r0:r0 + 64, chn, :], segs, sinkT[:, l, h:h + 1], h % 2, oi, ["VW", "cV"], ["KW", "cK", "swamask"],
                           after=((lambda oi=oi, chn=chn, tok0=tok0: put_mix(oi, 8 + chn, tok0)) if h % 2 == 1 else None))
            flush_heads()
        for s in range(2):
            p0 = 2048 + 256 * s
            for g in range(2):
                load_swa_k(g, (0, 256), PTs[28 * 128 + g * 64:28 * 128 + (g + 1) * 64, p0:p0 + 256], pts_reads([28], p0, p0 + 256))
            ld(A["VW"][:, 0:2, 0:128], VT[p0:p0 + 256, 512:640].rearrange("(b p) c -> p b c", p=128), ["VW"], vt_reads("sv", p0, p0 + 256), "sp", 0)
            for qt in range(2):
                tok0 = p0 + qt * 128
                qi = rot("QT", 2)
                QT = A["QT"][qi]
                ld(QT[:], PTs[24 * 128:28 * 128, tok0:tok0 + 128].rearrange("(c p) t -> p c t", p=128), ["QT"], pts_reads(range(24, 28), tok0, tok0 + 128), "sp", 0)
                for h in range(8):
                    g, chn, r0 = h // 4, h // 2, (h % 2) * 64
                    segs = [(KS[r0:r0 + 64, g, 0:256], 256, None, [(128, A["VW"][:, b, g * 64:(g + 1) * 64]) for b in range(2)])]
                    oi = (h // 2) % 2
                    plain_head(QT[r0:r0 + 64, chn, :], segs, sinkT[:, l, h:h + 1], h % 2, oi, ["VW"], ["KW"],
                               after=((lambda oi=oi, chn=chn, tok0=tok0: put_mix(oi, 8 + chn, tok0)) if h % 2 == 1 else None))
                flush_heads()

        for h in range(4):
            KT, VD = A["KT"], A["VD"]
            ld(KT[:, 0:2048], PTs[(34 + h) * 128:(35 + h) * 128, 0:2048], ["KT"], pts_reads([34 + h], 0, 2048), "sp", 3)
            ld(KT[:, 2048:2304], cdiffKT[l, h], ["KT"], (), "pool", 2)
            ld(VD[:, 0:16, :], VT[0:2048, 640 + h * 128:640 + (h + 1) * 128].rearrange("(b p) c -> p b c", p=128), ["VD"], vt_reads("dv", 0, 2048), "sp", 3)
            ld(VD[:, 16:18, :], cdiffV[l, :, h * 128:(h + 1) * 128].rearrange("(b p) c -> p b c", p=128), ["VD"], (), "pool", 2)
            for i in range(16):
                tok0 = i * 128
                qi = rot("QT", 2)
                QT = A["QT"][qi]
                ld(QT[:, 0, :], PTs[(30 + h) * 128:(31 + h) * 128, tok0:tok0 + 128], ["QT"], pts_reads([30 + h], tok0, tok0 + 128), "sp", 0)
                segs = [(KT[:, :], 2304, [(128, VD[:, b, :]) for b in range(18)])]
                oi = rot("otokd", 2)
                diff_head(QT[:, 0, :], segs, oi, ["VD"], ["KT"])
                put_mix(oi, 12 + h, tok0)
            for s in range(2):
                p0 = 2048 + 256 * s
                ld(KT[:, 0:256], PTs[(34 + h) * 128:(35 + h) * 128, p0:p0 + 256], ["KT"], pts_reads([34 + h], p0, p0 + 256), "sp", 3)
                ld(VD[:, 0:2, :], VT[p0:p0 + 256, 640 + h * 128:640 + (h + 1) * 128].rearrange("(b p) c -> p b c", p=128), ["VD"], vt_reads("dv", p0, p0 + 256), "sp", 3)
                for qt in range(2):
                    tok0 = p0 + qt * 128
                    qi = rot("QT", 2)
                    QT = A["QT"][qi]
                    ld(QT[:, 0, :], PTs[(30 + h) * 128:(31 + h) * 128, tok0:tok0 + 128], ["QT"], pts_reads([30 + h], tok0, tok0 + 128), "sp", 0)
                    segs = [(KT[:, 0:256], 256, [(128, VD[:, b, :]) for b in range(2)])]
                    oi = rot("otokd", 2)
                    diff_head(QT[:, 0, :], segs, oi, ["VD"], ["KT"])
                    put_mix(oi, 12 + h, tok0)

        if STOP < 4:
            break
        fence("s3")
        alloc_work([("x", [128, 16, TS], F32, 1), ("y", [128, 16, 256], F32, 1), ("sq", [128, TS], BF16, 2), ("rs", [128, TS], F32, 1),
                    ("tmp", [128, TS], F32, 2)])

        def branch_update(l, t, half, xt, y, kc, po):
            g = 1 if t < 4 else 0
            rs = wk["rs"]
            rstd_from_psum(po, rs[:, 0:256], 1.0 / D)
            for c in range(16):
                ti = rot("tmp", 2)
                tmp = wk["tmp"][ti]
                S.op("dve", lambda e, c=c, tmp=tmp: e.scalar_tensor_tensor(out=tmp[:, 0:256], in0=y[:, c, :], scalar=coef[:, l, kc, c, g:g + 1], in1=rs[:, 0:256],
                                                                          op0=ALU.mult, op1=ALU.mult), reads=["y", "rs", ("coef", l)], writes=[("tmp", ti)])
                S.op("pool", lambda e, c=c, tmp=tmp: e.tensor_tensor(out=xt[:, c, half * 256:(half + 1) * 256], in0=xt[:, c, half * 256:(half + 1) * 256],
                                                                    in1=tmp[:, 0:256], op=ALU.add), reads=[("tmp", ti), "xt"], writes=["xt"])

        for t in range(NT):
            xt = wk["x"]
            y = wk["y"]
            S.op("sp", lambda e, t=t, xt=xt: e.dma_start(out=xt[:], in_=xview(src_x, t)), reads=[("XS", t)], writes=["xt"], chan=ch_x)
            for half in range(2):
                c0 = t * TS + half * 256
                po = rot("psO", 2)
                for wt in range(4):
                    wv, wr = load_w(w_out[l, wt], 16, 512)
                    for j in range(4):
                        m = wt * 4 + j
                        pi = rot("ps", 4)
                        for kc in range(16):
                            S.op("pe", lambda e, j=j, kc=kc, pi=pi, wv=wv, c0=c0: e.matmul(ps[pi][:, 0:256], lhsT=wv[:, kc, j * 128:(j + 1) * 128],
                                                                                         rhs=mixT[:, kc, c0:c0 + 256], start=(kc == 0), stop=(kc == 15)),
                                 reads=[wr, ("hall", t)], writes=[("ps", pi)])
                        S.op("dve", lambda e, m=m, pi=pi: e.tensor_copy(out=y[:, m, :], in_=ps[pi][:, 0:256]), reads=[("ps", pi)], writes=["y"])
                        si = rot("sq", 2)
                        sq = wk["sq"][si]
                        S.op("act", lambda e, m=m, sq=sq: e.activation(out=sq[:, 0:256], in_=y[:, m, :], func=AF.Square), reads=["y"], writes=[("sq", si)])
                        S.op("pe", lambda e, m=m, sq=sq, po=po: e.matmul(psO[po][:, 0:256], lhsT=ones[:], rhs=sq[:, 0:256], start=(m == 0), stop=(m == 15)),
                             reads=[("sq", si), "ones"], writes=[("psO", po)])
                branch_update(l, t, half, xt, y, 2, po)
            S.op("sp", lambda e, t=t, xt=xt: e.dma_start(out=xview(XS, t), in_=xt[:]), reads=["xt"], writes=[("XS", t)], chan=ch_xo)
            premix_tile(l, t, xt, 1, "xt")

        if STOP < 5:
            break
        alloc_work([("r", [128, TS], F32, 2), ("astg", [128, TS], BF16, 4)])
        fence("s4")
        for wt in range(16):
            wv, wr = load_w(w1[l, wt], 16, 512)
            for t in range(NT):
                for j in range(4):
                    m = wt * 4 + j
                    pi = rot("ps", 4)
                    for kc in range(16):
                        S.op("pe", lambda e, j=j, kc=kc, pi=pi, t=t, wv=wv: e.matmul(ps[pi][:], lhsT=wv[:, kc, j * 128:(j + 1) * 128],
                                                                                   rhs=hall[:, kc, t * TS:(t + 1) * TS], start=(kc == 0), stop=(kc == 15)),
                             reads=[wr, ("hall", t)], writes=[("ps", pi)])
                    ri = rot("r", 2)
                    r = wk["r"][ri]
                    ai = rot("astg", 4)
                    a = wk["astg"][ai]
                    S.op("act", lambda e, pi=pi, r=r: e.activation(out=r[:], in_=ps[pi][:], func=AF.Relu), reads=[("ps", pi)], writes=[("r", ri)])
                    S.op("dve", lambda e, r=r, a=a: e.tensor_tensor(out=a[:], in0=r[:], in1=r[:], op=ALU.mult), reads=[("r", ri)], writes=[("astg", ai)])
                    S.op("sp", lambda e, m=m, t=t, a=a: e.dma_start(out=AH[m * 128:(m + 1) * 128, t * TS:(t + 1) * TS], in_=a[:]),
                         reads=[("astg", ai)], writes=[("AH", m, t)], chan=ch_stg[ai])

        if STOP < 6:
            break
        fence("s5")
        alloc_work([("y", [128, 16, TS], F32, 1), ("sq", [128, TS], BF16, 2), ("rs", [128, TS], F32, 1),
                    ("tmp", [128, TS], F32, 2), ("xc", [128, TS], F32, 3)])
        at_flat = hall[:].rearrange("p a b -> p (a b)")
        last = (l == depth - 1)
        for t in range(NT):
            y = wk["y"]
            g = 1 if t < 4 else 0
            a_t = at_flat[:, 0:64 * TS].rearrange("p (a b) -> p a b", a=64)
            S.op("sp", lambda e, t=t, a_t=a_t: e.dma_start(out=a_t, in_=AH[:, t * TS:(t + 1) * TS].rearrange("(a p) t -> p a t", p=128)),
                 reads=[("AH", m, t) for m in range(64)], writes=[("hall", k) for k in range(NT)], chan=ch_ld[4])
            po = rot("psO", 2)
            for m in range(16):
                i = rot("wb", 2)
                buf = WB[i]
                w2f = buf[:].rearrange("p a b -> p (a b)")[:, 0:64 * 128]
                w2v = w2f.rearrange("p (a b) -> p a b", a=64)
                S.op("pool", lambda e, m=m, w2f=w2f: e.dma_start(out=w2f, in_=w2[l, m]),
                     writes=[("wb", i)], chan=ch_w[i])
                pi = rot("ps", 4)
                for kc in range(64):
                    S.op("pe", lambda e, kc=kc, pi=pi, w2v=w2v, a_t=a_t: e.matmul(ps[pi][:], lhsT=w2v[:, kc, :], rhs=a_t[:, kc, :],
                                                                                start=(kc == 0), stop=(kc == 63)),
                         reads=[("wb", i), ("hall", 0)], writes=[("ps", pi)])
                S.op("dve", lambda e, m=m, pi=pi: e.tensor_copy(out=y[:, m, :], in_=ps[pi][:]), reads=[("ps", pi)], writes=["y"])
                si = rot("sq", 2)
                sq = wk["sq"][si]
                S.op("act", lambda e, m=m, sq=sq: e.activation(out=sq[:], in_=y[:, m, :], func=AF.Square), reads=["y"], writes=[("sq", si)])
                S.op("pe", lambda e, m=m, sq=sq, po=po: e.matmul(psO[po][:], lhsT=ones[:], rhs=sq[:], start=(m == 0), stop=(m == 15)),
                     reads=[("sq", si), "ones"], writes=[("psO", po)])
            rs = wk["rs"]
            rstd_from_psum(po, rs[:], 1.0 / D)
            for c in range(16):
                xi = rot("xc", 3)
                xc = wk["xc"][xi]
                S.op("sp", lambda e, c=c, t=t, xc=xc: e.dma_start(out=xc[:], in_=XS[c * 128:(c + 1) * 128, t * TS:(t + 1) * TS]),
                     reads=[("XS", t)], writes=[("xc", xi)], chan=ch_ld[5])
                ti = rot("tmp", 2)
                tmp = wk["tmp"][ti]
                S.op("dve", lambda e, c=c, tmp=tmp, g=g: e.scalar_tensor_tensor(out=tmp[:], in0=y[:, c, :], scalar=coef[:, l, 5, c, g:g + 1], in1=rs[:],
                                                                           op0=ALU.mult, op1=ALU.mult), reads=["y", "rs", ("coef", l)], writes=[("tmp", ti)])
                S.op("pool", lambda e, xc=xc, tmp=tmp: e.tensor_tensor(out=xc[:], in0=xc[:], in1=tmp[:], op=ALU.add),
                     reads=[("tmp", ti), ("xc", xi)], writes=[("xc", xi)])
                if last:
                    S.op("sp", lambda e, c=c, t=t, xc=xc: e.dma_start(out=yT[c * 128:(c + 1) * 128, t * TS:(t + 1) * TS], in_=xc[:]),
                         reads=[("xc", xi)], chan=ch_out)
                else:
                    S.op("sp", lambda e, c=c, t=t, xc=xc: e.dma_start(out=XS[c * 128:(c + 1) * 128, t * TS:(t + 1) * TS], in_=xc[:]),
                         reads=[("xc", xi)], writes=[("XSo", t, c)], chan=ch_xo)
        if not last:
            for t in range(NT):
                S.op("dve", lambda e: e.memset(small[:, 62:63], 0.0), reads=[("XSo", t, c) for c in range(16)], writes=[("XS", t)])

    cnt = S.finalize(final_waits=STP)
    S.emit()
    return nc, len(S.ops), cnt


def _na_bias_tables(rpb):
    out = np.empty((rpb.shape[0], 5, 8, 128, 576), np.float32)
    for pi, i in enumerate((0, 1, 2, 14, 15)):
        ks = na_ks(i)
        q = np.arange(128)
        r = 2 * i + q // 64
        c = q % 64
        k = np.arange(576)
        kr = ks + k // 64
        kc = k % 64
        rs = np.clip(r - 4, 0, 24)
        row_ok = (kr[None, :] >= rs[:, None]) & (kr[None, :] < rs[:, None] + 8)
        cs = np.clip(c - 8, 0, 48)
        col_ok = (kc[None, :] >= cs[:, None]) & (kc[None, :] < cs[:, None] + 16)
        ok = row_ok & col_ok
        dr = np.clip(kr[None, :] - r[:, None] + 7, 0, 14)
        dc = np.clip(kc[None, :] - c[:, None], -15, 15) + 15
        g = rpb[:, :, dr, dc]
        out[:, pi] = np.where(ok[None, None], g, np.float32(NEG))
    return out


def _rope_tables():
    t = np.arange(2048)
    rows = (t // 64).astype(np.float32)
    cols = (t % 64).astype(np.float32)
    n = 16
    inv = (np.float32(10000.0) ** (-np.arange(n, dtype=np.float32) / n)).astype(np.float32)
    cosT = np.zeros((128, 2048), np.float32)
    sinT = np.zeros((128, 2048), np.float32)
    for p in range(128):
        d = p % 64
        pos = rows if d < 32 else cols
        j = d % 16
        ang = pos * inv[j]
        cosT[p] = np.cos(ang)
        sinT[p] = -np.sin(ang) if (d % 32) < 16 else np.sin(ang)
    perm = np.zeros((128, 128), np.float32)
    for m in range(128):
        partner = m + 16 if (m % 32) < 16 else m - 16
        perm[partner, m] = 1.0
    return cosT, sinT, perm


def _swa_masks():
    out = np.zeros((128, 3, 384), np.float32)
    for pat, jq in enumerate((0, 5, 15)):
        ws = min(max(128 * (jq - 1), 0), 2048 - 384)
        q = jq * 128 + np.arange(128)
        k = ws + np.arange(384)
        ok = np.abs(q[:, None] - k[None, :]) <= 128
        out[:, pat, :] = np.where(ok, 0.0, NEG)
    return out


_CACHE = {}


def _tile_w(w, ncols):
    L, K, N = w.shape
    nt = N // ncols
    return np.ascontiguousarray(w.reshape(L, K // 128, 128, nt, ncols).transpose(0, 3, 2, 1, 4).reshape(L, nt, 128, (K // 128) * ncols))


def kernel(x_prompt, x_sample, cache_na_kv, cache_swa_kv, cache_diff_kv, c, c_ctx,
           ada_w, ada_b, norm_g, w_in, conv_w, na_rpb, swa_sink, diff_lambda, diff_norm_g,
           w_out, mlp_w1, mlp_w2, _depth=DEPTH):
    f32 = np.float32
    A = lambda a: np.ascontiguousarray(np.asarray(a, dtype=f32))
    x_prompt, x_sample = A(x_prompt), A(x_sample)
    cache_na_kv, cache_swa_kv, cache_diff_kv = A(cache_na_kv), A(cache_swa_kv), A(cache_diff_kv)
    c, c_ctx = A(c), A(c_ctx)
    ada_w, ada_b, norm_g, w_in, conv_w = A(ada_w), A(ada_b), A(norm_g), A(w_in), A(conv_w)
    na_rpb, swa_sink, diff_lambda, diff_norm_g = A(na_rpb), A(swa_sink), A(diff_lambda), A(diff_norm_g)
    w_out, mlp_w1, mlp_w2 = A(w_out), A(mlp_w1), A(mlp_w2)

    if _depth not in _CACHE:
        _CACHE[_depth] = build_program(_depth)[0]
    nc = _CACHE[_depth]

    cosT, sinT, perm = _rope_tables()
    shared = {
        "ada_w": _tile_w(ada_w[:_depth], 512),
        "ada_bT": A(ada_b.reshape(DEPTH, 96, 128).transpose(2, 0, 1)),
        "normgT": A(norm_g.reshape(DEPTH, 4, 16, 128).transpose(3, 0, 1, 2)),
        "w_in": _tile_w(w_in[:_depth], 256), "w_out": _tile_w(w_out[:_depth], 512), "w1": _tile_w(mlp_w1[:_depth], 512), "w2": A(mlp_w2[:_depth].reshape(_depth, 64, 128, 16, 128).transpose(0, 3, 2, 1, 4).reshape(_depth, 16, 128, 64 * 128)),
        "convwT": A(conv_w.reshape(DEPTH, 3, 4, 128).transpose(3, 0, 2, 1)),
        "nab": A(_na_bias_tables(na_rpb)[:_depth]),
        "sinkB": A(np.broadcast_to(swa_sink[None], (128, DEPTH, 8))),
        "lamP": A(np.broadcast_to(diff_lambda[None], (128, DEPTH, 4, 64))),
        "dgB": A(np.broadcast_to(diff_norm_g[None], (128, DEPTH, 128))),
        "cosT": cosT, "sinT": sinT, "perm": perm, "ident": np.eye(128, dtype=f32),
        "swamask": _swa_masks(),
    }
    in_maps = []
    for core in range(8):
        b = core // 4
        s0, s1 = 2 * core, 2 * core + 1
        m = dict(shared)
        m["xT"] = A(np.concatenate([x_sample[b].T, x_prompt[s0].T, x_prompt[s1].T], axis=1))
        cv = np.stack([c_ctx, c[b]], axis=-1)
        m["cvec"] = A(cv.reshape(16, 128, 2).transpose(1, 0, 2))
        na = cache_na_kv[b]
        m["cnaKT"] = A(na[:, 0].reshape(DEPTH, 256, 4, 128).transpose(0, 2, 3, 1))
        m["cnaV"] = A(na[:, 1].reshape(DEPTH, 256, 512))
        sw = cache_swa_kv[b]
        m["cswaKT"] = A(sw[:, 0].reshape(DEPTH, 256, 128).transpose(0, 2, 1))
        m["cswaV"] = A(sw[:, 1].reshape(DEPTH, 256, 128))
        df = cache_diff_kv[b]
        m["cdiffKT"] = A(df[:, 0].transpose(0, 2, 3, 1))
        m["cdiffV"] = A(df[:, 1].reshape(DEPTH, 256, 512))
        in_maps.append(m)

    res = run_bass_kernel_spmd(nc, in_maps, core_ids=list(range(8)))
    R = res.results
    yp = np.empty((16, 256, D), f32)
    ys = np.empty((2, 2048, D), f32)
    nna = np.empty((16, DEPTH, 2, 256, 8, 64), f32)
    nsw = np.empty((16, DEPTH, 2, 256, 2, 64), f32)
    ndf = np.empty((16, DEPTH, 2, 256, 4, 128), f32)
    for core in range(8):
        yt = np.asarray(R[core]["yT"])
        if core % 4 == 0:
            ys[core // 4] = yt[:, 0:2048].T
        for s in range(2):
            yp[2 * core + s] = yt[:, 2048 + 256 * s:2048 + 256 * (s + 1)].T
            nna[2 * core + s] = np.asarray(R[core]["ona"])[s].reshape(DEPTH, 2, 256, 8, 64)
            nsw[2 * core + s] = np.asarray(R[core]["oswa"])[s].reshape(DEPTH, 2, 256, 2, 64)
            ndf[2 * core + s] = np.asarray(R[core]["odiff"])[s].reshape(DEPTH, 2, 256, 4, 128)
    return (yp, ys, nna, nsw, ndf)
```
